# Optimizing a Trainium2 kernel written in Bass

```python
import jax, jax.numpy as jnp
from jax import lax
import numpy as np

D_MODEL = 2048
BATCH = 4
SEQ = 2048
DEPTH = 4

GRID_W = 64
CTX_LEN = 256
N_MIXERS = 3
EPS = 1e-6
NEG_INF = -1e30

D_FF = ((8 * D_MODEL // 3 + 255) // 256) * 256

GLA_HEADS = 4
GLA_DK = D_MODEL // 2
GLA_DV = D_MODEL
GLA_DK_HEAD = GLA_DK // GLA_HEADS
GLA_DV_HEAD = GLA_DV // GLA_HEADS
GLA_RANK = 16
GLA_TAU = 16.0
GLA_CHUNK = 64
GLA_IN = 2 * GLA_DK + 2 * GLA_DV + 2 * GLA_RANK

SWA_HEAD_DIM = 64
SWA_HEADS = D_MODEL // SWA_HEAD_DIM
SWA_KV_HEADS = 4
SWA_GROUP = SWA_HEADS // SWA_KV_HEADS
SWA_WINDOW = 128
SWA_BLOCK = 128
SWA_IN = (SWA_HEADS + 2 * SWA_KV_HEADS) * SWA_HEAD_DIM
ROPE_BASE = 10000.0

GMLP_WIDTH = D_MODEL
GMLP_CHUNK = 128
GMLP_GROUPS = 16
GMLP_GROUP_DIM = GMLP_WIDTH // GMLP_GROUPS

N_A = (DEPTH + 2) // 3
N_B = (DEPTH + 1) // 3
N_C = DEPTH // 3

kernel_name = "hybrid_gla_swa_gmlp_diffusion_trunk"

F32 = jnp.float32


def rmsnorm(x, g):
    xf = x.astype(F32)
    y = xf * lax.rsqrt(jnp.mean(xf * xf, axis=-1, keepdims=True) + EPS)
    return (y * g.astype(F32)).astype(x.dtype)


def layernorm(x, g, b):
    xf = x.astype(F32)
    mu = jnp.mean(xf, axis=-1, keepdims=True)
    var = jnp.mean(jnp.square(xf - mu), axis=-1, keepdims=True)
    return ((xf - mu) * lax.rsqrt(var + EPS) * g.astype(F32) + b.astype(F32)).astype(x.dtype)


def swiglu(h, w1, w3, w2):
    return (jax.nn.silu(h @ w1) * (h @ w3)) @ w2


def axial_rope_tables(n_tokens):
    rows = n_tokens // GRID_W
    quarter = SWA_HEAD_DIM // 4
    inv_freq = ROPE_BASE ** (-jnp.arange(quarter, dtype=F32) / quarter)
    row = jnp.repeat(jnp.arange(rows), GRID_W).astype(F32)
    col = jnp.tile(jnp.arange(GRID_W), rows).astype(F32)
    ang_r = row[:, None] * inv_freq
    ang_c = col[:, None] * inv_freq
    ang = jnp.concatenate([ang_r, ang_r, ang_c, ang_c], axis=-1)
    return jnp.cos(ang), jnp.sin(ang)


def apply_axial_rope(x, cos, sin):
    half, quarter = SWA_HEAD_DIM // 2, SWA_HEAD_DIM // 4

    def rot_half(a):
        return jnp.concatenate([-a[..., quarter:], a[..., :quarter]], axis=-1)

    rot = jnp.concatenate([rot_half(x[..., :half]), rot_half(x[..., half:])], axis=-1)
    return (x * cos + rot * sin).astype(x.dtype)


def gla_chunk_scan(q, k, v, g, s0):
    B, H, T, DK = q.shape
    n = T // GLA_CHUNK

    def split(a):
        return a.reshape(B, H, n, GLA_CHUNK, a.shape[-1]).transpose(2, 0, 1, 3, 4)

    mask = jnp.tril(jnp.ones((GLA_CHUNK, GLA_CHUNK), bool))[:, :, None]

    def step(s, inp):
        qc, kc, vc, gc = inp
        b = jnp.cumsum(gc.astype(F32), axis=2)
        o_inter = jnp.einsum('bhik,bhkv->bhiv', qc * jnp.exp(b), s)
        diff = b[:, :, :, None, :] - b[:, :, None, :, :]
        decay = jnp.where(mask, jnp.exp(jnp.where(mask, diff, 0.0)), 0.0)
        att = jnp.einsum('bhik,bhjk,bhijk->bhij', qc, kc, decay)
        o_intra = jnp.einsum('bhij,bhjv->bhiv', att, vc)
        b_last = b[:, :, -1:, :]
        s_new = (jnp.exp(b_last[:, :, 0, :])[..., None] * s
                 + jnp.einsum('bhjk,bhjv->bhkv', kc * jnp.exp(b_last - b), vc))
        return s_new, o_inter + o_intra

    s_fin, o = lax.scan(step, s0, (split(q), split(k), split(v), split(g)))
    return o.transpose(1, 2, 0, 3, 4).reshape(B, H, T, v.shape[-1]), s_fin


def gla_mixer(hx, hc, w_in, wa2, ba, onorm_g, wo, need_ctx):
    B, L, _ = hx.shape
    Lc = hc.shape[1]
    h = jnp.concatenate([hc, hx], axis=1)
    T = Lc + L
    p = h @ w_in
    q, k, v, og, a_f, a_b = jnp.split(
        p, [GLA_DK, 2 * GLA_DK, 2 * GLA_DK + GLA_DV, 2 * GLA_DK + 2 * GLA_DV,
            2 * GLA_DK + 2 * GLA_DV + GLA_RANK], axis=-1)

    def heads(a):
        return a.reshape(B, T, GLA_HEADS, -1).transpose(0, 2, 1, 3)

    q = heads(q) * (GLA_DK_HEAD ** -0.5)
    k, v = heads(k), heads(v)

    def log_decay(a, d):
        return heads(jax.nn.log_sigmoid((a @ wa2[d] + ba[d]).astype(F32)) / GLA_TAU)

    g_f, g_b = log_decay(a_f, 0), log_decay(a_b, 1)
    s0 = jnp.zeros((B, GLA_HEADS, GLA_DK_HEAD, GLA_DV_HEAD), F32)
    o_f, _ = gla_chunk_scan(q, k, v, g_f, s0)

    def rev(a):
        return jnp.concatenate([jnp.flip(a[:, :, :Lc], 2), jnp.flip(a[:, :, Lc:], 2)], axis=2)

    o_b, _ = gla_chunk_scan(rev(q), rev(k), rev(v), rev(g_b), s0)
    o = o_f + rev(o_b)
    o = rmsnorm(o, onorm_g.reshape(GLA_HEADS, 1, GLA_DV_HEAD))
    o = o.transpose(0, 2, 1, 3).reshape(B, T, GLA_DV).astype(hx.dtype) * jax.nn.silu(og)
    yx = o[:, Lc:] @ wo
    yc = (o[:, :Lc] @ wo) if need_ctx else None
    return yx, yc


def sink_softmax(logits, sink):
    s = jnp.broadcast_to(sink.astype(F32).reshape(SWA_KV_HEADS, SWA_GROUP, 1, 1),
                         logits.shape[:-1] + (1,))
    return jax.nn.softmax(jnp.concatenate([s, logits], axis=-1), axis=-1)[..., 1:]


def swa_mixer(hx, hc, w_in, sink, wo, cos, sin, need_ctx):
    B, L, _ = hx.shape
    Lc = hc.shape[1]
    scale = SWA_HEAD_DIM ** -0.5
    nq, nk = SWA_HEADS * SWA_HEAD_DIM, SWA_KV_HEADS * SWA_HEAD_DIM

    def proj(h):
        p = h @ w_in
        n = h.shape[1]
        q = p[..., :nq].reshape(B, n, SWA_KV_HEADS, SWA_GROUP, SWA_HEAD_DIM)
        k = p[..., nq:nq + nk].reshape(B, n, SWA_KV_HEADS, SWA_HEAD_DIM)
        v = p[..., nq + nk:].reshape(B, n, SWA_KV_HEADS, SWA_HEAD_DIM)
        return q, k, v

    qc, kc, vc = proj(hc)
    qx, kx, vx = proj(hx)
    qx = apply_axial_rope(qx, cos[:, None, None, :], sin[:, None, None, :]) * scale
    kx = apply_axial_rope(kx, cos[:, None, :], sin[:, None, :])
    qc = qc * scale

    pad = ((0, 0), (SWA_WINDOW, SWA_WINDOW), (0, 0), (0, 0))
    kx_p, vx_p = jnp.pad(kx, pad), jnp.pad(vx, pad)
    span = SWA_BLOCK + 2 * SWA_WINDOW

    def block(j):
        start = j * SWA_BLOCK
        qb = lax.dynamic_slice_in_dim(qx, start, SWA_BLOCK, axis=1)
        kb = lax.dynamic_slice_in_dim(kx_p, start, span, axis=1)
        vb = lax.dynamic_slice_in_dim(vx_p, start, span, axis=1)
        qpos = start + jnp.arange(SWA_BLOCK)
        kpos = start - SWA_WINDOW + jnp.arange(span)
        valid = ((jnp.abs(qpos[:, None] - kpos[None, :]) <= SWA_WINDOW)
                 & (kpos[None, :] >= 0) & (kpos[None, :] < L))
        lw = jnp.where(valid, jnp.einsum('bqkgd,bskd->bkgqs', qb, kb).astype(F32), NEG_INF)
        lc = jnp.einsum('bqkgd,bskd->bkgqs', qb, kc).astype(F32)
        p = sink_softmax(jnp.concatenate([lc, lw], axis=-1), sink).astype(vb.dtype)
        return (jnp.einsum('bkgqs,bskd->bqkgd', p[..., :Lc], vc)
                + jnp.einsum('bkgqs,bskd->bqkgd', p[..., Lc:], vb))

    o = lax.map(block, jnp.arange(L // SWA_BLOCK))
    o = jnp.moveaxis(o, 0, 1).reshape(B, L, nq)
    yx = o @ wo
    yc = None
    if need_ctx:
        pc = sink_softmax(jnp.einsum('bqkgd,bskd->bkgqs', qc, kc).astype(F32), sink).astype(vc.dtype)
        oc = jnp.einsum('bkgqs,bskd->bqkgd', pc, vc).reshape(B, Lc, nq)
        yc = oc @ wo
    return yx, yc


def gmlp_mixer(h, w_in, ln_g, ln_b, ws, bs, wo):
    B, T, _ = h.shape
    p = jax.nn.gelu(h @ w_in)
    u, v = p[..., :GMLP_WIDTH], p[..., GMLP_WIDTH:]
    v = layernorm(v, ln_g, ln_b)
    vg = v.reshape(B, T // GMLP_CHUNK, GMLP_CHUNK, GMLP_GROUPS, GMLP_GROUP_DIM)
    mixed = jnp.einsum('gij,bnjgc->bnigc', ws, vg) + bs.T[None, None, :, :, None]
    return (u * mixed.reshape(B, T, GMLP_WIDTH)) @ wo


def setup_inputs(seed: int = 0) -> dict:
    key = jax.random.key(seed)
    ks = jax.random.split(key, 26)
    D, F = D_MODEL, D_FF
    nrm = jax.random.normal
    return {
        "x": nrm(ks[0], (BATCH, SEQ, D), F32),
        "c": nrm(ks[1], (BATCH, D), F32),
        "ctx": nrm(ks[2], (BATCH, CTX_LEN, D), F32),
        "c_ctx": nrm(ks[3], (D,), F32),
        "ada_w": nrm(ks[4], (DEPTH, D, 6 * D), F32) * (0.5 * D ** -0.5),
        "ada_b": nrm(ks[5], (DEPTH, 6 * D), F32) * 0.02,
        "norm_g": 1.0 + 0.05 * nrm(ks[6], (DEPTH, 4, D), F32),
        "ffn_w1": nrm(ks[7], (DEPTH, D, F), F32) * D ** -0.5,
        "ffn_w3": nrm(ks[8], (DEPTH, D, F), F32) * D ** -0.5,
        "ffn_w2": nrm(ks[9], (DEPTH, F, D), F32) * F ** -0.5,
        "gla_w_in": nrm(ks[10], (N_A, D, GLA_IN), F32) * D ** -0.5,
        "gla_wa2": nrm(ks[11], (N_A, 2, GLA_RANK, GLA_DK), F32) * GLA_RANK ** -0.5,
        "gla_ba": nrm(ks[12], (N_A, 2, GLA_DK), F32) * 0.1,
        "gla_onorm_g": 1.0 + 0.05 * nrm(ks[13], (N_A, GLA_DV), F32),
        "gla_wo": nrm(ks[14], (N_A, GLA_DV, D), F32) * GLA_DV ** -0.5,
        "attn_w_in": nrm(ks[15], (N_B, D, SWA_IN), F32) * D ** -0.5,
        "attn_sink": nrm(ks[16], (N_B, SWA_HEADS), F32) * 0.5,
        "attn_wo": nrm(ks[17], (N_B, SWA_HEADS * SWA_HEAD_DIM, D), F32) * (SWA_HEADS * SWA_HEAD_DIM) ** -0.5,
        "gmlp_w_in": nrm(ks[18], (N_C, D, 2 * GMLP_WIDTH), F32) * D ** -0.5,
        "gmlp_ln_g": 1.0 + 0.05 * nrm(ks[19], (N_C, GMLP_WIDTH), F32),
        "gmlp_ln_b": 0.02 * nrm(ks[20], (N_C, GMLP_WIDTH), F32),
        "gmlp_ws": nrm(ks[21], (N_C, GMLP_GROUPS, GMLP_CHUNK, GMLP_CHUNK), F32) * GMLP_CHUNK ** -0.5,
        "gmlp_bs": 1.0 + 0.1 * nrm(ks[22], (N_C, GMLP_GROUPS, GMLP_CHUNK), F32),
        "gmlp_wo": nrm(ks[23], (N_C, GMLP_WIDTH, D), F32) * GMLP_WIDTH ** -0.5,
    }


def reference(x, c, ctx, c_ctx, ada_w, ada_b, norm_g, ffn_w1, ffn_w3, ffn_w2,
              gla_w_in, gla_wa2, gla_ba, gla_onorm_g, gla_wo,
              attn_w_in, attn_sink, attn_wo,
              gmlp_w_in, gmlp_ln_g, gmlp_ln_b, gmlp_ws, gmlp_bs, gmlp_wo):
    L = x.shape[1]
    cos, sin = axial_rope_tables(L)
    cx = ctx
    for i in range(DEPTH):
        last = i == DEPTH - 1
        kind, slot = i % N_MIXERS, i // N_MIXERS
        mod_x = (jax.nn.silu(c) @ ada_w[i] + ada_b[i])[:, None, :]
        mod_c = (jax.nn.silu(c_ctx) @ ada_w[i] + ada_b[i])[None, None, :]
        shm, scm, gm, shf, scf, gf = jnp.split(mod_x, 6, axis=-1)
        shm_c, scm_c, gm_c, shf_c, scf_c, gf_c = jnp.split(mod_c, 6, axis=-1)

        hx = rmsnorm(x, norm_g[i, 0]) * (1.0 + scm) + shm
        hc = rmsnorm(cx, norm_g[i, 0]) * (1.0 + scm_c) + shm_c
        if kind == 0:
            yx, yc = gla_mixer(hx, hc, gla_w_in[slot], gla_wa2[slot], gla_ba[slot],
                               gla_onorm_g[slot], gla_wo[slot], not last)
        elif kind == 1:
            yx, yc = swa_mixer(hx, hc, attn_w_in[slot], attn_sink[slot], attn_wo[slot],
                               cos, sin, not last)
        else:
            gp = (gmlp_w_in[slot], gmlp_ln_g[slot], gmlp_ln_b[slot], gmlp_ws[slot],
                  gmlp_bs[slot], gmlp_wo[slot])
            yx = gmlp_mixer(hx, *gp)
            yc = None if last else gmlp_mixer(hc, *gp)
        x = x + gm * rmsnorm(yx, norm_g[i, 1])
        if not last:
            cx = cx + gm_c * rmsnorm(yc, norm_g[i, 1])

        hx = rmsnorm(x, norm_g[i, 2]) * (1.0 + scf) + shf
        x = x + gf * rmsnorm(swiglu(hx, ffn_w1[i], ffn_w3[i], ffn_w2[i]), norm_g[i, 3])
        if not last:
            hc = rmsnorm(cx, norm_g[i, 2]) * (1.0 + scf_c) + shf_c
            cx = cx + gf_c * rmsnorm(swiglu(hc, ffn_w1[i], ffn_w3[i], ffn_w2[i]), norm_g[i, 3])
    return x
```

```python
import numpy as np
import ml_dtypes
from contextlib import ExitStack
import concourse.bass as bass
import concourse.mybir as mybir
from concourse.bass_utils import run_bass_kernel_spmd

F32 = mybir.dt.float32
BF16 = mybir.dt.bfloat16
AF = mybir.ActivationFunctionType
ALU = mybir.AluOpType

D = 2048
KC = 16
NCTX = 256
NLAT = 1024
NT = NCTX + NLAT
FF = 5632
NFC = FF // 128
EPS = 1e-6
DEPTH = 4
NSLOT = 16
A_OFF, A_SZ = 0, 40960
BC_OFF, BC_SZ = 40960, 81920
D_OFF, D_SZ = 122880, 20224
ARENA_B = D_OFF + D_SZ

TILES_ALL = [(0, 512), (512, 512), (1024, 256)]
TILES_LAT = [(256, 512), (768, 512)]
PIECES_ALL = [(0, 256, 1), (256, 768, 0), (768, 1280, 0)]
PIECES_LAT = [(256, 768, 0), (768, 1280, 0)]


class Sem:
    def __init__(self, h):
        self.h = h
        self.cnt = 0


class Buf:
    __slots__ = ("w", "r")

    def __init__(self):
        self.w = {}
        self.r = {}


class Grid:
    def __init__(self):
        self.d = {}

    def __getitem__(self, key):
        b = self.d.get(key)
        if b is None:
            b = self.d[key] = Buf()
        return b


def cbs(c0, c1):
    return range(c0 // 256, (c1 + 255) // 256)


class ActT:
    def __init__(self, ap):
        self.ap = ap
        self.g = Grid()

    def b(self, k, c0, c1):
        return [self.g[(k, cb)] for cb in cbs(c0, c1)]

    def ball(self, ks, c0, c1):
        out = []
        for k in ks:
            out += self.b(k, c0, c1)
        return out


class Sched:
    def __init__(self, nc, es):
        self.nc = nc
        self.E = {"pe": nc.tensor, "act": nc.scalar, "dve": nc.vector, "pool": nc.gpsimd, "sp": nc.sync}
        self.es = es
        self.esem = {k: Sem(es.enter_context(nc.semaphore("e_" + k))) for k in self.E}
        self.seen = {k: {} for k in self.E}
        self.pending_dma = {}

    def new_sem(self, name):
        return Sem(self.es.enter_context(self.nc.semaphore(name)))

    def _deps(self, reads, writes):
        deps = {}
        for b in reads:
            for s, v in b.w.items():
                if deps.get(s, 0) < v:
                    deps[s] = v
        for b in writes:
            for s, v in b.w.items():
                if deps.get(s, 0) < v:
                    deps[s] = v
            for s, v in b.r.items():
                if deps.get(s, 0) < v:
                    deps[s] = v
        return deps

    def _wait(self, eng, deps):
        seen = self.seen[eng]
        e = self.E[eng]
        pes = self.esem["pe"]
        for s, v in deps.items():
            if eng == "pe" and s is pes:
                continue
            if seen.get(s, 0) < v:
                e.wait_ge(s.h, v)
                seen[s] = v

    def _commit(self, t, reads, writes, fresh):
        s, v = t
        for b in reads:
            if b.r.get(s, 0) < v:
                b.r[s] = v
        for b in writes:
            if fresh:
                b.w = {s: v}
            else:
                b.w[s] = v
            b.r = {}

    def op(self, eng, fn, reads=(), writes=(), fresh=True):
        self._wait(eng, self._deps(reads, writes))
        ins = fn(self.E[eng])
        sem = self.esem[eng]
        sem.cnt += 1
        ins.then_inc(sem.h, 1)
        t = (sem, sem.cnt)
        self._commit(t, reads, writes, fresh)
        return t

    def group(self, eng, fns, reads=(), writes=(), fresh=True):
        self._wait(eng, self._deps(reads, writes))
        ins = None
        for fn in fns:
            ins = fn(self.E[eng])
        sem = self.esem[eng]
        sem.cnt += 1
        ins.then_inc(sem.h, 1)
        t = (sem, sem.cnt)
        self._commit(t, reads, writes, fresh)
        return t

    def dma(self, q, sem, pairs, reads=(), writes=(), fresh=True):
        self._wait(q, self._deps(reads, writes))
        e = self.E[q]
        for (o, i) in pairs:
            e.dma_start(out=o, in_=i).then_inc(sem.h, 16)
            sem.cnt += 16
        t = (sem, sem.cnt)
        self._commit(t, reads, writes, fresh)
        self.pending_dma[sem] = sem.cnt
        return t

    def barrier(self, engs=("act", "dve", "sp")):
        deps = {self.esem[k]: self.esem[k].cnt for k in ("pe", "act", "dve") if self.esem[k].cnt}
        deps.update(self.pending_dma)
        for eng in engs:
            self._wait(eng, deps)

    def final_wait(self):
        self._wait("sp", dict(self.pending_dma))


class Ctx:
    def __init__(self, nc, es):
        self.nc = nc
        self.es = es
        self.S = Sched(nc, es)
        self.arena = es.enter_context(nc.sbuf_tensor("arena", [128, ARENA_B // 4], F32))
        self.ring = es.enter_context(nc.sbuf_tensor("ring", [128, NSLOT, 2048], BF16))
        self.psum = es.enter_context(nc.psum_tensor("psum", [128, 8, 512], F32))
        self.bank = [Buf() for _ in range(8)]
        self.slot = [Buf() for _ in range(NSLOT)]
        self.slot_sem = [self.S.new_sem(f"ws{i}") for i in range(NSLOT)]
        self.rp = 0
        self.dsems = [self.S.new_sem(f"d{i}") for i in range(48)]
        self.bank_rr = {}
        self.ones = es.enter_context(nc.sbuf_tensor("ones", [128, 128], BF16))
        self.ident = es.enter_context(nc.sbuf_tensor("ident_sb", [128, 128], BF16))
        self.epst = es.enter_context(nc.sbuf_tensor("epst", [128, 1], F32))
        self.cst = es.enter_context(nc.sbuf_tensor("cst", [128, 2], F32))
        self.sel = es.enter_context(nc.sbuf_tensor("sel_sb", [128, 2], F32))
        self.cc_sem = self.S.new_sem("cc")
        self.coefs = [es.enter_context(nc.sbuf_tensor(f"coef{i}", [128, 6, 2, KC], F32)) for i in range(2)]
        self.coef_bs = [Buf(), Buf()]
        self.mod = es.enter_context(nc.sbuf_tensor("mod", [128, 96, 2], F32))
        self.c2f = es.enter_context(nc.sbuf_tensor("c2f", [128, KC, 2], F32))
        self.c2b = es.enter_context(nc.sbuf_tensor("c2b", [128, KC, 2], BF16))
        self.abt = es.enter_context(nc.sbuf_tensor("abt", [128, 96], F32))
        self.ngt = es.enter_context(nc.sbuf_tensor("ngt", [128, 4, KC], F32))
        self.ada_in_b = Buf()
        self.cb = Buf()
        self.mod_b = Buf()
        self.cur_l = 0

    @property
    def coef(self):
        return self.coefs[self.cur_l % 2]

    @property
    def coef_b(self):
        return self.coef_bs[self.cur_l % 2]

    def carve(self, off_b, dtype, shape):
        n = int(np.prod(shape))
        esz = 4 if dtype is F32 else 2
        a = self.arena[:, off_b // 4:(off_b + n * esz) // 4]
        if dtype is not F32:
            a = a.bitcast(dtype)
        if len(shape) == 2:
            return a.rearrange("p (a b) -> p a b", a=shape[0])
        if len(shape) == 3:
            return a.rearrange("p (a b c) -> p a b c", a=shape[0], b=shape[1])
        return a

    def next_bank(self, banks):
        i = self.bank_rr.get(banks, 0)
        self.bank_rr[banks] = (i + 1) % len(banks)
        return banks[i]

    def load_w(self, in_ap, nslots, view):
        if self.rp + nslots > NSLOT:
            self.rp = 0
        s0 = self.rp
        self.rp += nslots
        bufs = self.slot[s0:s0 + nslots]
        span = self.ring[:, s0:s0 + nslots, :]
        out = view(span)
        self.S.dma("pool", self.slot_sem[s0], [(out, in_ap)], writes=bufs)
        return out, bufs

    def mm(self, out_ps, pairs, reads, bank_b):
        n = len(pairs)
        fns = []
        for i, (l, r) in enumerate(pairs):
            fns.append(lambda e, l=l, r=r, i=i: e.matmul(out_ps, lhsT=l, rhs=r, start=(i == 0), stop=(i == n - 1)))
        return self.S.group("pe", fns, reads=reads, writes=[bank_b])


def ada_inputs(C, l, T):
    S = C.S
    tb = Buf()
    S.dma("sp", C.dsems[40], [(C.c2f[:], T["c2"]), (C.abt[:], T[f"ada_b{l}"]), (C.ngt[:], T[f"norm_g{l}"])], writes=[tb, C.ada_in_b])
    S.op("act", lambda e: e.activation(out=C.c2b[:], in_=C.c2f[:], func=AF.Silu), reads=[tb], writes=[C.ada_in_b], fresh=False)


def ada_quad(C, l, T, m0, bank):
    ps = C.psum[:, bank, :192].rearrange("p (m c) -> p m c", c=2)
    wd = T[f"ada_w{l}"]
    w, wb = C.load_w(wd[m0:m0 + 4].rearrange("c p k m -> p c k m"), 4,
                     lambda sp: sp.rearrange("p c (k m) -> p c k m", k=KC))
    for j in range(4):
        C.mm(ps[:, m0 + j, :], [(w[:, j, k, :], C.c2b[:, k, :]) for k in range(KC)], reads=wb + [C.ada_in_b], bank_b=C.bank[bank])


def ada_finish(C, l, bank, subs_idx):
    S = C.S
    coef = C.coefs[l % 2]
    coef_b = C.coef_bs[l % 2]
    ps = C.psum[:, bank, :192].rearrange("p (m c) -> p m c", c=2)
    src_chunk = {0: 1, 1: 0, 2: 2, 3: 4, 4: 3, 5: 5}
    for ci in subs_idx:
        mc0 = src_chunk[ci] * 16
        sub = ci // 3
        for c in range(2):
            S.op("dve", lambda e: e.tensor_tensor(out=C.mod[:, mc0:mc0 + 16, c], in0=ps[:, mc0:mc0 + 16, c], in1=C.abt[:, mc0:mc0 + 16], op=ALU.add),
                 reads=[C.bank[bank], C.ada_in_b], writes=[C.mod_b], fresh=False)
            m = C.mod[:, mc0:mc0 + 16, c]
            if ci % 3 == 0:
                S.op("dve", lambda e: e.scalar_tensor_tensor(out=coef[:, ci, c, :], in0=m, scalar=1.0, in1=C.ngt[:, 2 * sub, :],
                                                              op0=ALU.add, op1=ALU.mult),
                     reads=[C.mod_b, C.ada_in_b], writes=[coef_b], fresh=False)
            elif ci % 3 == 1:
                S.op("dve", lambda e: e.tensor_copy(out=coef[:, ci, c, :], in_=m), reads=[C.mod_b], writes=[coef_b], fresh=False)
            else:
                S.op("dve", lambda e: e.tensor_tensor(out=coef[:, ci, c, :], in0=m, in1=C.ngt[:, 2 * sub + 1, :], op=ALU.mult),
                     reads=[C.mod_b, C.ada_in_b], writes=[coef_b], fresh=False)


def ada_gen(C, l, T, bank=7):
    ada_inputs(C, l, T)
    for m0 in range(0, 96, 4):
        ada_quad(C, l, T, m0, bank)
        if m0 < 92:
            yield
    ada_finish(C, l, bank, [0, 1, 2, 3, 4, 5])
    yield


def ada_now(C, l, T):
    for _ in ada_gen(C, l, T):
        pass
    C.cur_l = l


def sumsq_rstd(C, src, tiles, rstd, rstd_b, sqb, sqb_b, tmpsd):
    S = C.S
    c0 = tiles[0][0]
    c1 = tiles[-1][0] + tiles[-1][1]
    banks = [5, 6, 7][:len(tiles)]
    for k in range(KC):
        sq = sqb[k % 2]
        S.op("act", lambda e, sq=sq, k=k: e.activation(out=sq[:, c0:c1], in_=src.ap[:, k, c0:c1], func=AF.Square),
             reads=src.b(k, c0, c1), writes=[sqb_b[k % 2]])
        fns = []
        for (t0, tn), bk in zip(tiles, banks):
            fns.append(lambda e, sq=sq, t0=t0, tn=tn, bk=bk, k=k: e.matmul(
                C.psum[:, bk, :tn], lhsT=C.ones[:], rhs=sq[:, t0:t0 + tn], start=(k == 0), stop=(k == KC - 1)))
        S.group("pe", fns, reads=[sqb_b[k % 2], C.cb], writes=[C.bank[b] for b in banks], fresh=(k == 0))
    for (t0, tn), bk in zip(tiles, banks):
        S.op("act", lambda e, t0=t0, tn=tn, bk=bk: e.activation(out=tmpsd[:, :tn], in_=C.psum[:, bk, :tn], func=AF.Sqrt,
                                                                  bias=C.epst[:], scale=1.0 / D),
             reads=[C.bank[bk], C.cb], writes=[rstd_b[1]])
        S.op("dve", lambda e, t0=t0, tn=tn: e.reciprocal(out=rstd[:, t0:t0 + tn], in_=tmpsd[:, :tn]),
             reads=[rstd_b[1]], writes=[rstd_b[0]], fresh=False)


def prenorm(C, T, sub, last):
    S = C.S
    pieces = [(256, 1280, 0)] if last else [(0, 256, 1), (256, 1280, 0)]
    tiles = TILES_LAT if last else TILES_ALL
    S.barrier()
    xs = ActT(C.carve(BC_OFF, F32, [KC, NT]))
    h = ActT(C.carve(A_OFF, BF16, [KC, NT]))
    sqb = [C.carve(D_OFF, BF16, [NT]), C.carve(D_OFF + 2560, BF16, [NT])]
    sqb_b = [Buf(), Buf()]
    rstd = C.carve(D_OFF + 5120, F32, [NT])
    rstd_b = [Buf(), Buf()]
    tmps = [C.carve(D_OFF + 10240, F32, [1024]), C.carve(D_OFF + 14336, F32, [1024])]
    tmps_b = [Buf(), Buf()]
    tmpsd = C.carve(D_OFF + 14336, F32, [512])
    xv = T["XD"].rearrange("(k p) t -> p k t", p=128)
    c0 = pieces[0][0]
    for q in range(4):
        S.dma("sp", C.dsems[q], [(xs.ap[:, 4 * q:4 * q + 4, c0:NT], xv[:, 4 * q:4 * q + 4, c0:NT])],
              reads=[T["XD_b"][k] for k in range(4 * q, 4 * q + 4)], writes=xs.ball(range(4 * q, 4 * q + 4), c0, NT))
    sumsq_rstd(C, xs, tiles, rstd, rstd_b, sqb, sqb_b, tmpsd)
    i = 0
    for k in range(KC):
        for (q0, q1, mc) in pieces:
            tmp = tmps[i % 2]
            tb = tmps_b[i % 2]
            i += 1
            S.op("dve", lambda e: e.scalar_tensor_tensor(
                out=tmp[:, :q1 - q0], in0=xs.ap[:, k, q0:q1], scalar=C.coef[:, 3 * sub, mc, k:k + 1],
                in1=rstd[:, q0:q1], op0=ALU.mult, op1=ALU.mult),
                reads=xs.b(k, q0, q1) + [rstd_b[0], rstd_b[1], C.coef_b], writes=[tb])
            S.op("act", lambda e: e.activation(
                out=h.ap[:, k, q0:q1], in_=tmp[:, :q1 - q0], func=AF.Identity,
                bias=C.coef[:, 3 * sub + 1, mc, k:k + 1], scale=1.0),
                reads=[tb, C.coef_b], writes=h.b(k, q0, q1))
    return h


def postnorm_resid(C, T, y, sub, last):
    S = C.S
    pieces = [(256, 1280, 0)] if last else [(0, 256, 1), (256, 1280, 0)]
    tiles = TILES_LAT if last else TILES_ALL
    c0 = pieces[0][0]
    S.barrier()
    sqb = [C.carve(D_OFF, BF16, [NT]), C.carve(D_OFF + 2560, BF16, [NT])]
    sqb_b = [Buf(), Buf()]
    rstd = C.carve(D_OFF + 5120, F32, [NT])
    rstd_b = [Buf(), Buf()]
    tmps = [C.carve(D_OFF + 10240, F32, [1024]), C.carve(D_OFF + 14336, F32, [1024])]
    tmps_b = [Buf(), Buf()]
    tmpsd = C.carve(D_OFF + 14336, F32, [512])
    xk = [C.carve(A_OFF + j * 5120, F32, [NT]) for j in range(4)]
    xk_b = [Buf() for _ in range(4)]
    xv = T["XD"].rearrange("(k p) t -> p k t", p=128)
    sumsq_rstd(C, y, tiles, rstd, rstd_b, sqb, sqb_b, tmpsd)
    i = 0
    for k in range(KC):
        xb = xk[k % 4]
        xbb = xk_b[k % 4]
        S.dma("sp", C.dsems[8 + k % 4], [(xb[:, c0:NT], xv[:, k, c0:NT])], reads=[T["XD_b"][k]], writes=[xbb])
        for (q0, q1, mc) in pieces:
            tmp = tmps[i % 2]
            tb = tmps_b[i % 2]
            i += 1
            S.op("dve", lambda e: e.scalar_tensor_tensor(
                out=tmp[:, :q1 - q0], in0=y.ap[:, k, q0:q1], scalar=C.coef[:, 3 * sub + 2, mc, k:k + 1],
                in1=rstd[:, q0:q1], op0=ALU.mult, op1=ALU.mult),
                reads=y.b(k, q0, q1) + [rstd_b[0], rstd_b[1], C.coef_b], writes=[tb])
            S.op("pool", lambda e: e.tensor_tensor(out=xb[:, q0:q1], in0=xb[:, q0:q1], in1=tmp[:, :q1 - q0], op=ALU.add),
                 reads=[tb], writes=[xbb], fresh=False)
        S.dma("sp", C.dsems[12 + k % 4], [(xv[:, k, c0:NT], xb[:, c0:NT])], reads=[xbb], writes=[T["XD_b"][k]])


def post_pre(C, T, y, sub_post, l_post, sub_pre, l_pre, last):
    S = C.S
    pieces = [(256, 1280, 0)] if last else [(0, 256, 1), (256, 1280, 0)]
    tiles = TILES_LAT if last else TILES_ALL
    c0 = pieces[0][0]
    cpo, cpo_b = C.coefs[l_post % 2], C.coef_bs[l_post % 2]
    cpr, cpr_b = C.coefs[l_pre % 2], C.coef_bs[l_pre % 2]
    S.barrier()
    sqb = [C.carve(D_OFF, BF16, [NT]), C.carve(D_OFF + 2560, BF16, [NT])]
    sqb_b = [Buf(), Buf()]
    rstd = C.carve(D_OFF + 5120, F32, [NT])
    rstd_b = [Buf(), Buf()]
    tmpsd = C.carve(D_OFF, F32, [512])
    tmps = [C.carve(D_OFF + 10240, F32, [1024]), C.carve(D_OFF + 14336, F32, [1024])]
    tmps_b = [Buf(), Buf()]
    xk = [C.carve(A_OFF + j * 5120, F32, [NT]) for j in range(4)]
    xk_b = [Buf() for _ in range(4)]
    h = ActT(C.carve(A_OFF, BF16, [KC, NT]))
    xv = T["XD"].rearrange("(k p) t -> p k t", p=128)
    sumsq_rstd(C, y, tiles, rstd, rstd_b, sqb, sqb_b, C.carve(D_OFF + 10240, F32, [512]))
    banks = [5, 6, 7][:len(tiles)]
    i = 0
    for k in range(KC):
        xb = xk[k % 4]
        xbb = xk_b[k % 4]
        S.dma("sp", C.dsems[8 + k % 4], [(xb[:, c0:NT], xv[:, k, c0:NT])], reads=[T["XD_b"][k]], writes=[xbb])
        for (q0, q1, mc) in pieces:
            tmp = tmps[i % 2]
            tb = tmps_b[i % 2]
            i += 1
            S.op("dve", lambda e: e.scalar_tensor_tensor(
                out=tmp[:, :q1 - q0], in0=y.ap[:, k, q0:q1], scalar=cpo[:, 3 * sub_post + 2, mc, k:k + 1],
                in1=rstd[:, q0:q1], op0=ALU.mult, op1=ALU.mult),
                reads=y.b(k, q0, q1) + [rstd_b[0], rstd_b[1], cpo_b], writes=[tb])
            S.op("pool", lambda e: e.tensor_tensor(out=y.ap[:, k, q0:q1], in0=xb[:, q0:q1], in1=tmp[:, :q1 - q0], op=ALU.add),
                 reads=[tb, xbb], writes=y.b(k, q0, q1))
        S.dma("sp", C.dsems[12 + k % 4], [(xv[:, k, c0:NT], y.ap[:, k, c0:NT])], reads=y.b(k, c0, NT), writes=[T["XD_b"][k]])
        sq = sqb[k % 2]
        S.op("act", lambda e: e.activation(out=sq[:, c0:NT], in_=y.ap[:, k, c0:NT], func=AF.Square),
             reads=y.b(k, c0, NT) + ([rstd_b[1]] if k == 0 else []), writes=[sqb_b[k % 2]])
        fns = []
        for (t0, tn), bk in zip(tiles, banks):
            fns.append(lambda e, t0=t0, tn=tn, bk=bk: e.matmul(
                C.psum[:, bk, :tn], lhsT=C.ones[:], rhs=sq[:, t0:t0 + tn], start=(k == 0), stop=(k == KC - 1)))
        S.group("pe", fns, reads=[sqb_b[k % 2], C.cb], writes=[C.bank[b] for b in banks], fresh=(k == 0))
    for (t0, tn), bk in zip(tiles, banks):
        S.op("act", lambda e: e.activation(out=tmpsd[:, :tn], in_=C.psum[:, bk, :tn], func=AF.Sqrt, bias=C.epst[:], scale=1.0 / D),
             reads=[C.bank[bk], C.cb], writes=[rstd_b[1], sqb_b[0]])
        S.op("dve", lambda e: e.reciprocal(out=rstd[:, t0:t0 + tn], in_=tmpsd[:, :tn]),
             reads=[rstd_b[1]], writes=[rstd_b[0]], fresh=(t0 == tiles[0][0]))
    for k in range(KC):
        for (q0, q1, mc) in pieces:
            tmp = tmps[i % 2]
            tb = tmps_b[i % 2]
            i += 1
            S.op("dve", lambda e: e.scalar_tensor_tensor(
                out=tmp[:, :q1 - q0], in0=y.ap[:, k, q0:q1], scalar=cpr[:, 3 * sub_pre, mc, k:k + 1],
                in1=rstd[:, q0:q1], op0=ALU.mult, op1=ALU.mult),
                reads=y.b(k, q0, q1) + [rstd_b[0], cpr_b], writes=[tb])
            S.op("act", lambda e: e.activation(
                out=h.ap[:, k, q0:q1], in_=tmp[:, :q1 - q0], func=AF.Identity,
                bias=cpr[:, 3 * sub_pre + 1, mc, k:k + 1], scale=1.0),
                reads=[tb, cpr_b], writes=h.b(k, q0, q1) + xk_b)
    return h


def ffn(C, T, l, h, last, hook=None):
    S = C.S
    tiles = TILES_LAT if last else TILES_ALL
    S.barrier()
    y = ActT(C.carve(BC_OFF, F32, [KC, NT]))
    hid = [C.carve(D_OFF, BF16, [2, NT]), C.carve(D_OFF + 5120, BF16, [2, NT])]
    hid_g = [Grid(), Grid()]
    sl = [C.carve(D_OFF + 10240 + j * 2048, F32, [512]) for j in range(3)]
    sl_b = [Buf() for _ in range(3)]
    w1d, w3d, w2d = T[f"w1_{l}"], T[f"w3_{l}"], T[f"w2_{l}"]
    si = 0
    for g in range(NFC // 2):
        vw = lambda sp: sp.rearrange("p c (k m) -> p c k m", k=KC)
        w1, w1b = C.load_w(w1d[2 * g:2 * g + 2].rearrange("c p k m -> p c k m"), 2, vw)
        w3, w3b = C.load_w(w3d[2 * g:2 * g + 2].rearrange("c p k m -> p c k m"), 2, vw)
        w2, w2b = C.load_w(w2d[256 * g:256 * g + 256, :].rearrange("(c p) n -> p c n", p=128), 2, lambda sp: sp)
        hd = hid[g % 2]
        hg = hid_g[g % 2]
        for j in range(2):
            for (t0, tn) in tiles:
                ba = C.next_bank((0, 1))
                bb = C.next_bank((2, 3))
                C.mm(C.psum[:, ba, :tn], [(w1[:, j, k, :], h.ap[:, k, t0:t0 + tn]) for k in range(KC)],
                     reads=w1b + h.ball(range(KC), t0, t0 + tn), bank_b=C.bank[ba])
                C.mm(C.psum[:, bb, :tn], [(w3[:, j, k, :], h.ap[:, k, t0:t0 + tn]) for k in range(KC)],
                     reads=w3b + h.ball(range(KC), t0, t0 + tn), bank_b=C.bank[bb])
                s_ = sl[si % 3]
                sb = sl_b[si % 3]
                si += 1
                S.op("act", lambda e, s_=s_, ba=ba, tn=tn: e.activation(out=s_[:, :tn], in_=C.psum[:, ba, :tn], func=AF.Silu),
                     reads=[C.bank[ba]], writes=[sb])
                S.op("dve", lambda e, s_=s_, bb=bb, tn=tn, t0=t0, j=j, hd=hd: e.tensor_tensor(
                    out=hd[:, j, t0:t0 + tn], in0=s_[:, :tn], in1=C.psum[:, bb, :tn], op=ALU.mult),
                    reads=[sb, C.bank[bb]], writes=[hg[(j, t0)]])
        for dc in range(KC):
            for (t0, tn) in tiles:
                by = C.next_bank((4, 5, 6))
                C.mm(C.psum[:, by, :tn], [(w2[:, j, dc * 128:(dc + 1) * 128], hd[:, j, t0:t0 + tn]) for j in range(2)],
                     reads=w2b + [hg[(0, t0)], hg[(1, t0)]], bank_b=C.bank[by])
                if g == 0:
                    S.op("dve", lambda e, dc=dc, t0=t0, tn=tn, by=by: e.tensor_copy(out=y.ap[:, dc, t0:t0 + tn], in_=C.psum[:, by, :tn]),
                         reads=[C.bank[by]], writes=y.b(dc, t0, t0 + tn))
                else:
                    S.op("dve", lambda e, dc=dc, t0=t0, tn=tn, by=by: e.tensor_tensor(
                        out=y.ap[:, dc, t0:t0 + tn], in0=y.ap[:, dc, t0:t0 + tn], in1=C.psum[:, by, :tn], op=ALU.add),
                        reads=[C.bank[by]], writes=y.b(dc, t0, t0 + tn))
        if hook is not None:
            hook(g)
    return y


def linear_fm(C, wds, n_mc, src, tiles, epilogue, per_load=2, bank_sets=((0, 1), (2, 3))):
    for m0 in range(0, n_mc, per_load):
        n = min(per_load, n_mc - m0)
        ws = []
        for wd in wds:
            ws.append(C.load_w(wd[m0:m0 + n].rearrange("c p k m -> p c k m"), n,
                               lambda sp: sp.rearrange("p c (k m) -> p c k m", k=KC)))
        for j in range(n):
            for (t0, tn) in tiles:
                bks = []
                for wi, (w, wb) in enumerate(ws):
                    bk = C.next_bank(bank_sets[wi])
                    C.mm(C.psum[:, bk, :tn], [(w[:, j, k, :], src.ap[:, k, t0:t0 + tn]) for k in range(KC)],
                         reads=wb + src.ball(range(KC), t0, t0 + tn), bank_b=C.bank[bk])
                    bks.append(bk)
                epilogue(m0 + j, t0, tn, bks)


def linear_tm(C, wd, n_nb, nbw, src, tok_tiles, epilogue, banks=(0, 1, 2, 3)):
    nsl = nbw * KC // 2048
    for nb in range(n_nb):
        w, wb = C.load_w(wd[nb], nsl, lambda sp: sp.rearrange("p c (k m) -> p (c k) m", m=nbw))
        for ti, (t0, tn) in enumerate(tok_tiles):
            bk = C.next_bank(banks)
            C.mm(C.psum[:tn, bk, :nbw], [(src.ap[:, k, t0:t0 + tn], w[:, k, :]) for k in range(KC)],
                 reads=wb + src.ball(range(KC), t0, t0 + tn), bank_b=C.bank[bk])
            epilogue(nb, ti, t0, tn, bk)


def out_proj(C, wd, oT, tiles):
    S = C.S
    y = ActT(C.carve(BC_OFF, F32, [KC, NT]))
    cnt = [0]

    def epi(mc, t0, tn, bks):
        bk = bks[0]
        eng = "act" if cnt[0] % 2 == 0 else "dve"
        cnt[0] += 1
        if eng == "act":
            S.op("act", lambda e: e.activation(out=y.ap[:, mc, t0:t0 + tn], in_=C.psum[:, bk, :tn], func=AF.Copy),
                 reads=[C.bank[bk]], writes=y.b(mc, t0, t0 + tn))
        else:
            S.op("dve", lambda e: e.tensor_copy(out=y.ap[:, mc, t0:t0 + tn], in_=C.psum[:, bk, :tn]),
                 reads=[C.bank[bk]], writes=y.b(mc, t0, t0 + tn))
    linear_fm(C, [wd], KC, oT, tiles, epi, bank_sets=((0, 1, 2, 3),))
    return y


TOK_TILES = [(i * 128, 128) for i in range(NT // 128)]
PAIRS = [[0, 1], [2, 3], [4, 5], [6, 7]]


def pair_allgather(C, src, dst, src_b, dst_b):
    S = C.S
    S._wait("pool", S._deps([src_b], [dst_b]))
    ins = C.nc.gpsimd.collective_compute("AllGather", ALU.bypass, replica_groups=PAIRS, ins=[src.opt()], outs=[dst.opt()])
    sem = C.cc_sem
    ins.then_inc(sem.h)
    sem.cnt += 1
    t = (sem, sem.cnt)
    S._commit(t, [src_b], [dst_b], True)
    S.pending_dma[sem] = sem.cnt


def gmlp_mixer(C, T, h):
    S = C.S
    S.barrier()
    vtm = C.carve(BC_OFF, BF16, [NT // 128, D])
    vg = Grid()
    gB = C.carve(BC_OFF + 40960, F32, [D])
    bB = C.carve(BC_OFF + 49152, F32, [D])
    tmpf = [C.carve(BC_OFF + 57344, F32, [D]), C.carve(BC_OFF + 65536, F32, [D])]
    tmpf_b = [Buf(), Buf()]
    lnb = Buf()
    S.dma("sp", C.dsems[16], [(gB, T["gm_ln_g"].partition_broadcast(128)), (bB, T["gm_ln_b"].partition_broadcast(128))], writes=[lnb])
    stats = C.carve(D_OFF, F32, [4, 6])
    mv = C.carve(D_OFF + 128, F32, [2])
    sd = C.carve(D_OFF + 192, F32, [1])
    st_b = Buf()

    def epi_v(nb, ti, t0, tn, bk):
        S.op("act", lambda e: e.activation(out=vtm[:, ti, nb * 512:(nb + 1) * 512], in_=C.psum[:, bk, :], func=AF.Gelu_apprx_tanh),
             reads=[C.bank[bk]], writes=[vg[(ti, nb)]])
    linear_tm(C, T["gm_wv"], 4, 512, h, TOK_TILES, epi_v)
    for ti in range(NT // 128):
        for q in range(4):
            S.op("dve", lambda e, q=q: e.bn_stats(out=stats[:, q, :], in_=vtm[:, ti, q * 512:(q + 1) * 512]),
                 reads=[vg[(ti, q)]], writes=[st_b], fresh=(q == 0))
        S.op("dve", lambda e: e.bn_aggr(out=mv, in_=stats), reads=[st_b], writes=[st_b], fresh=False)
        S.op("act", lambda e: e.activation(out=sd, in_=mv[:, 1:2], func=AF.Sqrt, bias=C.epst[:], scale=1.0),
             reads=[st_b, C.cb], writes=[st_b], fresh=False)
        S.op("dve", lambda e: e.reciprocal(out=sd, in_=sd), reads=[st_b], writes=[st_b], fresh=False)
        tf = tmpf[ti % 2]
        tfb = tmpf_b[ti % 2]
        nmr = C.carve(D_OFF + 224, F32, [1])
        S.op("dve", lambda e: e.tensor_scalar(out=nmr, in0=mv[:, 0:1], scalar1=sd[:, 0:1], scalar2=-1.0, op0=ALU.mult, op1=ALU.mult),
             reads=[], writes=[st_b], fresh=False)
        S.op("act", lambda e: e.activation(out=tf, in_=vtm[:, ti, :], func=AF.Identity, bias=nmr[:, 0:1], scale=sd[:, 0:1]),
             reads=[st_b] + [vg[(ti, q)] for q in range(4)], writes=[tfb])
        S.op("dve", lambda e: e.tensor_tensor(out=tf, in0=tf, in1=gB, op=ALU.mult), reads=[lnb], writes=[tfb], fresh=False)
        S.op("pool", lambda e: e.tensor_tensor(out=vtm[:, ti, :], in0=tf, in1=bB, op=ALU.add),
             reads=[lnb, tfb], writes=[vg[(ti, q)] for q in range(4)])
    S.barrier()
    uT = ActT(C.carve(BC_OFF + 40960, BF16, [KC, NT]))

    def epi_u(mc, t0, tn, bks):
        S.op("act", lambda e: e.activation(out=uT.ap[:, mc, t0:t0 + tn], in_=C.psum[:, bks[0], :tn], func=AF.Gelu_apprx_tanh),
             reads=[C.bank[bks[0]]], writes=uT.b(mc, t0, t0 + tn))
    linear_fm(C, [T["gm_wu"]], KC, h, TILES_ALL, epi_u, bank_sets=((0, 1, 2, 3),))
    S.barrier()
    pT = ActT(C.carve(A_OFF, BF16, [KC, NT]))
    bsB = C.carve(D_OFF + 256, F32, [16, 128])
    bsb = Buf()
    S.dma("sp", C.dsems[17], [(bsB, T["gm_bs"].partition_broadcast(128))], writes=[bsb])
    wst, wsb = C.load_w(T["gm_wsT"], 1, lambda sp: sp.rearrange("p c (g i) -> p (c g) i", i=128))
    tmps = [C.carve(D_OFF + 8448 + j * 2048, F32, [4, 128]) for j in range(2)]
    tmps_b = [Buf(), Buf()]
    i = 0
    for n in range(NT // 128):
        for q in range(4):
            bk = C.next_bank((0, 1, 2, 3))
            psv = C.psum[:, bk, :].rearrange("p (g i) -> p g i", i=128)
            fns = []
            for gg in range(4):
                g = 4 * q + gg
                fns.append(lambda e, g=g, gg=gg: e.matmul(psv[:, gg, :], lhsT=vtm[:, n, g * 128:(g + 1) * 128], rhs=wst[:, g, :],
                                                           start=True, stop=True))
            S.group("pe", fns, reads=wsb + [vg[(n, q)]], writes=[C.bank[bk]])
            tmp = tmps[i % 2]
            tb = tmps_b[i % 2]
            i += 1
            S.op("dve", lambda e: e.tensor_tensor(out=tmp, in0=psv, in1=bsB[:, 4 * q:4 * q + 4, :], op=ALU.add),
                 reads=[C.bank[bk], bsb], writes=[tb])
            S.op("dve", lambda e: e.tensor_tensor(out=pT.ap[:, 4 * q:4 * q + 4, n * 128:(n + 1) * 128], in0=tmp,
                                                   in1=uT.ap[:, 4 * q:4 * q + 4, n * 128:(n + 1) * 128], op=ALU.mult),
                 reads=[tb] + uT.ball(range(4 * q, 4 * q + 4), n * 128, n * 128 + 128),
                 writes=pT.ball(range(4 * q, 4 * q + 4), n * 128, n * 128 + 128), fresh=False)
    S.barrier()
    return out_proj(C, T["gm_wo"], pT, TILES_ALL)


NEG = -1.0e30
AX = mybir.AxisListType


def swa_proj(C, T, h):
    S = C.S
    S.barrier()
    qT = ActT(C.carve(BC_OFF, BF16, [KC, NT]))
    KD = ActT(C.carve(D_OFF, BF16, [4, NT]))
    VT = C.carve(D_OFF + 10240, BF16, [NT // 128, 256])
    vb = Buf()
    co = BC_OFF + 40960
    tabs = [C.carve(co + i * 5120, F32, [NT]) for i in range(4)]
    tab_b = Buf()
    S.dma("sp", C.dsems[16], [(tabs[i], T["rope_tab"][i]) for i in range(4)], writes=[tab_b])
    t1 = [C.carve(co + 20480 + i * 2048, F32, [512]) for i in range(2)]
    t2 = [C.carve(co + 24576 + i * 2048, F32, [512]) for i in range(2)]
    tb = [Buf(), Buf()]
    cnt = [0]

    def mk_epi(dst, cs, sn):
        def epi(mc, t0, tn, bks):
            i = cnt[0] % 2
            cnt[0] += 1
            S.op("dve", lambda e: e.tensor_tensor(out=t1[i][:, :tn], in0=C.psum[:, bks[0], :tn], in1=cs[:, t0:t0 + tn], op=ALU.mult),
                 reads=[C.bank[bks[0]], tab_b], writes=[tb[i]])
            S.op("dve", lambda e: e.tensor_tensor(out=t2[i][:, :tn], in0=C.psum[:, bks[1], :tn], in1=sn[:, t0:t0 + tn], op=ALU.mult),
                 reads=[C.bank[bks[1]], tab_b], writes=[tb[i]], fresh=False)
            S.op("dve", lambda e: e.tensor_tensor(out=dst.ap[:, mc, t0:t0 + tn], in0=t1[i][:, :tn], in1=t2[i][:, :tn], op=ALU.add),
                 reads=[tb[i]], writes=dst.b(mc, t0, t0 + tn))
        return epi
    linear_fm(C, [T["sw_wq"], T["sw_wqp"]], KC, h, TILES_ALL, mk_epi(qT, tabs[0], tabs[1]))
    linear_fm(C, [T["sw_wk"], T["sw_wkp"]], 4, h, TILES_ALL, mk_epi(KD, tabs[2], tabs[3]))

    def epi_v(nb, ti, t0, tn, bk):
        S.op("act", lambda e: e.activation(out=VT[:, ti, :], in_=C.psum[:, bk, :256], func=AF.Copy),
             reads=[C.bank[bk]], writes=[vb], fresh=False)
    linear_tm(C, T["sw_wv"], 1, 256, h, TOK_TILES, epi_v)
    S.dma("sp", C.dsems[17], [(T["sw_qT"].rearrange("p (k t) -> p k t", k=KC), qT.ap)], reads=qT.ball(range(KC), 0, NT), writes=[T["sw_b"]], fresh=False)
    S.dma("sp", C.dsems[18], [(T["sw_KD"].rearrange("p (k t) -> p k t", k=4), KD.ap)], reads=KD.ball(range(4), 0, NT), writes=[T["sw_b"]], fresh=False)
    S.dma("sp", C.dsems[19], [(T["sw_VT"].rearrange("p (k t) -> p k t", k=NT // 128), VT)], reads=[vb], writes=[T["sw_b"]], fresh=False)
    if "sw_hx" in T:
        hb = T["gl_b"]["sw_hx"]
        S.dma("sp", C.dsems[20], [(T["sw_hx"][:, 0:512].rearrange("p (k t) -> p k t", k=4), KD.ap[:, :, NT - 128:NT]),
                                   (T["sw_hx"][:, 512:768], VT[:, NT // 128 - 1, :])],
              reads=KD.ball(range(4), NT - 128, NT) + [vb], writes=[hb])
        pair_allgather(C, T["sw_hx"], T["sw_hxg"], hb, T["gl_b"]["sw_hxg"])


def swa_attn(C, T):
    S = C.S
    S.barrier()
    qT = C.carve(BC_OFF, BF16, [KC, NT])
    KD = C.carve(D_OFF, BF16, [4, NT])
    VT = C.carve(D_OFF + 10240, BF16, [NT // 128, 256])
    KDh = C.carve(D_OFF + 15360, BF16, [4, 128])
    VTh = C.carve(D_OFF + 16384, BF16, [256])
    inb = Buf()
    co = BC_OFF + 40960
    prs = [(qT, T["sw_qT"].rearrange("p (k t) -> p k t", k=KC)),
           (KD, T["sw_KD"].rearrange("p (k t) -> p k t", k=4)),
           (VT, T["sw_VT"].rearrange("p (k t) -> p k t", k=NT // 128))]
    if "sw_hxg" in T:
        hA = C.carve(co + 37376, BF16, [768])
        hB = C.carve(co + 38912, BF16, [768])
        hh = C.carve(D_OFF + 15360, BF16, [768])
        S.dma("sp", C.dsems[16], prs + [(hA, T["sw_hxg"][0:128, :]), (hB, T["sw_hxg"][128:256, :])],
              reads=[T["sw_b"], T["gl_b"]["sw_hxg"]], writes=[inb])
        S.op("dve", lambda e: e.tensor_scalar(out=hh, in0=hA, scalar1=C.sel[:, 0:1], scalar2=None, op0=ALU.mult), reads=[C.cb], writes=[inb], fresh=False)
        S.op("dve", lambda e: e.scalar_tensor_tensor(out=hh, in0=hB, scalar=C.sel[:, 1:2], in1=hh, op0=ALU.mult, op1=ALU.add),
             reads=[C.cb], writes=[inb], fresh=False)
    else:
        S.dma("sp", C.dsems[16], prs + [(KDh, T["sw_KDh"].rearrange("p (k t) -> p k t", k=4)), (VTh, T["sw_VTh"])],
              reads=[T["sw_b"]], writes=[inb])
    masks = C.carve(co, F32, [3, 640])
    sinkT = C.carve(co + 7680, F32, [32])
    S.dma("sp", C.dsems[17], [(masks, T["sw_masks"]), (sinkT, T["sw_sink"].partition_broadcast(128))], writes=[inb], fresh=False)
    NP = 4
    sc = [C.carve(co + 8192 + i * 2560, F32, [640]) for i in range(NP)]
    pp = [C.carve(co + 18432 + i * 1280, BF16, [640]) for i in range(NP)]
    pts = [C.carve(co + 23552 + i * 1280, BF16, [640]) for i in range(NP)]
    otm = [C.carve(co + 28672 + i * 4096, BF16, [D]) for i in range(2)]
    cols = [C.carve(co + 36864 + i * 64, F32, [8]) for i in range(NP)]
    sc_b = [Buf() for _ in range(NP)]
    pp_b = [Buf() for _ in range(NP)]
    pts_b = [Buf() for _ in range(NP)]
    col_b = [Buf() for _ in range(NP)]
    yb_b = [Buf() for _ in range(NP)]
    otm_b = [Buf(), Buf()]
    oT = ActT(C.carve(A_OFF, BF16, [KC, NT]))
    blocks = [(0, "c"), (128, "c")] + [(256 + 128 * j, 0 if j == 0 else (2 if j == 7 else 1)) for j in range(8)]
    units = [(bi, hq) for bi in range(len(blocks)) for hq in range(32)]
    info = {}

    def stage_a(t):
        bi, hq = units[t]
        q0, kind = blocks[bi]
        kvh = hq // 8
        pb = (hq % 2) * 64
        ch = hq // 2
        i = t % NP
        bx = i
        by = 4
        yq = i * 128
        lq = qT[pb:pb + 64, ch, q0:q0 + 128]
        fns = [lambda e: e.matmul(C.psum[:, bx, 0:256], lhsT=lq, rhs=KD[pb:pb + 64, kvh, 0:256], start=True, stop=True)]
        if kind == "c":
            n1, ntot = 256, 256
        else:
            if kind == 0:
                kc0, kn = q0, 128
            else:
                kc0, kn = q0 - 128, 256
            fns.append(lambda e: e.matmul(C.psum[:, bx, 256:256 + kn], lhsT=lq, rhs=KD[pb:pb + 64, kvh, kc0:kc0 + kn], start=True, stop=True))
            n1 = 256 + kn
            ntot = n1 + 128
            nxt = KDh[pb:pb + 64, kvh, :] if kind == 2 else KD[pb:pb + 64, kvh, q0 + 128:q0 + 256]
            fns.append(lambda e: e.matmul(C.psum[:, by, yq:yq + 128], lhsT=lq, rhs=nxt, start=True, stop=True))
        S.group("pe", fns, reads=[inb], writes=[C.bank[bx], yb_b[i]])
        s_ = sc[i]
        if kind == "c":
            S.op("dve", lambda e: e.tensor_copy(out=s_[:, :256], in_=C.psum[:, bx, :256]), reads=[C.bank[bx]], writes=[sc_b[i]])
        else:
            mk = masks[:, kind, :]
            S.op("dve", lambda e: e.tensor_tensor(out=s_[:, :n1], in0=C.psum[:, bx, :n1], in1=mk[:, :n1], op=ALU.add),
                 reads=[C.bank[bx], inb], writes=[sc_b[i]])
            S.op("dve", lambda e: e.tensor_tensor(out=s_[:, n1:ntot], in0=C.psum[:, by, yq:yq + 128], in1=mk[:, n1:ntot], op=ALU.add),
                 reads=[yb_b[i], inb], writes=[sc_b[i]], fresh=False)
        cl = cols[i]
        S.op("dve", lambda e: e.tensor_reduce(out=cl[:, 0:1], in_=s_[:, :ntot], axis=AX.X, op=ALU.max),
             reads=[sc_b[i]], writes=[col_b[i]])
        S.op("dve", lambda e: e.tensor_scalar(out=cl[:, 1:2], in0=cl[:, 0:1], scalar1=sinkT[:, hq:hq + 1], scalar2=-1.0,
                                               op0=ALU.max, op1=ALU.mult),
             reads=[inb], writes=[col_b[i]], fresh=False)
        S.op("act", lambda e: e.activation(out=pp[i][:, :ntot], in_=s_[:, :ntot], func=AF.Exp, bias=cl[:, 1:2], scale=1.0,
                                            accum_out=cl[:, 2:3]),
             reads=[sc_b[i], col_b[i]], writes=[pp_b[i], col_b[i]], fresh=False)
        S.op("act", lambda e: e.activation(out=cl[:, 3:4], in_=sinkT[:, hq:hq + 1], func=AF.Exp, bias=cl[:, 1:2], scale=1.0),
             reads=[inb], writes=[col_b[i]], fresh=False)
        info[t] = (ntot, kind, q0, kvh)

    def stage_b(t):
        ntot = info[t][0]
        i = t % NP
        nblk = ntot // 128
        bz = (5, 6)[t % 2]
        pzb = C.psum[:, bz, :].bitcast(BF16)
        fns = [lambda e, b_=b_: e.transpose(out=pzb[:, b_ * 128:(b_ + 1) * 128], in_=pp[i][:, b_ * 128:(b_ + 1) * 128], identity=C.ident[:])
               for b_ in range(nblk)]
        S.group("pe", fns, reads=[pp_b[i], C.cb], writes=[C.bank[bz]])
        S.op("act", lambda e: e.activation(out=pts[i][:, :ntot], in_=pzb[:, :ntot], func=AF.Copy),
             reads=[C.bank[bz]], writes=[pts_b[i]])

    def stage_c(t):
        ntot, kind, q0, kvh = info.pop(t)
        bi, hq = units[t]
        i = t % NP
        nblk = ntot // 128
        cl = cols[i]
        ob = otm[bi % 2]
        obb = otm_b[bi % 2]
        bw = 7
        hs = hq % 8
        vsl = slice(kvh * 64, (kvh + 1) * 64)
        vts = [VT[:, 0, vsl], VT[:, 1, vsl]]
        if kind != "c":
            t_cur = q0 // 128
            if kind != 0:
                vts.append(VT[:, t_cur - 1, vsl])
            vts.append(VT[:, t_cur, vsl])
            vts.append(VTh[:, vsl] if kind == 2 else VT[:, t_cur + 1, vsl])
        S.op("dve", lambda e: e.tensor_tensor(out=cl[:, 4:5], in0=cl[:, 2:3], in1=cl[:, 3:4], op=ALU.add),
             reads=[], writes=[col_b[i]], fresh=False)
        S.op("dve", lambda e: e.reciprocal(out=cl[:, 5:6], in_=cl[:, 4:5]), reads=[], writes=[col_b[i]], fresh=False)
        fns = [lambda e, b_=b_: e.matmul(C.psum[:, bw, hs * 64:(hs + 1) * 64], lhsT=pts[i][:, b_ * 128:(b_ + 1) * 128], rhs=vts[b_],
                                          start=(b_ == 0), stop=(b_ == nblk - 1)) for b_ in range(nblk)]
        S.group("pe", fns, reads=[pts_b[i], inb], writes=[C.bank[bw]], fresh=(hs == 0))
        S.op("dve", lambda e: e.tensor_scalar(out=ob[:, hq * 64:(hq + 1) * 64], in0=C.psum[:, bw, hs * 64:(hs + 1) * 64],
                                               scalar1=cl[:, 5:6], scalar2=None, op0=ALU.mult),
             reads=[C.bank[bw], col_b[i]], writes=[obb], fresh=(hq == 0))
        if hq == 31:
            for half in range(2):
                bz = (5, 6)[half]
                pzb = C.psum[:, bz, :].bitcast(BF16)
                fns = [lambda e, c_=c_: e.transpose(out=pzb[:, c_ * 128:(c_ + 1) * 128],
                                                    in_=ob[:, (half * 8 + c_) * 128:(half * 8 + c_ + 1) * 128], identity=C.ident[:])
                       for c_ in range(8)]
                S.group("pe", fns, reads=[obb, C.cb], writes=[C.bank[bz]])
                S.op("act", lambda e: e.activation(out=oT.ap[:, half * 8:half * 8 + 8, q0:q0 + 128],
                                                    in_=pzb.rearrange("p (c q) -> p c q", q=128), func=AF.Copy),
                     reads=[C.bank[bz]], writes=oT.ball(range(half * 8, half * 8 + 8), q0, q0 + 128), fresh=False)

    N = len(units)
    for t in range(N + 2):
        if t < N:
            stage_a(t)
        if 1 <= t <= N:
            stage_b(t - 1)
        if t >= 2:
            stage_c(t - 2)
    return oT


GC = 128
NCH = NT // GC
LNQ = float(np.log(1.0 / 16.0))


def gla_proj(C, T, h):
    S = C.S
    gb = T["gl_b"]
    S.barrier()
    for name, wname, fn, off, dsi in (("gl_V", "gl_wv", AF.Copy, BC_OFF, 20), ("gl_OG", "gl_wog", AF.Silu, BC_OFF + 40960, 21)):
        st = C.carve(off, BF16, [NCH, D])
        sb = Buf()

        def epi(nb, ti, t0, tn, bk, st=st, sb=sb, fn=fn):
            S.op("act", lambda e: e.activation(out=st[:, ti, nb * 512:(nb + 1) * 512], in_=C.psum[:, bk, :], func=fn),
                 reads=[C.bank[bk]], writes=[sb], fresh=False)
        linear_tm(C, T[wname], 4, 512, h, TOK_TILES, epi)
        S.dma("sp", C.dsems[dsi], [(T[name].rearrange("(t p) f -> p t f", p=128), st)], reads=[sb], writes=[gb[name]])
    S.barrier()
    aT = C.carve(D_OFF, BF16, [NT])
    wa2 = C.carve(D_OFF + 2560, BF16, [1024])
    nba = C.carve(D_OFF + 4608, F32, [2, 8])
    dect = C.carve(D_OFF + 4736, F32, [2, 80])
    rmask = C.carve(D_OFF + 5632, F32, [NT])
    ab = Buf()
    S.dma("pool", C.dsems[22], [(wa2, T["gl_wa2"])], writes=[ab])
    S.dma("sp", C.dsems[23], [(nba, T["gl_ba"])], writes=[ab], fresh=False)
    S.op("act", lambda e: e.activation(out=nba, in_=nba, func=AF.Identity, scale=-1.0), reads=[ab], writes=[ab], fresh=False)
    S.op("dve", lambda e: e.memset(rmask, 1.0), writes=[ab], fresh=False)
    S.op("dve", lambda e: e.memset(rmask.rearrange("p (c t) -> p c t", t=GC)[:, :, 0:1], 0.0), writes=[ab], fresh=False)

    def epi_a(mc, t0, tn, bks):
        S.op("act", lambda e: e.activation(out=aT[:, t0:t0 + tn], in_=C.psum[:, bks[0], :tn], func=AF.Copy),
             reads=[C.bank[bks[0]]], writes=[ab], fresh=False)
    linear_fm(C, [T["gl_wa"]], 1, h, TILES_ALL, epi_a, per_load=1, bank_sets=((0, 1),))
    tt = [C.carve(BC_OFF + i * 5120, F32, [NT]) for i in range(3)]
    tt_b = [Buf() for _ in range(3)]
    FF_ = [C.carve(BC_OFF + 15360 + i * 5120, F32, [NT]) for i in range(6)]
    F_b = [Buf() for _ in range(6)]
    qk = [[C.carve(BC_OFF + 46080 + par * 10240 + i * 2560, BF16, [NT]) for i in range(4)] for par in range(2)]
    qk_b = [[Buf() for _ in range(4)] for par in range(2)]
    khT2 = [[C.carve(BC_OFF + 66560 + (par * 2 + i) * 2560, BF16, [NT]) for i in range(2)] for par in range(2)]
    khT2_b = [[Buf(), Buf()], [Buf(), Buf()]]
    khs = [C.carve(BC_OFF + 76800 + i * 2560, BF16, [NCH, 128]) for i in range(2)]
    khs_b = [Buf(), Buf()]

    def kh_transposes(c8):
        khT = khT2[c8 % 2]
        khT_b = khT2_b[c8 % 2]
        for d in range(2):
            for (tb0, ntb) in ((0, 8), (8, 2)):
                bz = C.next_bank((0, 1, 2, 3))
                pzb = C.psum[:, bz, :].bitcast(BF16)
                fns = [lambda e, j=j: e.transpose(out=pzb[:, j * 128:(j + 1) * 128], in_=khT[d][:, (tb0 + j) * 128:(tb0 + j + 1) * 128],
                                                  identity=C.ident[:]) for j in range(ntb)]
                S.group("pe", fns, reads=[khT_b[d], C.cb], writes=[C.bank[bz]])
                S.op("act", lambda e: e.activation(out=khs[d][:, tb0:tb0 + ntb, :], in_=pzb[:, :ntb * 128].rearrange("p (t f) -> p t f", f=128),
                                                    func=AF.Copy),
                     reads=[C.bank[bz]], writes=[khs_b[d]], fresh=(tb0 == 0))
            S.dma("sp", C.dsems[32 + d], [(T["gl_KH"][d].rearrange("(t p) f -> p t f", p=128)[:, :, c8 * 128:(c8 + 1) * 128], khs[d])],
                  reads=[khs_b[d]], writes=[gb["gl_KH"]], fresh=False)
    for c8 in range(8):
        par = c8 % 2
        for d in range(2):
            rb = 32 * d
            t0_, t1_, t2_ = tt
            for (c0, cn) in TILES_ALL:
                bk = C.next_bank((0, 1, 2, 3))
                C.mm(C.psum[:, bk, :cn], [(wa2[rb:rb + 16, c8 * 128:(c8 + 1) * 128], aT[rb:rb + 16, c0:c0 + cn])], reads=[ab], bank_b=C.bank[bk])
                S.op("act", lambda e: e.activation(out=t0_[:, c0:c0 + cn], in_=C.psum[:, bk, :cn], func=AF.Exp, scale=-1.0,
                                                    bias=nba[:, d, c8:c8 + 1]),
                     reads=[C.bank[bk], ab], writes=[tt_b[0]], fresh=(c0 == 0))
            S.op("act", lambda e: e.activation(out=t0_, in_=t0_, func=AF.Ln, bias=C.cst[:, 0:1], scale=1.0),
                 reads=[C.cb], writes=[tt_b[0]], fresh=False)
            S.op("dve", lambda e: e.tensor_tensor_scan(out=t1_, data0=rmask, data1=t0_, initial=0.0, op0=ALU.mult, op1=ALU.add),
                 reads=[tt_b[0], ab], writes=[tt_b[1]])
            cs3 = t1_.rearrange("p (c t) -> p c t", t=GC)
            S.op("act", lambda e: e.activation(out=dect[:, d, c8 * NCH:(c8 + 1) * NCH], in_=cs3[:, :, GC - 1], func=AF.Exp, scale=-1.0 / 16),
                 reads=[tt_b[1]], writes=[ab], fresh=False)
            fq, fk, fkh = FF_[3 * d:3 * d + 3]
            fqb, fkb, fkhb = F_b[3 * d:3 * d + 3]
            if d == 0:
                for ch in range(NCH):
                    S.op("dve", lambda e: e.tensor_scalar(out=t2_[:, ch * GC:(ch + 1) * GC], in0=t1_[:, ch * GC:(ch + 1) * GC],
                                                           scalar1=t1_[:, ch * GC + GC - 1:ch * GC + GC], scalar2=None, op0=ALU.subtract),
                         reads=[tt_b[1]], writes=[tt_b[2]], fresh=(ch == 0))
                S.op("act", lambda e: e.activation(out=fq, in_=t1_, func=AF.Exp, scale=-1.0 / 16, bias=C.cst[:, 1:2]), reads=[tt_b[1], C.cb], writes=[fqb])
                S.op("act", lambda e: e.activation(out=fk, in_=t1_, func=AF.Exp, scale=1.0 / 16), reads=[tt_b[1]], writes=[fkb])
                S.op("act", lambda e: e.activation(out=fkh, in_=t2_, func=AF.Exp, scale=1.0 / 16), reads=[tt_b[2]], writes=[fkhb])
            else:
                S.op("dve", lambda e: e.tensor_tensor(out=t2_, in0=t1_, in1=t0_, op=ALU.subtract), reads=[tt_b[0], tt_b[1]], writes=[tt_b[2]])
                for ch in range(NCH):
                    S.op("dve", lambda e: e.tensor_scalar(out=t0_[:, ch * GC:(ch + 1) * GC], in0=t2_[:, ch * GC:(ch + 1) * GC],
                                                           scalar1=-1.0, scalar2=t1_[:, ch * GC + GC - 1:ch * GC + GC], op0=ALU.mult, op1=ALU.add),
                         reads=[tt_b[1], tt_b[2]], writes=[tt_b[0]], fresh=(ch == 0))
                S.op("act", lambda e: e.activation(out=fq, in_=t0_, func=AF.Exp, scale=-1.0 / 16, bias=C.cst[:, 1:2]), reads=[tt_b[0], C.cb], writes=[fqb])
                S.op("act", lambda e: e.activation(out=fk, in_=t0_, func=AF.Exp, scale=1.0 / 16), reads=[tt_b[0]], writes=[fkb])
                S.op("act", lambda e: e.activation(out=fkh, in_=t2_, func=AF.Exp, scale=-1.0 / 16), reads=[tt_b[2]], writes=[fkhb])
        q1, q2, k1, k2 = qk[par]
        q1b, q2b, k1b, k2b = qk_b[par]
        khT = khT2[par]
        khT_b = khT2_b[par]

        def epi_qk(mc, t0, tn, bks):
            bq, bkk = bks
            for (dst, db, bank, F, Fb, first) in ((q1, q1b, bq, FF_[0], F_b[0], True), (q2, q2b, bq, FF_[3], F_b[3], True),
                                                  (k1, k1b, bkk, FF_[1], F_b[1], True), (k2, k2b, bkk, FF_[4], F_b[4], True),
                                                  (khT[0], khT_b[0], bkk, FF_[2], F_b[2], True), (khT[1], khT_b[1], bkk, FF_[5], F_b[5], True)):
                S.op("dve", lambda e: e.tensor_tensor(out=dst[:, t0:t0 + tn], in0=C.psum[:, bank, :tn], in1=F[:, t0:t0 + tn], op=ALU.mult),
                     reads=[C.bank[bank], Fb], writes=[db], fresh=(t0 == 0))
        linear_fm(C, [T["gl_wq"][c8:c8 + 1], T["gl_wk"][c8:c8 + 1]], 1, h, TILES_ALL, epi_qk, per_load=1, bank_sets=((4, 5), (6, 7)))
        for d in range(2):
            S.dma("sp", C.dsems[24 + par * 4 + d], [(T["gl_QT"][d].rearrange("p (k t) -> p k t", k=8)[:, c8, :], (q1, q2)[d])],
                  reads=[(q1b, q2b)[d]], writes=[gb["gl_QT"]], fresh=False)
            S.dma("sp", C.dsems[26 + par * 4 + d], [(T["gl_KT"][d].rearrange("p (k t) -> p k t", k=8)[:, c8, :], (k1, k2)[d])],
                  reads=[(k1b, k2b)[d]], writes=[gb["gl_KT"]], fresh=False)
        if c8 >= 1:
            kh_transposes(c8 - 1)
    kh_transposes(7)
    S.dma("sp", C.dsems[34], [(T["gl_DEC"].rearrange("d p f -> p d f"), dect)], reads=[ab], writes=[gb["gl_DEC"]])


def gla_scan(C, T, d, last=False):
    S = C.S
    gb = T["gl_b"]
    S.barrier()
    St = C.carve(BC_OFF, F32, [8, 512])
    Sb = C.carve(BC_OFF + 16384, BF16, [8, 512])
    S_b = [Buf() for _ in range(8)]
    Sb_b = [Buf() for _ in range(8)]
    o_t = [C.carve(BC_OFF + 24576 + i * 8192, F32, [D]) for i in range(2)]
    o_b = [Buf(), Buf()]
    qt = [C.carve(BC_OFF + 40960 + i * 2048, BF16, [8, GC]) for i in range(2)]
    kt = [C.carve(BC_OFF + 45056 + i * 2048, BF16, [8, GC]) for i in range(2)]
    kh = [C.carve(BC_OFF + 49152 + i * 2048, BF16, [1024]) for i in range(2)]
    vv = [C.carve(BC_OFF + 53248 + i * 4096, BF16, [D]) for i in range(2)]
    in_b = [Buf(), Buf()]
    att = C.carve(BC_OFF + 61440, BF16, [4, GC])
    att_b = Buf()
    mask = C.carve(BC_OFF + 62464, F32, [2, GC])
    dec = C.carve(BC_OFF + 63488, F32, [2, 80])
    cb2 = Buf()
    S.dma("sp", C.dsems[16], [(mask, T["gl_mask"]), (dec, T["gl_DEC"].rearrange("d p f -> p d f"))], reads=[gb["gl_DEC"]], writes=[cb2])
    if d == 1:
        o1 = [C.carve(BC_OFF + 64512 + i * 8192, F32, [D]) for i in range(2)]
        og = [C.carve(D_OFF + i * 4096, BF16, [D]) for i in range(2)]
        gB = C.carve(D_OFF + 8192, F32, [D])
        S.dma("sp", C.dsems[17], [(gB, T["gl_ong"].partition_broadcast(128))], writes=[cb2], fresh=False)
        fin_b = [Buf(), Buf()]
        on = C.carve(D_OFF + 16384 - 0, BF16, [8]) if False else None
        oT = ActT(C.carve(A_OFF, BF16, [KC, NT]))
    order = list(range(NCH)) if d == 0 else [1, 0] + list(range(NCH - 1, 1, -1))
    QTv = T["gl_QT"][d].rearrange("p (k t) -> p k t", k=8)
    KTv = T["gl_KT"][d].rearrange("p (k t) -> p k t", k=8)
    KHv = T["gl_KH"][d].rearrange("(t p) f -> p t f", p=128)
    Vv = T["gl_V"].rearrange("(t p) f -> p t f", p=128)

    inB_b = [Buf(), Buf()]

    def load(si):
        c = order[si]
        i = si % 2
        prs = [(qt[i], QTv[:, :, c * GC:(c + 1) * GC]), (kt[i], KTv[:, :, c * GC:(c + 1) * GC]), (kh[i], KHv[:, c, :]), (vv[i], Vv[:, c, :])]
        rd = [gb["gl_QT"], gb["gl_KT"], gb["gl_KH"], gb["gl_V"]]
        S.dma("sp", C.dsems[18 + i], prs, reads=rd, writes=[in_b[i]])

    def load_b(si):
        c = order[si]
        i = si % 2
        prs = [(o1[i], T["gl_O1"].rearrange("(t p) f -> p t f", p=128)[:, c, :]), (og[i], T["gl_OG"].rearrange("(t p) f -> p t f", p=128)[:, c, :])]
        S.dma("sp", C.dsems[20 + i], prs, reads=[gb["gl_O1"], gb["gl_OG"]], writes=[inB_b[i]])

    def fin2(si):
        c = order[si]
        i = si % 2
        ot = o_t[i]
        sq = o1[i]
        cl = C.carve(D_OFF + 16384 + i * 64, F32, [8])
        for hd in range(4):
            S.op("act", lambda e: e.activation(out=sq[:, hd * 512:(hd + 1) * 512], in_=ot[:, hd * 512:(hd + 1) * 512], func=AF.Square,
                                                accum_out=cl[:, hd:hd + 1]),
                 reads=[o_b[i]], writes=[fin_b[i], inB_b[i]], fresh=False)
        S.op("act", lambda e: e.activation(out=cl[:, 4:8], in_=cl[:, 0:4], func=AF.Sqrt, bias=C.epst[:], scale=1.0 / 512),
             reads=[C.cb], writes=[fin_b[i]], fresh=False)
        S.op("dve", lambda e: e.reciprocal(out=cl[:, 4:8], in_=cl[:, 4:8]), reads=[], writes=[fin_b[i]], fresh=False)
        for hd in range(4):
            S.op("dve", lambda e: e.scalar_tensor_tensor(out=sq[:, hd * 512:(hd + 1) * 512], in0=ot[:, hd * 512:(hd + 1) * 512],
                                                          scalar=cl[:, 4 + hd:5 + hd], in1=gB[:, hd * 512:(hd + 1) * 512],
                                                          op0=ALU.mult, op1=ALU.mult),
                 reads=[o_b[i], cb2], writes=[fin_b[i], inB_b[i]], fresh=False)
        onb = og[i]
        S.op("pool", lambda e: e.tensor_tensor(out=onb, in0=sq, in1=og[i], op=ALU.mult), reads=[fin_b[i]], writes=[inB_b[i]], fresh=False)
        for half in range(2):
            bz = C.next_bank(ubanks)
            pzb = C.psum[:, bz, :].bitcast(BF16)
            fns = [lambda e, c_=c_: e.transpose(out=pzb[:, c_ * 128:(c_ + 1) * 128],
                                                in_=onb[:, (half * 8 + c_) * 128:(half * 8 + c_ + 1) * 128], identity=C.ident[:])
                   for c_ in range(8)]
            S.group("pe", fns, reads=[inB_b[i], C.cb], writes=[C.bank[bz]])
            S.op("act", lambda e: e.activation(out=oT.ap[:, half * 8:half * 8 + 8, c * GC:(c + 1) * GC],
                                                in_=pzb.rearrange("p (c q) -> p c q", q=128), func=AF.Copy),
                 reads=[C.bank[bz]], writes=oT.ball(range(half * 8, half * 8 + 8), c * GC, (c + 1) * GC), fresh=False)

    ubanks = (5, 6, 7)
    load(0)
    if d == 1:
        load_b(0)
    have_state = False
    pend = None
    for si, c in enumerate(order):
        i = si % 2
        if si + 1 < len(order):
            load(si + 1)
        if d == 1 and si == 2:
            if pend is not None:
                fin2(pend)
                pend = None
            if "gl_S1g" in T:
                G = T["gl_S1g"]
                tmpB = C.carve(BC_OFF + 24576, F32, [8, 512])
                S.dma("sp", C.dsems[22], [(St, G[0:128, :].rearrange("p (k f) -> p k f", k=8)),
                                           (tmpB, G[128:256, :].rearrange("p (k f) -> p k f", k=8))],
                      reads=[gb["gl_S1g"]], writes=S_b + o_b)
                for c8 in range(8):
                    S.op("dve", lambda e: e.tensor_scalar(out=St[:, c8, :], in0=St[:, c8, :], scalar1=C.sel[:, 0:1], scalar2=None, op0=ALU.mult),
                         reads=[C.cb], writes=[S_b[c8]])
                    S.op("dve", lambda e: e.scalar_tensor_tensor(out=St[:, c8, :], in0=tmpB[:, c8, :], scalar=C.sel[:, 1:2], in1=St[:, c8, :],
                                                                  op0=ALU.mult, op1=ALU.add),
                         reads=o_b + [C.cb], writes=[S_b[c8]])
            else:
                S.dma("sp", C.dsems[22], [(St, T["gl_S1in"].rearrange("p (k f) -> p k f", k=8))], writes=S_b)
            for c8 in range(8):
                S.op("act", lambda e: e.activation(out=Sb[:, c8, :], in_=St[:, c8, :], func=AF.Copy), reads=[S_b[c8]], writes=[Sb_b[c8]])
            have_state = True
        fns = []
        for hd in range(4):
            for kc in range(2):
                fns.append(lambda e, hd=hd, kc=kc: e.matmul(C.psum[:, 4, hd * GC:(hd + 1) * GC], lhsT=kt[i][:, 2 * hd + kc, :],
                                                            rhs=qt[i][:, 2 * hd + kc, :], start=(kc == 0), stop=(kc == 1)))
        S.group("pe", fns, reads=[in_b[i]], writes=[C.bank[4]])
        if have_state:
            for hd in range(4):
                fns = [lambda e, kc=kc: e.matmul(C.psum[:, hd, :], lhsT=qt[i][:, 2 * hd + kc, :], rhs=Sb[:, 2 * hd + kc, :],
                                                 start=(kc == 0), stop=False) for kc in range(2)]
                S.group("pe", fns, reads=[in_b[i], Sb_b[2 * hd], Sb_b[2 * hd + 1]], writes=[C.bank[hd]])
        for hd in range(4):
            S.op("dve", lambda e: e.tensor_tensor(out=att[:, hd, :], in0=C.psum[:, 4, hd * GC:(hd + 1) * GC], in1=mask[:, d, :], op=ALU.mult),
                 reads=[C.bank[4], cb2], writes=[att_b], fresh=(hd == 0))
        had_state = have_state
        need_update = not (d == 1 and si == 1)
        if need_update:
            for c8 in range(8):
                hd = c8 // 2
                bu = C.next_bank(ubanks)
                S.group("pe", [lambda e: e.matmul(C.psum[:, bu, :], lhsT=kh[i][:, c8 * 128:(c8 + 1) * 128], rhs=vv[i][:, hd * 512:(hd + 1) * 512],
                                                  start=True, stop=True)], reads=[in_b[i]], writes=[C.bank[bu]])
                if have_state:
                    S.op("dve", lambda e: e.scalar_tensor_tensor(out=St[:, c8, :], in0=St[:, c8, :], scalar=dec[:, d, c8 * NCH + c:c8 * NCH + c + 1],
                                                                  in1=C.psum[:, bu, :], op0=ALU.mult, op1=ALU.add),
                         reads=[C.bank[bu], cb2], writes=[S_b[c8]])
                else:
                    S.op("dve", lambda e: e.tensor_copy(out=St[:, c8, :], in_=C.psum[:, bu, :]), reads=[C.bank[bu]], writes=[S_b[c8]])
                S.op("act", lambda e: e.activation(out=Sb[:, c8, :], in_=St[:, c8, :], func=AF.Copy), reads=[S_b[c8]], writes=[Sb_b[c8]])
            have_state = True
        for hd in range(4):
            S.group("pe", [lambda e: e.matmul(C.psum[:, hd, :], lhsT=att[:, hd, :], rhs=vv[i][:, hd * 512:(hd + 1) * 512],
                                              start=(not had_state), stop=True)],
                    reads=[att_b, in_b[i]], writes=[C.bank[hd]], fresh=(not had_state))
        ot = o_t[i]
        skip_fin = (d == 1 and last and c < 2)
        if d == 0:
            for hd in range(4):
                S.op("act", lambda e: e.activation(out=ot[:, hd * 512:(hd + 1) * 512], in_=C.psum[:, hd, :], func=AF.Copy),
                     reads=[C.bank[hd]], writes=[o_b[i]], fresh=(hd == 0))
            S.dma("sp", C.dsems[20 + i], [(T["gl_O1"].rearrange("(t p) f -> p t f", p=128)[:, c, :], ot)], reads=[o_b[i]],
                  writes=[gb["gl_O1"]], fresh=False)
        else:
            if not skip_fin:
                for hd in range(4):
                    S.op("dve", lambda e: e.tensor_tensor(out=ot[:, hd * 512:(hd + 1) * 512], in0=C.psum[:, hd, :],
                                                           in1=o1[i][:, hd * 512:(hd + 1) * 512], op=ALU.add),
                         reads=[C.bank[hd], inB_b[i]], writes=[o_b[i]], fresh=(hd == 0))
            if pend is not None:
                fin2(pend)
            pend = None if skip_fin else si
            if si + 1 < len(order):
                load_b(si + 1)
    if d == 1 and pend is not None:
        fin2(pend)
    if d == 0:
        S.dma("sp", C.dsems[22], [(T["gl_S1"].rearrange("p (k f) -> p k f", k=8), St)], reads=S_b, writes=[gb["gl_S1"]])
        if "gl_S1g" in T:
            pair_allgather(C, T["gl_S1"], T["gl_S1g"], gb["gl_S1"], gb["gl_S1g"])
        return None
    return oT


def fm_weight(W):
    K, M = W.shape
    return np.ascontiguousarray(W.reshape(K // 128, 128, M // 128, 128).transpose(2, 1, 0, 3))


def tm_weight(W, nb=512):
    K, M = W.shape
    return np.ascontiguousarray(W.reshape(K // 128, 128, M // nb, nb).transpose(2, 1, 0, 3))


def fm_vec(v):
    v = np.asarray(v)
    n = v.shape[-1] // 128
    r = v.reshape(v.shape[:-1] + (n, 128))
    return np.ascontiguousarray(np.moveaxis(r, -1, 0))


def dram(nc, name, shape, dtype, kind):
    if kind is None:
        return nc.dram_tensor(name, [int(x) for x in shape], dtype).ap()
    return nc.dram_tensor(name, [int(x) for x in shape], dtype, kind=kind).ap()


def init_consts(C, T=None):
    S = C.S
    S.op("dve", lambda e: e.memset(C.ones[:], 1.0), writes=[C.cb], fresh=False)
    S.op("dve", lambda e: e.memset(C.epst[:], EPS), writes=[C.cb], fresh=False)
    S.op("dve", lambda e: e.memset(C.cst[:, 0:1], 1.0), writes=[C.cb], fresh=False)
    S.op("dve", lambda e: e.memset(C.cst[:, 1:2], LNQ), writes=[C.cb], fresh=False)
    if T is not None and "ident" in T:
        S.dma("sp", C.dsems[42], [(C.ident[:], T["ident"])], writes=[C.cb], fresh=False)
    if T is not None and "sel" in T:
        S.dma("sp", C.dsems[43], [(C.sel[:], T["sel"])], writes=[C.cb], fresh=False)


def build_launch(steps, tensors):
    nc = bass.Bass("TRN2", target_bir_lowering=False)
    T = {}
    for name, (shape, dt_, kind) in tensors.items():
        T[name] = dram(nc, name, shape, dt_, kind)
    T["XD_b"] = [Buf() for _ in range(KC)]
    T["sw_b"] = Buf()
    T["gl_b"] = Grid()
    with ExitStack() as es:
        C = Ctx(nc, es)
        C.v = {}
        init_consts(C, T)
        if "XD" not in T:
            T["XD"] = T["XDin"]
        elif "XDin" in T:
            C.S.dma("sp", C.dsems[41], [(T["XD"], T["XDin"])], writes=T["XD_b"])
        for st in steps:
            st(C, T)
        C.S.final_wait()
    return nc


def st_ada(l):
    def f(C, T):
        ada_now(C, l, T)
    f.tag = "ada"
    return f


def st_ada0_a():
    def f(C, T):
        ada_inputs(C, 0, T)
        for m0 in range(0, 32, 4):
            ada_quad(C, 0, T, m0, 4)
        ada_finish(C, 0, 4, [0, 1])
        C.cur_l = 0
    f.tag = "ada0a"
    return f


def st_ada0_b():
    def f(C, T):
        for m0 in range(32, 96, 4):
            ada_quad(C, 0, T, m0, 4)
        ada_finish(C, 0, 4, [2, 3, 4, 5])
    f.tag = "ada0b"
    return f


def st_prenorm(sub, last=False):
    def f(C, T):
        C.v["h"] = prenorm(C, T, sub, last)
    f.tag = "pre"
    return f


def st_post(sub, last=False):
    def f(C, T):
        postnorm_resid(C, T, C.v["y"], sub, last)
    f.tag = "post"
    return f


def st_postpre(sub_post, l_post, sub_pre, l_pre, last=False):
    def f(C, T):
        C.v["h"] = post_pre(C, T, C.v["y"], sub_post, l_post, sub_pre, l_pre, last)
        C.cur_l = l_pre
    f.tag = "postpre"
    return f


def st_ffn(l, last=False, next_ada=None):
    def f(C, T):
        hook = None
        if next_ada is not None:
            gen = ada_gen(C, next_ada, T)

            def hook(g):
                for _ in range(2 if g < 2 else 1):
                    next(gen, None)
        C.v["y"] = ffn(C, T, l, C.v["h"], last, hook)
        if next_ada is not None:
            for _ in gen:
                pass
    f.tag = "ffn"
    return f


def st_setl(l):
    def f(C, T):
        C.cur_l = l
    f.tag = "setl"
    return f


def st_gmlp():
    def f(C, T):
        C.v["y"] = gmlp_mixer(C, T, C.v["h"])
    f.tag = "gmlp"
    return f


def st_swa_proj():
    def f(C, T):
        swa_proj(C, T, C.v["h"])
    f.tag = "swap"
    return f


def st_swa_attn():
    def f(C, T):
        oT = swa_attn(C, T)
        C.S.barrier()
        C.v["y"] = out_proj(C, T["sw_wo"], oT, TILES_ALL)
    f.tag = "swaa"
    return f


def st_gla_a():
    def f(C, T):
        gla_proj(C, T, C.v["h"])
        gla_scan(C, T, 0)
    f.tag = "glaa"
    return f


def st_gla_b(last=False):
    def f(C, T):
        oT = gla_scan(C, T, 1, last)
        C.S.barrier()
        C.v["y"] = out_proj(C, T["gl_wo"], oT, TILES_LAT if last else TILES_ALL)
    f.tag = "glab"
    return f


def spec_gla_w():
    return {"gl_wa": ([1, 128, KC, 128], F32), "gl_wa2": ([128, 1024], F32), "gl_ba": ([128, 2, 8], F32),
            "gl_wq": ([8, 128, KC, 128], F32), "gl_wk": ([8, 128, KC, 128], F32),
            "gl_wv": ([4, 128, KC, 512], F32), "gl_wog": ([4, 128, KC, 512], F32), "gl_mask": ([128, 2, 128], F32)}


def spec_gla_state():
    return {"gl_QT": ([2, 128, 8 * NT], BF16), "gl_KT": ([2, 128, 8 * NT], BF16), "gl_KH": ([2, NT, 1024], BF16),
            "gl_DEC": ([2, 128, 80], F32), "gl_V": ([NT, D], BF16), "gl_OG": ([NT, D], BF16), "gl_O1": ([NT, D], F32)}


def spec_gla_b():
    return {"gl_S1in": ([128, 8 * 512], F32), "gl_mask": ([128, 2, 128], F32), "gl_ong": ([D], F32), "gl_wo": ([KC, 128, KC, 128], F32)}


def spec_common():
    return {"c2": ([128, KC, 2], F32), "ident": ([128, 128], BF16)}


def spec_ada(l):
    return {f"ada_w{l}": ([96, 128, KC, 128], F32), f"ada_b{l}": ([128, 96], F32), f"norm_g{l}": ([128, 4, KC], F32)}


def spec_ffn(l):
    return {f"w1_{l}": ([NFC, 128, KC, 128], F32), f"w3_{l}": ([NFC, 128, KC, 128], F32), f"w2_{l}": ([FF, D], F32)}


def spec_layer(l):
    return {**spec_ada(l), **spec_ffn(l)}


def spec_gmlp():
    return {"gm_wv": ([4, 128, KC, 512], F32), "gm_wu": ([KC, 128, KC, 128], F32), "gm_ln_g": ([D], F32), "gm_ln_b": ([D], F32),
            "gm_wsT": ([128, 2048], F32), "gm_bs": ([16, 128], F32), "gm_wo": ([KC, 128, KC, 128], F32)}


def spec_swa_w():
    return {"sw_wq": ([KC, 128, KC, 128], F32), "sw_wqp": ([KC, 128, KC, 128], F32), "sw_wk": ([4, 128, KC, 128], F32),
            "sw_wkp": ([4, 128, KC, 128], F32), "sw_wv": ([1, 128, KC, 256], F32), "rope_tab": ([4, 128, NT], F32)}


def spec_swa_state():
    return {"sw_qT": ([128, KC * NT], BF16), "sw_KD": ([128, 4 * NT], BF16), "sw_VT": ([128, NT // 128 * 256], BF16)}


def spec_swa_attn():
    return {"sw_KDh": ([128, 4 * 128], BF16), "sw_VTh": ([128, 256], BF16), "sw_masks": ([128, 3, 640], F32), "sw_sink": ([32], F32),
            "sw_wo": ([KC, 128, KC, 128], F32)}


ROPE_SRC = np.array([(d + 16) if (d % 32) < 16 else (d - 16) for d in range(64)])
ROPE_SIGN = np.array([-1.0 if (d % 32) < 16 else 1.0 for d in range(64)], np.float32)


class Host:
    def __init__(self, inputs, core):
        self.inp = inputs
        self.core = core
        self.b = core // 2
        self.side = core % 2
        self.gslot = 0
        self.cache = {}

    def lat_global(self):
        i = np.arange(NLAT)
        return i if self.side == 0 else (2 * NLAT - 1 - i)

    def get(self, name):
        key = (name, self.gslot) if name.startswith("gl_") else name
        if key not in self.cache:
            self.cache[key] = self._make(name)
        return self.cache[key]

    def _make(self, name):
        z = self.inp
        b, side = self.b, self.side
        if name == "XDin":
            ctx = z["ctx"][b]
            x = z["x"][b][self.lat_global()]
            if side == 1:
                ctx = ctx[::-1]
            return np.ascontiguousarray(np.concatenate([ctx, x], 0).T)
        if name == "c2":
            return np.ascontiguousarray(fm_vec(np.stack([z["c"][b], z["c_ctx"]], 0)).transpose(0, 2, 1))
        if name == "ident":
            return np.eye(128, dtype=np.float32).astype(ml_dtypes.bfloat16)
        if name.startswith("ada_w"):
            return fm_weight(z["ada_w"][int(name[5:])])
        if name.startswith("ada_b"):
            return fm_vec(z["ada_b"][int(name[5:])])
        if name.startswith("norm_g"):
            return fm_vec(z["norm_g"][int(name[6:])])
        if name.startswith("w1_"):
            return fm_weight(z["ffn_w1"][int(name[3:])])
        if name.startswith("w3_"):
            return fm_weight(z["ffn_w3"][int(name[3:])])
        if name.startswith("w2_"):
            return np.ascontiguousarray(z["ffn_w2"][int(name[3:])])
        if name == "gm_wv":
            return tm_weight(z["gmlp_w_in"][0][:, D:])
        if name == "gm_wu":
            return fm_weight(z["gmlp_w_in"][0][:, :D])
        if name == "gm_ln_g":
            return np.ascontiguousarray(z["gmlp_ln_g"][0])
        if name == "gm_ln_b":
            return np.ascontiguousarray(z["gmlp_ln_b"][0])
        if name == "gm_wsT":
            ws = z["gmlp_ws"][0]
            if side == 1:
                ws = ws[:, ::-1, ::-1]
            return np.ascontiguousarray(ws.transpose(2, 0, 1).reshape(128, 2048))
        if name == "gm_bs":
            bs = z["gmlp_bs"][0]
            if side == 1:
                bs = bs[:, ::-1]
            return np.ascontiguousarray(bs)
        if name == "gm_wo":
            return fm_weight(z["gmlp_wo"][0])
        if name in ("sw_wq", "sw_wqp", "sw_wk", "sw_wkp", "sw_wv"):
            W = z["attn_w_in"][0]
            nq = 2048
            if name == "sw_wq":
                return fm_weight(W[:, :nq])
            if name == "sw_wqp":
                idx = (np.arange(nq) // 64) * 64 + ROPE_SRC[np.arange(nq) % 64]
                return fm_weight(W[:, idx])
            if name in ("sw_wk", "sw_wkp"):
                src = ROPE_SRC if name == "sw_wkp" else np.arange(64)
                idx = np.concatenate([nq + kvh * 64 + np.concatenate([src, src]) for kvh in range(4)])
                return fm_weight(W[:, idx])
            return tm_weight(W[:, nq + 256:], 256)
        if name == "rope_tab":
            t = self.lat_global().astype(np.float32)
            inv = (10000.0 ** (-np.arange(16, dtype=np.float32) / 16)).astype(np.float32)
            row = np.floor(t / 64)
            col = t - row * 64
            ang = np.concatenate([row[:, None] * inv, row[:, None] * inv, col[:, None] * inv, col[:, None] * inv], -1)
            cos = np.cos(ang).T.astype(np.float32)
            sin = (np.sin(ang) * ROPE_SIGN[None]).T.astype(np.float32)
            tab = np.zeros((4, 128, NT), np.float32)
            tab[0, :, :NCTX] = 0.125
            tab[2, :, :NCTX] = 1.0
            tab[0, :, NCTX:] = 0.125 * np.tile(cos, (2, 1))
            tab[1, :, NCTX:] = 0.125 * np.tile(sin, (2, 1))
            tab[2, :, NCTX:] = np.tile(cos, (2, 1))
            tab[3, :, NCTX:] = np.tile(sin, (2, 1))
            return tab
        if name == "sw_masks":
            qi = np.arange(128)[:, None]
            ki = np.arange(128)[None, :]
            ge = np.where(ki >= qi, 0.0, NEG).astype(np.float32)
            le = np.where(ki <= qi, 0.0, NEG).astype(np.float32)
            halo = np.where(ki + qi >= 127, 0.0, NEG).astype(np.float32)
            m = np.zeros((128, 3, 640), np.float32)
            m[:, 0, 384:512] = le
            m[:, 1, 256:384] = ge
            m[:, 1, 512:640] = le
            m[:, 2, 256:384] = ge
            m[:, 2, 512:640] = halo
            return m
        if name == "sw_sink":
            return np.ascontiguousarray(z["attn_sink"][0])
        if name == "sw_wo":
            return fm_weight(z["attn_wo"][0])
        if name.startswith("gl_"):
            gs = self.gslot
            W = z["gla_w_in"][gs]
            i1, i2 = (0, 1) if side == 0 else (1, 0)
            if name == "gl_wa":
                Wa = np.zeros((D, 128), np.float32)
                Wa[:, 0:16] = W[:, 6144 + 16 * i1:6144 + 16 * i1 + 16]
                Wa[:, 32:48] = W[:, 6144 + 16 * i2:6144 + 16 * i2 + 16]
                return fm_weight(Wa)
            if name == "gl_wa2":
                w = np.zeros((128, 1024), np.float32)
                w[0:16] = z["gla_wa2"][gs, i1]
                w[32:48] = z["gla_wa2"][gs, i2]
                return w
            if name == "gl_ba":
                ba = z["gla_ba"][gs][[i1, i2]]
                return np.ascontiguousarray(ba.reshape(2, 8, 128).transpose(2, 0, 1))
            if name == "gl_wq":
                return fm_weight(W[:, :1024])
            if name == "gl_wk":
                return fm_weight(W[:, 1024:2048])
            if name == "gl_wv":
                return tm_weight(W[:, 2048:4096])
            if name == "gl_wog":
                return tm_weight(W[:, 4096:6144])
            if name == "gl_mask":
                j = np.arange(128)[:, None]
                i = np.arange(128)[None, :]
                m = np.zeros((128, 2, 128), np.float32)
                m[:, 0, :] = (j <= i)
                m[:, 1, :] = (j >= i)
                return m
            if name == "gl_ong":
                return np.ascontiguousarray(z["gla_onorm_g"][gs])
            if name == "gl_wo":
                return fm_weight(z["gla_wo"][gs])
        raise KeyError(name)


SIDE_DEP = ("XDin", "c2", "rope_tab", "gm_wsT", "gm_bs", "gl_wa", "gl_wa2", "gl_ba")
_PROGS = {}


def _mk(specs_in, specs_out):
    tens = {}
    for sp in specs_in:
        for k, (sh, dt_) in sp.items():
            tens[k] = (sh, dt_, "ExternalInput")
    for sp in specs_out:
        for k, (sh, dt_) in sp.items():
            tens[k] = (sh, dt_, "ExternalOutput")
    return tens


def _launch_defs():
    XI = {"XDin": ([D, NT], F32)}
    XO = {"XD": ([D, NT], F32)}
    S1 = {"gl_S1": ([128, 8 * 512], F32)}
    L = []
    L.append((_mk([spec_common(), spec_ada(0), spec_gla_w(), XI], [spec_gla_state(), S1]),
              [st_ada(0), st_setl(0), st_prenorm(0), st_gla_a()], 0))
    L.append((_mk([spec_common(), spec_layer(0), spec_ada(1), spec_gla_state(), spec_gla_b(), spec_swa_w(), XI], [XO, spec_swa_state()]),
              [st_ada(0), st_setl(0), st_gla_b(), st_post(0), st_prenorm(1), st_ffn(0, next_ada=1), st_post(1),
               st_setl(1), st_prenorm(0), st_swa_proj()], 0))
    L.append((_mk([spec_common(), spec_layer(1), spec_layer(2), spec_ada(3), spec_swa_state(), spec_swa_attn(), spec_gmlp(), spec_gla_w(), XI],
                  [XO, spec_gla_state(), S1]),
              [st_ada(1), st_setl(1), st_swa_attn(), st_post(0), st_prenorm(1), st_ffn(1, next_ada=2), st_post(1),
               st_setl(2), st_prenorm(0), st_gmlp(), st_post(0), st_prenorm(1), st_ffn(2, next_ada=3), st_post(1),
               st_setl(3), st_prenorm(0), st_gla_a()], 1))
    L.append((_mk([spec_common(), spec_layer(3), spec_gla_state(), spec_gla_b(), XI], [XO]),
              [st_ada(3), st_setl(3), st_gla_b(True), st_post(0, True), st_prenorm(1, True), st_ffn(3, True), st_post(1, True)], 1))
    return L


def _fused_def():
    def internal(sp):
        return {k: (sh, dt_, None) for k, (sh, dt_) in sp.items()}
    tens = _mk([spec_common(), {"sel": ([128, 2], F32)}, spec_layer(0), spec_layer(1), spec_layer(2), spec_layer(3),
                spec_gla_w(), {"gl_ong": ([D], F32), "gl_wo": ([KC, 128, KC, 128], F32)},
                {k + "_B": v for k, v in spec_gla_w().items() if k != "gl_mask"}, {"gl_ong_B": ([D], F32), "gl_wo_B": ([KC, 128, KC, 128], F32)},
                spec_swa_w(), {k: v for k, v in spec_swa_attn().items() if k not in ("sw_KDh", "sw_VTh")}, spec_gmlp(),
                {"XDin": ([D, NT], F32)}], [{"XD": ([D, NT], F32)}])
    tens.update(internal(spec_gla_state()))
    tens.update(internal({"gl_S1": ([128, 8 * 512], F32), "gl_S1g": ([256, 8 * 512], F32)}))
    tens.update(internal(spec_swa_state()))
    tens.update(internal({"sw_hx": ([128, 768], BF16), "sw_hxg": ([256, 768], BF16)}))

    def use_slot(B):
        def f(C, T):
            for k in list(spec_gla_w().keys()) + ["gl_ong", "gl_wo"]:
                if k == "gl_mask":
                    continue
                if "_A_" + k not in T:
                    T["_A_" + k] = T[k]
                T[k] = T[k + "_B"] if B else T["_A_" + k]
        return f
    steps = [st_ada0_a(), st_prenorm(0), st_ada0_b(), st_gla_a(), st_gla_b(), st_postpre(0, 0, 1, 0), st_ffn(0, next_ada=1),
             st_postpre(1, 0, 0, 1), st_swa_proj(), st_swa_attn(), st_postpre(0, 1, 1, 1), st_ffn(1, next_ada=2),
             st_postpre(1, 1, 0, 2), st_gmlp(), st_postpre(0, 2, 1, 2), st_ffn(2, next_ada=3),
             st_postpre(1, 2, 0, 3), use_slot(True), st_gla_a(), st_gla_b(True), st_postpre(0, 3, 1, 3, True), st_ffn(3, True),
             st_post(1, True)]
    return tens, steps


def kernel(**inputs):
    z = {k: np.asarray(v) for k, v in inputs.items()}
    ncores = 8
    hosts = [Host(z, c) for c in range(ncores)]
    tens, steps = _fused_def()
    if "fused" not in _PROGS:
        _PROGS["fused"] = build_launch(steps, tens)
    nc = _PROGS["fused"]
    shared = {}
    in_maps = []
    for c in range(ncores):
        d = {}
        for name, (sh, dt_, kind) in tens.items():
            if kind != "ExternalInput":
                continue
            base, gs = (name[:-2], 1) if name.endswith("_B") else (name, 0)
            hosts[c].gslot = gs
            if base in SIDE_DEP:
                d[name] = hosts[c].get(base)
            elif base == "sel":
                d[name] = np.tile(np.array([[float(c % 2), 1.0 - float(c % 2)]], np.float32), (128, 1))
            else:
                if name not in shared:
                    shared[name] = hosts[c].get(base)
                d[name] = shared[name]
        in_maps.append(d)
    res = run_bass_kernel_spmd(nc, in_maps, core_ids=list(range(ncores)))
    out = np.empty((4, 2 * NLAT, D), np.float32)
    for c in range(ncores):
        xT = np.asarray(res.results[c]["XD"])
        out[hosts[c].b, hosts[c].lat_global(), :] = xT[:, NCTX:].T
    return out


def kernel_unfused(**inputs):
    z = {k: np.asarray(v) for k, v in inputs.items()}
    ncores = 8
    hosts = [Host(z, c) for c in range(ncores)]
    shared = {}

    def get(c, name):
        if name in SIDE_DEP:
            return hosts[c].get(name)
        key = (name, hosts[c].gslot) if name.startswith("gl_") else name
        if key not in shared:
            shared[key] = hosts[c].get(name)
        return shared[key]

    defs = _launch_defs()
    state = [dict() for _ in range(ncores)]
    for li, (tens, steps, gslot) in enumerate(defs):
        if li not in _PROGS:
            _PROGS[li] = build_launch(steps, tens)
        nc = _PROGS[li]
        for h in hosts:
            h.gslot = gslot
        in_maps = []
        for c in range(ncores):
            d = {}
            for name, (sh, dt_, kind) in tens.items():
                if kind != "ExternalInput":
                    continue
                if name in state[c]:
                    d[name] = state[c][name]
                else:
                    d[name] = get(c, name)
            in_maps.append(d)
        res = run_bass_kernel_spmd(nc, in_maps, core_ids=list(range(ncores)))
        outs = res.results
        shared.clear()
        for c in range(ncores):
            r = outs[c]
            p = outs[c ^ 1]
            st = state[c]
            for name in r:
                if name == "XD":
                    st["XDin"] = r["XD"]
                elif name == "gl_S1":
                    st["gl_S1in"] = p["gl_S1"]
                else:
                    st[name] = r[name]
            if "sw_KD" in r:
                st["sw_KDh"] = np.ascontiguousarray(np.asarray(p["sw_KD"]).reshape(128, 4, NT)[:, :, NT - 128:].reshape(128, 512))
                st["sw_VTh"] = np.ascontiguousarray(np.asarray(p["sw_VT"]).reshape(128, NT // 128, 256)[:, -1, :])
    out = np.empty((4, 2 * NLAT, D), np.float32)
    for c in range(ncores):
        xT = state[c]["XDin"]
        out[hosts[c].b, hosts[c].lat_global(), :] = xT[:, NCTX:].T
    return out
```

```python
import numpy as np
import ml_dtypes
from contextlib import ExitStack
import concourse.bass as bass
import concourse.mybir as mybir
from concourse.bass_utils import run_bass_kernel_spmd

F32 = mybir.dt.float32
BF16 = mybir.dt.bfloat16
AF = mybir.ActivationFunctionType
ALU = mybir.AluOpType

D = 2048
KC = 16
NCTX = 256
NLAT = 1024
NT = NCTX + NLAT
FF = 5632
NFC = FF // 128
EPS = 1e-6
DEPTH = 4
NSLOT = 16
A_OFF, A_SZ = 0, 40960
BC_OFF, BC_SZ = 40960, 81920
D_OFF, D_SZ = 122880, 20224
ARENA_B = D_OFF + D_SZ

TILES_ALL = [(0, 512), (512, 512), (1024, 256)]
TILES_LAT = [(256, 512), (768, 512)]
PIECES_ALL = [(0, 256, 1), (256, 768, 0), (768, 1280, 0)]
PIECES_LAT = [(256, 768, 0), (768, 1280, 0)]


class Sem:
    def __init__(self, h):
        self.h = h
        self.cnt = 0


class Buf:
    __slots__ = ("w", "r")

    def __init__(self):
        self.w = {}
        self.r = {}


class Grid:
    def __init__(self):
        self.d = {}

    def __getitem__(self, key):
        b = self.d.get(key)
        if b is None:
            b = self.d[key] = Buf()
        return b


def cbs(c0, c1):
    return range(c0 // 256, (c1 + 255) // 256)


class ActT:
    def __init__(self, ap):
        self.ap = ap
        self.g = Grid()

    def b(self, k, c0, c1):
        return [self.g[(k, cb)] for cb in cbs(c0, c1)]

    def ball(self, ks, c0, c1):
        out = []
        for k in ks:
            out += self.b(k, c0, c1)
        return out


class Sched:
    def __init__(self, nc, es):
        self.nc = nc
        self.E = {"pe": nc.tensor, "act": nc.scalar, "dve": nc.vector, "pool": nc.gpsimd, "sp": nc.sync}
        self.es = es
        self.esem = {k: Sem(es.enter_context(nc.semaphore("e_" + k))) for k in self.E}
        self.seen = {k: {} for k in self.E}
        self.pending_dma = {}

    def new_sem(self, name):
        return Sem(self.es.enter_context(self.nc.semaphore(name)))

    def _deps(self, reads, writes):
        deps = {}
        for b in reads:
            for s, v in b.w.items():
                if deps.get(s, 0) < v:
                    deps[s] = v
        for b in writes:
            for s, v in b.w.items():
                if deps.get(s, 0) < v:
                    deps[s] = v
            for s, v in b.r.items():
                if deps.get(s, 0) < v:
                    deps[s] = v
        return deps

    def _wait(self, eng, deps):
        seen = self.seen[eng]
        e = self.E[eng]
        pes = self.esem["pe"]
        for s, v in deps.items():
            if eng == "pe" and s is pes:
                continue
            if seen.get(s, 0) < v:
                e.wait_ge(s.h, v)
                seen[s] = v

    def _commit(self, t, reads, writes, fresh):
        s, v = t
        for b in reads:
            if b.r.get(s, 0) < v:
                b.r[s] = v
        for b in writes:
            if fresh:
                b.w = {s: v}
            else:
                b.w[s] = v
            b.r = {}

    def op(self, eng, fn, reads=(), writes=(), fresh=True):
        self._wait(eng, self._deps(reads, writes))
        ins = fn(self.E[eng])
        sem = self.esem[eng]
        sem.cnt += 1
        ins.then_inc(sem.h, 1)
        t = (sem, sem.cnt)
        self._commit(t, reads, writes, fresh)
        return t

    def group(self, eng, fns, reads=(), writes=(), fresh=True):
        self._wait(eng, self._deps(reads, writes))
        ins = None
        for fn in fns:
            ins = fn(self.E[eng])
        sem = self.esem[eng]
        sem.cnt += 1
        ins.then_inc(sem.h, 1)
        t = (sem, sem.cnt)
        self._commit(t, reads, writes, fresh)
        return t

    def dma(self, q, sem, pairs, reads=(), writes=(), fresh=True):
        self._wait(q, self._deps(reads, writes))
        e = self.E[q]
        for (o, i) in pairs:
            e.dma_start(out=o, in_=i).then_inc(sem.h, 16)
            sem.cnt += 16
        t = (sem, sem.cnt)
        self._commit(t, reads, writes, fresh)
        self.pending_dma[sem] = sem.cnt
        return t

    def barrier(self, engs=("act", "dve", "sp")):
        deps = {self.esem[k]: self.esem[k].cnt for k in ("pe", "act", "dve") if self.esem[k].cnt}
        deps.update(self.pending_dma)
        for eng in engs:
            self._wait(eng, deps)

    def final_wait(self):
        self._wait("sp", dict(self.pending_dma))


class Ctx:
    def __init__(self, nc, es):
        self.nc = nc
        self.es = es
        self.S = Sched(nc, es)
        self.arena = es.enter_context(nc.sbuf_tensor("arena", [128, ARENA_B // 4], F32))
        self.ring = es.enter_context(nc.sbuf_tensor("ring", [128, NSLOT, 2048], BF16))
        self.psum = es.enter_context(nc.psum_tensor("psum", [128, 8, 512], F32))
        self.bank = [Buf() for _ in range(8)]
        self.slot = [Buf() for _ in range(NSLOT)]
        self.slot_sem = [self.S.new_sem(f"ws{i}") for i in range(NSLOT)]
        self.rp = 0
        self.dsems = [self.S.new_sem(f"d{i}") for i in range(48)]
        self.bank_rr = {}
        self.ones = es.enter_context(nc.sbuf_tensor("ones", [128, 128], BF16))
        self.ident = es.enter_context(nc.sbuf_tensor("ident_sb", [128, 128], BF16))
        self.epst = es.enter_context(nc.sbuf_tensor("epst", [128, 1], F32))
        self.cst = es.enter_context(nc.sbuf_tensor("cst", [128, 2], F32))
        self.sel = es.enter_context(nc.sbuf_tensor("sel_sb", [128, 2], F32))
        self.cc_sem = self.S.new_sem("cc")
        self.coefs = [es.enter_context(nc.sbuf_tensor(f"coef{i}", [128, 6, 2, KC], F32)) for i in range(2)]
        self.coef_bs = [Buf(), Buf()]
        self.mod = es.enter_context(nc.sbuf_tensor("mod", [128, 96, 2], F32))
        self.c2f = es.enter_context(nc.sbuf_tensor("c2f", [128, KC, 2], F32))
        self.c2b = es.enter_context(nc.sbuf_tensor("c2b", [128, KC, 2], BF16))
        self.abt = es.enter_context(nc.sbuf_tensor("abt", [128, 96], F32))
        self.ngt = es.enter_context(nc.sbuf_tensor("ngt", [128, 4, KC], F32))
        self.ada_in_b = Buf()
        self.cb = Buf()
        self.mod_b = Buf()
        self.cur_l = 0

    @property
    def coef(self):
        return self.coefs[self.cur_l % 2]

    @property
    def coef_b(self):
        return self.coef_bs[self.cur_l % 2]

    def carve(self, off_b, dtype, shape):
        n = int(np.prod(shape))
        esz = 4 if dtype is F32 else 2
        a = self.arena[:, off_b // 4:(off_b + n * esz) // 4]
        if dtype is not F32:
            a = a.bitcast(dtype)
        if len(shape) == 2:
            return a.rearrange("p (a b) -> p a b", a=shape[0])
        if len(shape) == 3:
            return a.rearrange("p (a b c) -> p a b c", a=shape[0], b=shape[1])
        return a

    def next_bank(self, banks):
        i = self.bank_rr.get(banks, 0)
        self.bank_rr[banks] = (i + 1) % len(banks)
        return banks[i]

    def load_w(self, in_ap, nslots, view):
        if self.rp + nslots > NSLOT:
            self.rp = 0
        s0 = self.rp
        self.rp += nslots
        bufs = self.slot[s0:s0 + nslots]
        span = self.ring[:, s0:s0 + nslots, :]
        out = view(span)
        self.S.dma("pool", self.slot_sem[s0], [(out, in_ap)], writes=bufs)
        return out, bufs

    def mm(self, out_ps, pairs, reads, bank_b):
        n = len(pairs)
        fns = []
        for i, (l, r) in enumerate(pairs):
            fns.append(lambda e, l=l, r=r, i=i: e.matmul(out_ps, lhsT=l, rhs=r, start=(i == 0), stop=(i == n - 1)))
        return self.S.group("pe", fns, reads=reads, writes=[bank_b])


def ada_inputs(C, l, T):
    S = C.S
    tb = Buf()
    S.dma("sp", C.dsems[40], [(C.c2f[:], T["c2"]), (C.abt[:], T[f"ada_b{l}"]), (C.ngt[:], T[f"norm_g{l}"])], writes=[tb, C.ada_in_b])
    S.op("act", lambda e: e.activation(out=C.c2b[:], in_=C.c2f[:], func=AF.Silu), reads=[tb], writes=[C.ada_in_b], fresh=False)


def ada_quad(C, l, T, m0, bank):
    ps = C.psum[:, bank, :192].rearrange("p (m c) -> p m c", c=2)
    wd = T[f"ada_w{l}"]
    w, wb = C.load_w(wd[m0:m0 + 4].rearrange("c p k m -> p c k m"), 4,
                     lambda sp: sp.rearrange("p c (k m) -> p c k m", k=KC))
    for j in range(4):
        C.mm(ps[:, m0 + j, :], [(w[:, j, k, :], C.c2b[:, k, :]) for k in range(KC)], reads=wb + [C.ada_in_b], bank_b=C.bank[bank])


def ada_finish(C, l, bank, subs_idx):
    S = C.S
    coef = C.coefs[l % 2]
    coef_b = C.coef_bs[l % 2]
    ps = C.psum[:, bank, :192].rearrange("p (m c) -> p m c", c=2)
    src_chunk = {0: 1, 1: 0, 2: 2, 3: 4, 4: 3, 5: 5}
    for ci in subs_idx:
        mc0 = src_chunk[ci] * 16
        sub = ci // 3
        for c in range(2):
            S.op("dve", lambda e: e.tensor_tensor(out=C.mod[:, mc0:mc0 + 16, c], in0=ps[:, mc0:mc0 + 16, c], in1=C.abt[:, mc0:mc0 + 16], op=ALU.add),
                 reads=[C.bank[bank], C.ada_in_b], writes=[C.mod_b], fresh=False)
            m = C.mod[:, mc0:mc0 + 16, c]
            if ci % 3 == 0:
                S.op("dve", lambda e: e.scalar_tensor_tensor(out=coef[:, ci, c, :], in0=m, scalar=1.0, in1=C.ngt[:, 2 * sub, :],
                                                              op0=ALU.add, op1=ALU.mult),
                     reads=[C.mod_b, C.ada_in_b], writes=[coef_b], fresh=False)
            elif ci % 3 == 1:
                S.op("dve", lambda e: e.tensor_copy(out=coef[:, ci, c, :], in_=m), reads=[C.mod_b], writes=[coef_b], fresh=False)
            else:
                S.op("dve", lambda e: e.tensor_tensor(out=coef[:, ci, c, :], in0=m, in1=C.ngt[:, 2 * sub + 1, :], op=ALU.mult),
                     reads=[C.mod_b, C.ada_in_b], writes=[coef_b], fresh=False)


def ada_gen(C, l, T, bank=7):
    ada_inputs(C, l, T)
    for m0 in range(0, 96, 4):
        ada_quad(C, l, T, m0, bank)
        if m0 < 92:
            yield
    ada_finish(C, l, bank, [0, 1, 2, 3, 4, 5])
    yield


def ada_now(C, l, T):
    for _ in ada_gen(C, l, T):
        pass
    C.cur_l = l


def sumsq_rstd(C, src, tiles, rstd, rstd_b, sqb, sqb_b, tmpsd):
    S = C.S
    c0 = tiles[0][0]
    c1 = tiles[-1][0] + tiles[-1][1]
    banks = [5, 6, 7][:len(tiles)]
    for k in range(KC):
        sq = sqb[k % 2]
        S.op("act", lambda e, sq=sq, k=k: e.activation(out=sq[:, c0:c1], in_=src.ap[:, k, c0:c1], func=AF.Square),
             reads=src.b(k, c0, c1), writes=[sqb_b[k % 2]])
        fns = []
        for (t0, tn), bk in zip(tiles, banks):
            fns.append(lambda e, sq=sq, t0=t0, tn=tn, bk=bk, k=k: e.matmul(
                C.psum[:, bk, :tn], lhsT=C.ones[:], rhs=sq[:, t0:t0 + tn], start=(k == 0), stop=(k == KC - 1)))
        S.group("pe", fns, reads=[sqb_b[k % 2], C.cb], writes=[C.bank[b] for b in banks], fresh=(k == 0))
    for (t0, tn), bk in zip(tiles, banks):
        S.op("act", lambda e, t0=t0, tn=tn, bk=bk: e.activation(out=tmpsd[:, :tn], in_=C.psum[:, bk, :tn], func=AF.Sqrt,
                                                                  bias=C.epst[:], scale=1.0 / D),
             reads=[C.bank[bk], C.cb], writes=[rstd_b[1]])
        S.op("dve", lambda e, t0=t0, tn=tn: e.reciprocal(out=rstd[:, t0:t0 + tn], in_=tmpsd[:, :tn]),
             reads=[rstd_b[1]], writes=[rstd_b[0]], fresh=False)


def prenorm(C, T, sub, last, first=False):
    S = C.S
    pieces = [(256, 1280, 0)] if last else [(0, 256, 1), (256, 1280, 0)]
    tiles = TILES_LAT if last else TILES_ALL
    S.barrier()
    xs = ActT(C.carve(BC_OFF, F32, [KC, NT]))
    h = ActT(C.carve(A_OFF, BF16, [KC, NT]))
    sqb = [C.carve(D_OFF, BF16, [NT]), C.carve(D_OFF + 2560, BF16, [NT])]
    sqb_b = [Buf(), Buf()]
    rstd = C.carve(D_OFF + 5120, F32, [NT])
    rstd_b = [Buf(), Buf()]
    tmps = [C.carve(D_OFF + 10240, F32, [1024]), C.carve(D_OFF + 14336, F32, [1024])]
    tmps_b = [Buf(), Buf()]
    tmpsd = C.carve(D_OFF + 14336, F32, [512])
    use_in = first and "XDin" in T
    xv = (T["XDin"] if use_in else T["XD"]).rearrange("(k p) t -> p k t", p=128)
    c0 = pieces[0][0]
    for q in range(4):
        S.dma("sp", C.dsems[q], [(xs.ap[:, 4 * q:4 * q + 4, c0:NT], xv[:, 4 * q:4 * q + 4, c0:NT])],
              reads=([] if use_in else [T["XD_b"][k] for k in range(4 * q, 4 * q + 4)]), writes=xs.ball(range(4 * q, 4 * q + 4), c0, NT))
    sumsq_rstd(C, xs, tiles, rstd, rstd_b, sqb, sqb_b, tmpsd)
    i = 0
    for k in range(KC):
        for (q0, q1, mc) in pieces:
            tmp = tmps[i % 2]
            tb = tmps_b[i % 2]
            i += 1
            S.op("dve", lambda e: e.scalar_tensor_tensor(
                out=tmp[:, :q1 - q0], in0=xs.ap[:, k, q0:q1], scalar=C.coef[:, 3 * sub, mc, k:k + 1],
                in1=rstd[:, q0:q1], op0=ALU.mult, op1=ALU.mult),
                reads=xs.b(k, q0, q1) + [rstd_b[0], rstd_b[1], C.coef_b], writes=[tb])
            S.op("act", lambda e: e.activation(
                out=h.ap[:, k, q0:q1], in_=tmp[:, :q1 - q0], func=AF.Identity,
                bias=C.coef[:, 3 * sub + 1, mc, k:k + 1], scale=1.0),
                reads=[tb, C.coef_b], writes=h.b(k, q0, q1))
    return h


def postnorm_resid(C, T, y, sub, last):
    S = C.S
    pieces = [(256, 1280, 0)] if last else [(0, 256, 1), (256, 1280, 0)]
    tiles = TILES_LAT if last else TILES_ALL
    c0 = pieces[0][0]
    S.barrier()
    sqb = [C.carve(D_OFF, BF16, [NT]), C.carve(D_OFF + 2560, BF16, [NT])]
    sqb_b = [Buf(), Buf()]
    rstd = C.carve(D_OFF + 5120, F32, [NT])
    rstd_b = [Buf(), Buf()]
    tmps = [C.carve(D_OFF + 10240, F32, [1024]), C.carve(D_OFF + 14336, F32, [1024])]
    tmps_b = [Buf(), Buf()]
    tmpsd = C.carve(D_OFF + 14336, F32, [512])
    xk = [C.carve(A_OFF + j * 5120, F32, [NT]) for j in range(4)]
    xk_b = [Buf() for _ in range(4)]
    xv = T["XD"].rearrange("(k p) t -> p k t", p=128)
    sumsq_rstd(C, y, tiles, rstd, rstd_b, sqb, sqb_b, tmpsd)
    i = 0
    for k in range(KC):
        xb = xk[k % 4]
        xbb = xk_b[k % 4]
        S.dma("sp", C.dsems[8 + k % 4], [(xb[:, c0:NT], xv[:, k, c0:NT])], reads=[T["XD_b"][k]], writes=[xbb])
        for (q0, q1, mc) in pieces:
            tmp = tmps[i % 2]
            tb = tmps_b[i % 2]
            i += 1
            S.op("dve", lambda e: e.scalar_tensor_tensor(
                out=tmp[:, :q1 - q0], in0=y.ap[:, k, q0:q1], scalar=C.coef[:, 3 * sub + 2, mc, k:k + 1],
                in1=rstd[:, q0:q1], op0=ALU.mult, op1=ALU.mult),
                reads=y.b(k, q0, q1) + [rstd_b[0], rstd_b[1], C.coef_b], writes=[tb])
            S.op("pool", lambda e: e.tensor_tensor(out=xb[:, q0:q1], in0=xb[:, q0:q1], in1=tmp[:, :q1 - q0], op=ALU.add),
                 reads=[tb], writes=[xbb], fresh=False)
        S.dma("sp", C.dsems[12 + k % 4], [(xv[:, k, c0:NT], xb[:, c0:NT])], reads=[xbb], writes=[T["XD_b"][k]])


def post_pre(C, T, y, sub_post, l_post, sub_pre, l_pre, last):
    S = C.S
    pieces = [(256, 1280, 0)] if last else [(0, 256, 1), (256, 1280, 0)]
    tiles = TILES_LAT if last else TILES_ALL
    c0 = pieces[0][0]
    cpo, cpo_b = C.coefs[l_post % 2], C.coef_bs[l_post % 2]
    cpr, cpr_b = C.coefs[l_pre % 2], C.coef_bs[l_pre % 2]
    S.barrier()
    sqb = [C.carve(D_OFF, BF16, [NT]), C.carve(D_OFF + 2560, BF16, [NT])]
    sqb_b = [Buf(), Buf()]
    rstd = C.carve(D_OFF + 5120, F32, [NT])
    rstd_b = [Buf(), Buf()]
    tmpsd = C.carve(D_OFF, F32, [512])
    tmps = [C.carve(D_OFF + 10240, F32, [1024]), C.carve(D_OFF + 14336, F32, [1024])]
    tmps_b = [Buf(), Buf()]
    xk = [C.carve(A_OFF + j * 5120, F32, [NT]) for j in range(4)]
    xk_b = [Buf() for _ in range(4)]
    h = ActT(C.carve(A_OFF, BF16, [KC, NT]))
    xv = T["XD"].rearrange("(k p) t -> p k t", p=128)
    sumsq_rstd(C, y, tiles, rstd, rstd_b, sqb, sqb_b, C.carve(D_OFF + 10240, F32, [512]))
    banks = [5, 6, 7][:len(tiles)]
    i = 0
    for k in range(KC):
        xb = xk[k % 4]
        xbb = xk_b[k % 4]
        S.dma("sp", C.dsems[8 + k % 4], [(xb[:, c0:NT], xv[:, k, c0:NT])], reads=[T["XD_b"][k]], writes=[xbb])
        for (q0, q1, mc) in pieces:
            tmp = tmps[i % 2]
            tb = tmps_b[i % 2]
            i += 1
            S.op("dve", lambda e: e.scalar_tensor_tensor(
                out=tmp[:, :q1 - q0], in0=y.ap[:, k, q0:q1], scalar=cpo[:, 3 * sub_post + 2, mc, k:k + 1],
                in1=rstd[:, q0:q1], op0=ALU.mult, op1=ALU.mult),
                reads=y.b(k, q0, q1) + [rstd_b[0], rstd_b[1], cpo_b], writes=[tb])
            S.op("pool", lambda e: e.tensor_tensor(out=y.ap[:, k, q0:q1], in0=xb[:, q0:q1], in1=tmp[:, :q1 - q0], op=ALU.add),
                 reads=[tb, xbb], writes=y.b(k, q0, q1))
        S.dma("sp", C.dsems[12 + k % 4], [(xv[:, k, c0:NT], y.ap[:, k, c0:NT])], reads=y.b(k, c0, NT), writes=[T["XD_b"][k]])
        sq = sqb[k % 2]
        S.op("act", lambda e: e.activation(out=sq[:, c0:NT], in_=y.ap[:, k, c0:NT], func=AF.Square),
             reads=y.b(k, c0, NT) + ([rstd_b[1]] if k == 0 else []), writes=[sqb_b[k % 2]])
        fns = []
        for (t0, tn), bk in zip(tiles, banks):
            fns.append(lambda e, t0=t0, tn=tn, bk=bk: e.matmul(
                C.psum[:, bk, :tn], lhsT=C.ones[:], rhs=sq[:, t0:t0 + tn], start=(k == 0), stop=(k == KC - 1)))
        S.group("pe", fns, reads=[sqb_b[k % 2], C.cb], writes=[C.bank[b] for b in banks], fresh=(k == 0))
    for (t0, tn), bk in zip(tiles, banks):
        S.op("act", lambda e: e.activation(out=tmpsd[:, :tn], in_=C.psum[:, bk, :tn], func=AF.Sqrt, bias=C.epst[:], scale=1.0 / D),
             reads=[C.bank[bk], C.cb], writes=[rstd_b[1], sqb_b[0]])
        S.op("dve", lambda e: e.reciprocal(out=rstd[:, t0:t0 + tn], in_=tmpsd[:, :tn]),
             reads=[rstd_b[1]], writes=[rstd_b[0]], fresh=(t0 == tiles[0][0]))
    for k in range(KC):
        for (q0, q1, mc) in pieces:
            tmp = tmps[i % 2]
            tb = tmps_b[i % 2]
            i += 1
            S.op("dve", lambda e: e.scalar_tensor_tensor(
                out=tmp[:, :q1 - q0], in0=y.ap[:, k, q0:q1], scalar=cpr[:, 3 * sub_pre, mc, k:k + 1],
                in1=rstd[:, q0:q1], op0=ALU.mult, op1=ALU.mult),
                reads=y.b(k, q0, q1) + [rstd_b[0], cpr_b], writes=[tb])
            S.op("act", lambda e: e.activation(
                out=h.ap[:, k, q0:q1], in_=tmp[:, :q1 - q0], func=AF.Identity,
                bias=cpr[:, 3 * sub_pre + 1, mc, k:k + 1], scale=1.0),
                reads=[tb, cpr_b], writes=h.b(k, q0, q1) + xk_b)
    return h


def ffn(C, T, l, h, last, hook=None):
    S = C.S
    tiles = TILES_LAT if last else TILES_ALL
    S.barrier()
    G = 4
    y = ActT(C.carve(BC_OFF, F32, [KC, NT]))
    hd = C.carve(D_OFF, BF16, [G, NT])
    hg = Grid()
    sl = [C.carve(D_OFF + 10240 + j * 2048, F32, [512]) for j in range(3)]
    sl_b = [Buf() for _ in range(3)]
    w1d, w3d, w2d = T[f"w1_{l}"], T[f"w3_{l}"], T[f"w2_{l}"]
    si = 0
    for g in range(NFC // G):
        vw = lambda sp: sp.rearrange("p c (k m) -> p c k m", k=KC)
        w1, w1b = C.load_w(w1d[G * g:G * g + G].rearrange("c p k m -> p c k m"), G, vw)
        w3, w3b = C.load_w(w3d[G * g:G * g + G].rearrange("c p k m -> p c k m"), G, vw)
        w2, w2b = C.load_w(w2d[128 * G * g:128 * G * (g + 1), :].rearrange("(c p) n -> p c n", p=128), G, lambda sp: sp)
        for j in range(G):
            for (t0, tn) in tiles:
                ba = C.next_bank((0, 1))
                bb = C.next_bank((2, 3))
                C.mm(C.psum[:, ba, :tn], [(w1[:, j, k, :], h.ap[:, k, t0:t0 + tn]) for k in range(KC)],
                     reads=w1b + h.ball(range(KC), t0, t0 + tn), bank_b=C.bank[ba])
                C.mm(C.psum[:, bb, :tn], [(w3[:, j, k, :], h.ap[:, k, t0:t0 + tn]) for k in range(KC)],
                     reads=w3b + h.ball(range(KC), t0, t0 + tn), bank_b=C.bank[bb])
                s_ = sl[si % 3]
                sb = sl_b[si % 3]
                si += 1
                S.op("act", lambda e, s_=s_, ba=ba, tn=tn: e.activation(out=s_[:, :tn], in_=C.psum[:, ba, :tn], func=AF.Silu),
                     reads=[C.bank[ba]], writes=[sb])
                S.op("dve", lambda e, s_=s_, bb=bb, tn=tn, t0=t0, j=j: e.tensor_tensor(
                    out=hd[:, j, t0:t0 + tn], in0=s_[:, :tn], in1=C.psum[:, bb, :tn], op=ALU.mult),
                    reads=[sb, C.bank[bb]], writes=[hg[(j, t0)]])
        for dc in range(KC):
            for (t0, tn) in tiles:
                by = C.next_bank((4, 5, 6))
                C.mm(C.psum[:, by, :tn], [(w2[:, j, dc * 128:(dc + 1) * 128], hd[:, j, t0:t0 + tn]) for j in range(G)],
                     reads=w2b + [hg[(j, t0)] for j in range(G)], bank_b=C.bank[by])
                if g == 0:
                    S.op("dve", lambda e, dc=dc, t0=t0, tn=tn, by=by: e.tensor_copy(out=y.ap[:, dc, t0:t0 + tn], in_=C.psum[:, by, :tn]),
                         reads=[C.bank[by]], writes=y.b(dc, t0, t0 + tn))
                else:
                    S.op("dve", lambda e, dc=dc, t0=t0, tn=tn, by=by: e.tensor_tensor(
                        out=y.ap[:, dc, t0:t0 + tn], in0=y.ap[:, dc, t0:t0 + tn], in1=C.psum[:, by, :tn], op=ALU.add),
                        reads=[C.bank[by]], writes=y.b(dc, t0, t0 + tn))
        if hook is not None:
            hook(g)
    return y


def linear_fm(C, wds, n_mc, src, tiles, epilogue, per_load=2, bank_sets=((0, 1), (2, 3))):
    for m0 in range(0, n_mc, per_load):
        n = min(per_load, n_mc - m0)
        ws = []
        for wd in wds:
            ws.append(C.load_w(wd[m0:m0 + n].rearrange("c p k m -> p c k m"), n,
                               lambda sp: sp.rearrange("p c (k m) -> p c k m", k=KC)))
        for j in range(n):
            for (t0, tn) in tiles:
                bks = []
                for wi, (w, wb) in enumerate(ws):
                    bk = C.next_bank(bank_sets[wi])
                    C.mm(C.psum[:, bk, :tn], [(w[:, j, k, :], src.ap[:, k, t0:t0 + tn]) for k in range(KC)],
                         reads=wb + src.ball(range(KC), t0, t0 + tn), bank_b=C.bank[bk])
                    bks.append(bk)
                epilogue(m0 + j, t0, tn, bks)


def linear_tm(C, wd, n_nb, nbw, src, tok_tiles, epilogue, banks=(0, 1, 2, 3)):
    nsl = nbw * KC // 2048
    for nb in range(n_nb):
        w, wb = C.load_w(wd[nb], nsl, lambda sp: sp.rearrange("p c (k m) -> p (c k) m", m=nbw))
        for ti, (t0, tn) in enumerate(tok_tiles):
            bk = C.next_bank(banks)
            C.mm(C.psum[:tn, bk, :nbw], [(src.ap[:, k, t0:t0 + tn], w[:, k, :]) for k in range(KC)],
                 reads=wb + src.ball(range(KC), t0, t0 + tn), bank_b=C.bank[bk])
            epilogue(nb, ti, t0, tn, bk)


def out_proj(C, wd, oT, tiles):
    S = C.S
    y = ActT(C.carve(BC_OFF, F32, [KC, NT]))
    cnt = [0]

    def epi(mc, t0, tn, bks):
        bk = bks[0]
        eng = "act" if cnt[0] % 2 == 0 else "dve"
        cnt[0] += 1
        if eng == "act":
            S.op("act", lambda e: e.activation(out=y.ap[:, mc, t0:t0 + tn], in_=C.psum[:, bk, :tn], func=AF.Copy),
                 reads=[C.bank[bk]], writes=y.b(mc, t0, t0 + tn))
        else:
            S.op("dve", lambda e: e.tensor_copy(out=y.ap[:, mc, t0:t0 + tn], in_=C.psum[:, bk, :tn]),
                 reads=[C.bank[bk]], writes=y.b(mc, t0, t0 + tn))
    linear_fm(C, [wd], KC, oT, tiles, epi, bank_sets=((0, 1, 2, 3),))
    return y


TOK_TILES = [(i * 128, 128) for i in range(NT // 128)]
PAIRS = [[0, 1], [2, 3], [4, 5], [6, 7]]


def pair_allgather(C, src, dst, src_b, dst_b):
    S = C.S
    S._wait("pool", S._deps([src_b], [dst_b]))
    ins = C.nc.gpsimd.collective_compute("AllGather", ALU.bypass, replica_groups=PAIRS, ins=[src.opt()], outs=[dst.opt()])
    sem = C.cc_sem
    ins.then_inc(sem.h)
    sem.cnt += 1
    t = (sem, sem.cnt)
    S._commit(t, [src_b], [dst_b], True)
    S.pending_dma[sem] = sem.cnt


def gmlp_mixer(C, T, h):
    S = C.S
    S.barrier()
    vtm = C.carve(BC_OFF, BF16, [NT // 128, D])
    vg = Grid()
    gB = C.carve(BC_OFF + 40960, F32, [D])
    bB = C.carve(BC_OFF + 49152, F32, [D])
    tmpf = [C.carve(BC_OFF + 57344, F32, [D]), C.carve(BC_OFF + 65536, F32, [D])]
    tmpf_b = [Buf(), Buf()]
    lnb = Buf()
    S.dma("sp", C.dsems[16], [(gB, T["gm_ln_g"].partition_broadcast(128)), (bB, T["gm_ln_b"].partition_broadcast(128))], writes=[lnb])
    stats = C.carve(D_OFF, F32, [4, 6])
    mv = C.carve(D_OFF + 128, F32, [2])
    sd = C.carve(D_OFF + 192, F32, [1])
    st_b = Buf()

    def epi_v(nb, ti, t0, tn, bk):
        S.op("act", lambda e: e.activation(out=vtm[:, ti, nb * 512:(nb + 1) * 512], in_=C.psum[:, bk, :], func=AF.Gelu_apprx_tanh),
             reads=[C.bank[bk]], writes=[vg[(ti, nb)]])
    linear_tm(C, T["gm_wv"], 4, 512, h, TOK_TILES, epi_v)
    for ti in range(NT // 128):
        for q in range(4):
            S.op("dve", lambda e, q=q: e.bn_stats(out=stats[:, q, :], in_=vtm[:, ti, q * 512:(q + 1) * 512]),
                 reads=[vg[(ti, q)]], writes=[st_b], fresh=(q == 0))
        S.op("dve", lambda e: e.bn_aggr(out=mv, in_=stats), reads=[st_b], writes=[st_b], fresh=False)
        S.op("act", lambda e: e.activation(out=sd, in_=mv[:, 1:2], func=AF.Sqrt, bias=C.epst[:], scale=1.0),
             reads=[st_b, C.cb], writes=[st_b], fresh=False)
        S.op("dve", lambda e: e.reciprocal(out=sd, in_=sd), reads=[st_b], writes=[st_b], fresh=False)
        tf = tmpf[ti % 2]
        tfb = tmpf_b[ti % 2]
        S.op("dve", lambda e: e.tensor_scalar(out=tf, in0=vtm[:, ti, :], scalar1=mv[:, 0:1], scalar2=sd[:, 0:1],
                                               op0=ALU.subtract, op1=ALU.mult),
             reads=[st_b] + [vg[(ti, q)] for q in range(4)], writes=[tfb])
        S.op("dve", lambda e: e.tensor_tensor(out=tf, in0=tf, in1=gB, op=ALU.mult), reads=[lnb], writes=[tfb], fresh=False)
        S.op("dve", lambda e: e.tensor_tensor(out=vtm[:, ti, :], in0=tf, in1=bB, op=ALU.add),
             reads=[lnb, tfb], writes=[vg[(ti, q)] for q in range(4)])
    S.barrier()
    uT = ActT(C.carve(BC_OFF + 40960, BF16, [KC, NT]))

    def epi_u(mc, t0, tn, bks):
        S.op("act", lambda e: e.activation(out=uT.ap[:, mc, t0:t0 + tn], in_=C.psum[:, bks[0], :tn], func=AF.Gelu_apprx_tanh),
             reads=[C.bank[bks[0]]], writes=uT.b(mc, t0, t0 + tn))
    linear_fm(C, [T["gm_wu"]], KC, h, TILES_ALL, epi_u, bank_sets=((0, 1, 2, 3),))
    S.barrier()
    pT = ActT(C.carve(A_OFF, BF16, [KC, NT]))
    bsB = C.carve(D_OFF + 256, F32, [16, 128])
    bsb = Buf()
    S.dma("sp", C.dsems[17], [(bsB, T["gm_bs"].partition_broadcast(128))], writes=[bsb])
    wst, wsb = C.load_w(T["gm_wsT"], 1, lambda sp: sp.rearrange("p c (g i) -> p (c g) i", i=128))
    tmps = [C.carve(D_OFF + 8448 + j * 2048, F32, [4, 128]) for j in range(2)]
    tmps_b = [Buf(), Buf()]
    i = 0
    for n in range(NT // 128):
        for q in range(4):
            bk = C.next_bank((0, 1, 2, 3))
            psv = C.psum[:, bk, :].rearrange("p (g i) -> p g i", i=128)
            fns = []
            for gg in range(4):
                g = 4 * q + gg
                fns.append(lambda e, g=g, gg=gg: e.matmul(psv[:, gg, :], lhsT=vtm[:, n, g * 128:(g + 1) * 128], rhs=wst[:, g, :],
                                                           start=True, stop=True))
            S.group("pe", fns, reads=wsb + [vg[(n, q)]], writes=[C.bank[bk]])
            tmp = tmps[i % 2]
            tb = tmps_b[i % 2]
            i += 1
            S.op("dve", lambda e: e.tensor_tensor(out=tmp, in0=psv, in1=bsB[:, 4 * q:4 * q + 4, :], op=ALU.add),
                 reads=[C.bank[bk], bsb], writes=[tb])
            S.op("dve", lambda e: e.tensor_tensor(out=pT.ap[:, 4 * q:4 * q + 4, n * 128:(n + 1) * 128], in0=tmp,
                                                   in1=uT.ap[:, 4 * q:4 * q + 4, n * 128:(n + 1) * 128], op=ALU.mult),
                 reads=[tb] + uT.ball(range(4 * q, 4 * q + 4), n * 128, n * 128 + 128),
                 writes=pT.ball(range(4 * q, 4 * q + 4), n * 128, n * 128 + 128), fresh=False)
    S.barrier()
    return out_proj(C, T["gm_wo"], pT, TILES_ALL)


NEG = -1.0e30
AX = mybir.AxisListType


def swa_proj(C, T, h):
    S = C.S
    S.barrier()
    qT = ActT(C.carve(BC_OFF, BF16, [KC, NT]))
    KD = ActT(C.carve(D_OFF, BF16, [4, NT]))
    VT = C.carve(D_OFF + 10240, BF16, [NT // 128, 256])
    vb = Buf()
    co = BC_OFF + 40960
    tabs = [C.carve(co + i * 5120, F32, [NT]) for i in range(4)]
    tab_b = Buf()
    S.dma("sp", C.dsems[16], [(tabs[i], T["rope_tab"][i]) for i in range(4)], writes=[tab_b])
    t1 = [C.carve(co + 20480 + i * 2048, F32, [512]) for i in range(2)]
    t2 = [C.carve(co + 24576 + i * 2048, F32, [512]) for i in range(2)]
    tb = [Buf(), Buf()]
    cnt = [0]

    def mk_epi(dst, cs, sn):
        def epi(mc, t0, tn, bks):
            i = cnt[0] % 2
            cnt[0] += 1
            S.op("dve", lambda e: e.tensor_tensor(out=t1[i][:, :tn], in0=C.psum[:, bks[0], :tn], in1=cs[:, t0:t0 + tn], op=ALU.mult),
                 reads=[C.bank[bks[0]], tab_b], writes=[tb[i]])
            S.op("dve", lambda e: e.tensor_tensor(out=t2[i][:, :tn], in0=C.psum[:, bks[1], :tn], in1=sn[:, t0:t0 + tn], op=ALU.mult),
                 reads=[C.bank[bks[1]], tab_b], writes=[tb[i]], fresh=False)
            S.op("dve", lambda e: e.tensor_tensor(out=dst.ap[:, mc, t0:t0 + tn], in0=t1[i][:, :tn], in1=t2[i][:, :tn], op=ALU.add),
                 reads=[tb[i]], writes=dst.b(mc, t0, t0 + tn))
        return epi
    linear_fm(C, [T["sw_wq"], T["sw_wqp"]], KC, h, TILES_ALL, mk_epi(qT, tabs[0], tabs[1]))
    linear_fm(C, [T["sw_wk"], T["sw_wkp"]], 4, h, TILES_ALL, mk_epi(KD, tabs[2], tabs[3]))

    def epi_v(nb, ti, t0, tn, bk):
        S.op("act", lambda e: e.activation(out=VT[:, ti, :], in_=C.psum[:, bk, :256], func=AF.Copy),
             reads=[C.bank[bk]], writes=[vb], fresh=False)
    linear_tm(C, T["sw_wv"], 1, 256, h, TOK_TILES, epi_v)
    S.dma("sp", C.dsems[17], [(T["sw_qT"].rearrange("p (k t) -> p k t", k=KC), qT.ap)], reads=qT.ball(range(KC), 0, NT), writes=[T["sw_b"]], fresh=False)
    S.dma("sp", C.dsems[18], [(T["sw_KD"].rearrange("p (k t) -> p k t", k=4), KD.ap)], reads=KD.ball(range(4), 0, NT), writes=[T["sw_b"]], fresh=False)
    S.dma("sp", C.dsems[19], [(T["sw_VT"].rearrange("p (k t) -> p k t", k=NT // 128), VT)], reads=[vb], writes=[T["sw_b"]], fresh=False)
    if "sw_hx" in T:
        hb = T["gl_b"]["sw_hx"]
        S.dma("sp", C.dsems[20], [(T["sw_hx"][:, 0:512].rearrange("p (k t) -> p k t", k=4), KD.ap[:, :, NT - 128:NT]),
                                   (T["sw_hx"][:, 512:768], VT[:, NT // 128 - 1, :])],
              reads=KD.ball(range(4), NT - 128, NT) + [vb], writes=[hb])
        pair_allgather(C, T["sw_hx"], T["sw_hxg"], hb, T["gl_b"]["sw_hxg"])


def swa_attn(C, T):
    S = C.S
    S.barrier()
    qT = C.carve(BC_OFF, BF16, [KC, NT])
    KD = C.carve(D_OFF, BF16, [4, NT])
    VT = C.carve(D_OFF + 10240, BF16, [NT // 128, 256])
    KDh = C.carve(D_OFF + 15360, BF16, [4, 128])
    VTh = C.carve(D_OFF + 16384, BF16, [256])
    inb = Buf()
    co = BC_OFF + 40960
    prs = [(qT, T["sw_qT"].rearrange("p (k t) -> p k t", k=KC)),
           (KD, T["sw_KD"].rearrange("p (k t) -> p k t", k=4)),
           (VT, T["sw_VT"].rearrange("p (k t) -> p k t", k=NT // 128))]
    if "sw_hxg" in T:
        hA = C.carve(co + 37376, BF16, [768])
        hB = C.carve(co + 38912, BF16, [768])
        hh = C.carve(D_OFF + 15360, BF16, [768])
        S.dma("sp", C.dsems[16], prs + [(hA, T["sw_hxg"][0:128, :]), (hB, T["sw_hxg"][128:256, :])],
              reads=[T["sw_b"], T["gl_b"]["sw_hxg"]], writes=[inb])
        S.op("dve", lambda e: e.tensor_scalar(out=hh, in0=hA, scalar1=C.sel[:, 0:1], scalar2=None, op0=ALU.mult), reads=[C.cb], writes=[inb], fresh=False)
        S.op("dve", lambda e: e.scalar_tensor_tensor(out=hh, in0=hB, scalar=C.sel[:, 1:2], in1=hh, op0=ALU.mult, op1=ALU.add),
             reads=[C.cb], writes=[inb], fresh=False)
    else:
        S.dma("sp", C.dsems[16], prs + [(KDh, T["sw_KDh"].rearrange("p (k t) -> p k t", k=4)), (VTh, T["sw_VTh"])],
              reads=[T["sw_b"]], writes=[inb])
    masks = C.carve(co, F32, [3, 640])
    sinkT = C.carve(co + 7680, F32, [32])
    S.dma("sp", C.dsems[17], [(masks, T["sw_masks"]), (sinkT, T["sw_sink"].partition_broadcast(128))], writes=[inb], fresh=False)
    NP = 4
    sc = [C.carve(co + 8192 + i * 2560, F32, [640]) for i in range(NP)]
    pp = [C.carve(co + 18432 + i * 1280, BF16, [640]) for i in range(NP)]
    pts = [C.carve(co + 23552 + i * 1280, BF16, [640]) for i in range(NP)]
    otm = [C.carve(co + 28672 + i * 4096, BF16, [D]) for i in range(2)]
    cols = [C.carve(co + 36864 + i * 64, F32, [8]) for i in range(NP)]
    sc_b = [Buf() for _ in range(NP)]
    pp_b = [Buf() for _ in range(NP)]
    pts_b = [Buf() for _ in range(NP)]
    col_b = [Buf() for _ in range(NP)]
    yb_b = [Buf() for _ in range(NP)]
    otm_b = [Buf(), Buf()]
    oT = ActT(C.carve(A_OFF, BF16, [KC, NT]))
    blocks = [(0, "c"), (128, "c")] + [(256 + 128 * j, 0 if j == 0 else (2 if j == 7 else 1)) for j in range(8)]
    units = [(bi, hq) for bi in range(len(blocks)) for hq in range(32)]
    info = {}

    def stage_a(t):
        bi, hq = units[t]
        q0, kind = blocks[bi]
        kvh = hq // 8
        pb = (hq % 2) * 64
        ch = hq // 2
        i = t % NP
        bx = i
        by = 4
        yq = i * 128
        lq = qT[pb:pb + 64, ch, q0:q0 + 128]
        fns = [lambda e: e.matmul(C.psum[:, bx, 0:256], lhsT=lq, rhs=KD[pb:pb + 64, kvh, 0:256], start=True, stop=True)]
        if kind == "c":
            n1, ntot = 256, 256
        else:
            if kind == 0:
                kc0, kn = q0, 128
            else:
                kc0, kn = q0 - 128, 256
            fns.append(lambda e: e.matmul(C.psum[:, bx, 256:256 + kn], lhsT=lq, rhs=KD[pb:pb + 64, kvh, kc0:kc0 + kn], start=True, stop=True))
            n1 = 256 + kn
            ntot = n1 + 128
            nxt = KDh[pb:pb + 64, kvh, :] if kind == 2 else KD[pb:pb + 64, kvh, q0 + 128:q0 + 256]
            fns.append(lambda e: e.matmul(C.psum[:, by, yq:yq + 128], lhsT=lq, rhs=nxt, start=True, stop=True))
        S.group("pe", fns, reads=[inb], writes=[C.bank[bx], yb_b[i]])
        s_ = sc[i]
        if kind == "c":
            S.op("dve", lambda e: e.tensor_copy(out=s_[:, :256], in_=C.psum[:, bx, :256]), reads=[C.bank[bx]], writes=[sc_b[i]])
        else:
            mk = masks[:, kind, :]
            S.op("dve", lambda e: e.tensor_tensor(out=s_[:, :n1], in0=C.psum[:, bx, :n1], in1=mk[:, :n1], op=ALU.add),
                 reads=[C.bank[bx], inb], writes=[sc_b[i]])
            S.op("dve", lambda e: e.tensor_tensor(out=s_[:, n1:ntot], in0=C.psum[:, by, yq:yq + 128], in1=mk[:, n1:ntot], op=ALU.add),
                 reads=[yb_b[i], inb], writes=[sc_b[i]], fresh=False)
        cl = cols[i]
        S.op("dve", lambda e: e.tensor_reduce(out=cl[:, 0:1], in_=s_[:, :ntot], axis=AX.X, op=ALU.max),
             reads=[sc_b[i]], writes=[col_b[i]])
        S.op("dve", lambda e: e.tensor_scalar(out=cl[:, 1:2], in0=cl[:, 0:1], scalar1=sinkT[:, hq:hq + 1], scalar2=-1.0,
                                               op0=ALU.max, op1=ALU.mult),
             reads=[inb], writes=[col_b[i]], fresh=False)
        S.op("act", lambda e: e.activation(out=pp[i][:, :ntot], in_=s_[:, :ntot], func=AF.Exp, bias=cl[:, 1:2], scale=1.0,
                                            accum_out=cl[:, 2:3]),
             reads=[sc_b[i], col_b[i]], writes=[pp_b[i], col_b[i]], fresh=False)
        S.op("act", lambda e: e.activation(out=cl[:, 3:4], in_=sinkT[:, hq:hq + 1], func=AF.Exp, bias=cl[:, 1:2], scale=1.0),
             reads=[inb], writes=[col_b[i]], fresh=False)
        info[t] = (ntot, kind, q0, kvh)

    def stage_b(t):
        ntot = info[t][0]
        i = t % NP
        nblk = ntot // 128
        bz = (5, 6)[t % 2]
        pzb = C.psum[:, bz, :].bitcast(BF16)
        fns = [lambda e, b_=b_: e.transpose(out=pzb[:, b_ * 128:(b_ + 1) * 128], in_=pp[i][:, b_ * 128:(b_ + 1) * 128], identity=C.ident[:])
               for b_ in range(nblk)]
        S.group("pe", fns, reads=[pp_b[i], C.cb], writes=[C.bank[bz]])
        S.op("act", lambda e: e.activation(out=pts[i][:, :ntot], in_=pzb[:, :ntot], func=AF.Copy),
             reads=[C.bank[bz]], writes=[pts_b[i]])

    def stage_c(t):
        ntot, kind, q0, kvh = info.pop(t)
        bi, hq = units[t]
        i = t % NP
        nblk = ntot // 128
        cl = cols[i]
        ob = otm[bi % 2]
        obb = otm_b[bi % 2]
        bw = 7
        hs = hq % 8
        vsl = slice(kvh * 64, (kvh + 1) * 64)
        vts = [VT[:, 0, vsl], VT[:, 1, vsl]]
        if kind != "c":
            t_cur = q0 // 128
            if kind != 0:
                vts.append(VT[:, t_cur - 1, vsl])
            vts.append(VT[:, t_cur, vsl])
            vts.append(VTh[:, vsl] if kind == 2 else VT[:, t_cur + 1, vsl])
        S.op("dve", lambda e: e.tensor_tensor(out=cl[:, 4:5], in0=cl[:, 2:3], in1=cl[:, 3:4], op=ALU.add),
             reads=[], writes=[col_b[i]], fresh=False)
        S.op("dve", lambda e: e.reciprocal(out=cl[:, 5:6], in_=cl[:, 4:5]), reads=[], writes=[col_b[i]], fresh=False)
        fns = [lambda e, b_=b_: e.matmul(C.psum[:, bw, hs * 64:(hs + 1) * 64], lhsT=pts[i][:, b_ * 128:(b_ + 1) * 128], rhs=vts[b_],
                                          start=(b_ == 0), stop=(b_ == nblk - 1)) for b_ in range(nblk)]
        S.group("pe", fns, reads=[pts_b[i], inb], writes=[C.bank[bw]], fresh=(hs == 0))
        S.op("dve", lambda e: e.tensor_scalar(out=ob[:, hq * 64:(hq + 1) * 64], in0=C.psum[:, bw, hs * 64:(hs + 1) * 64],
                                               scalar1=cl[:, 5:6], scalar2=None, op0=ALU.mult),
             reads=[C.bank[bw], col_b[i]], writes=[obb], fresh=(hq == 0))
        if hq == 31:
            for half in range(2):
                bz = (5, 6)[half]
                pzb = C.psum[:, bz, :].bitcast(BF16)
                fns = [lambda e, c_=c_: e.transpose(out=pzb[:, c_ * 128:(c_ + 1) * 128],
                                                    in_=ob[:, (half * 8 + c_) * 128:(half * 8 + c_ + 1) * 128], identity=C.ident[:])
                       for c_ in range(8)]
                S.group("pe", fns, reads=[obb, C.cb], writes=[C.bank[bz]])
                S.op("act", lambda e: e.activation(out=oT.ap[:, half * 8:half * 8 + 8, q0:q0 + 128],
                                                    in_=pzb.rearrange("p (c q) -> p c q", q=128), func=AF.Copy),
                     reads=[C.bank[bz]], writes=oT.ball(range(half * 8, half * 8 + 8), q0, q0 + 128), fresh=False)

    N = len(units)
    for t in range(N + 2):
        if t < N:
            stage_a(t)
        if 1 <= t <= N:
            stage_b(t - 1)
        if t >= 2:
            stage_c(t - 2)
    return oT


GC = 128
NCH = NT // GC
LNQ = float(np.log(1.0 / 16.0))


def gla_proj(C, T, h):
    S = C.S
    gb = T["gl_b"]
    S.barrier()
    for name, wname, fn, off, dsi in (("gl_V", "gl_wv", AF.Copy, BC_OFF, 20), ("gl_OG", "gl_wog", AF.Silu, BC_OFF + 40960, 21)):
        st = C.carve(off, BF16, [NCH, D])
        sb = Buf()

        def epi(nb, ti, t0, tn, bk, st=st, sb=sb, fn=fn):
            S.op("act", lambda e: e.activation(out=st[:, ti, nb * 512:(nb + 1) * 512], in_=C.psum[:, bk, :], func=fn),
                 reads=[C.bank[bk]], writes=[sb], fresh=False)
        linear_tm(C, T[wname], 4, 512, h, TOK_TILES, epi)
        S.dma("sp", C.dsems[dsi], [(T[name].rearrange("(t p) f -> p t f", p=128), st)], reads=[sb], writes=[gb[name]])
    S.barrier()
    aT = C.carve(D_OFF, BF16, [NT])
    wa2 = C.carve(D_OFF + 2560, BF16, [1024])
    nba = C.carve(D_OFF + 4608, F32, [2, 8])
    dect = C.carve(D_OFF + 4736, F32, [2, 80])
    rmask = C.carve(D_OFF + 5632, F32, [NT])
    ab = Buf()
    S.dma("pool", C.dsems[22], [(wa2, T["gl_wa2"])], writes=[ab])
    S.dma("sp", C.dsems[23], [(nba, T["gl_ba"])], writes=[ab], fresh=False)
    S.op("act", lambda e: e.activation(out=nba, in_=nba, func=AF.Identity, scale=-1.0), reads=[ab], writes=[ab], fresh=False)
    S.op("dve", lambda e: e.memset(rmask, 1.0), writes=[ab], fresh=False)
    S.op("dve", lambda e: e.memset(rmask.rearrange("p (c t) -> p c t", t=GC)[:, :, 0:1], 0.0), writes=[ab], fresh=False)

    def epi_a(mc, t0, tn, bks):
        S.op("act", lambda e: e.activation(out=aT[:, t0:t0 + tn], in_=C.psum[:, bks[0], :tn], func=AF.Copy),
             reads=[C.bank[bks[0]]], writes=[ab], fresh=False)
    linear_fm(C, [T["gl_wa"]], 1, h, TILES_ALL, epi_a, per_load=1, bank_sets=((0, 1),))
    tt = [C.carve(BC_OFF + i * 5120, F32, [NT]) for i in range(3)]
    tt_b = [Buf() for _ in range(3)]
    FF_ = [C.carve(BC_OFF + 15360 + i * 5120, F32, [NT]) for i in range(6)]
    F_b = [Buf() for _ in range(6)]
    qk = [[C.carve(BC_OFF + 46080 + par * 10240 + i * 2560, BF16, [NT]) for i in range(4)] for par in range(2)]
    qk_b = [[Buf() for _ in range(4)] for par in range(2)]
    khT2 = [[C.carve(BC_OFF + 66560 + (par * 2 + i) * 2560, BF16, [NT]) for i in range(2)] for par in range(2)]
    khT2_b = [[Buf(), Buf()], [Buf(), Buf()]]
    khs = [C.carve(BC_OFF + 76800 + i * 2560, BF16, [NCH, 128]) for i in range(2)]
    khs_b = [Buf(), Buf()]

    def kh_transposes(c8):
        khT = khT2[c8 % 2]
        khT_b = khT2_b[c8 % 2]
        for d in range(2):
            for (tb0, ntb) in ((0, 8), (8, 2)):
                bz = C.next_bank((0, 1, 2, 3))
                pzb = C.psum[:, bz, :].bitcast(BF16)
                fns = [lambda e, j=j: e.transpose(out=pzb[:, j * 128:(j + 1) * 128], in_=khT[d][:, (tb0 + j) * 128:(tb0 + j + 1) * 128],
                                                  identity=C.ident[:]) for j in range(ntb)]
                S.group("pe", fns, reads=[khT_b[d], C.cb], writes=[C.bank[bz]])
                S.op("act", lambda e: e.activation(out=khs[d][:, tb0:tb0 + ntb, :], in_=pzb[:, :ntb * 128].rearrange("p (t f) -> p t f", f=128),
                                                    func=AF.Copy),
                     reads=[C.bank[bz]], writes=[khs_b[d]], fresh=(tb0 == 0))
            S.dma("sp", C.dsems[32 + d], [(T["gl_KH"][d].rearrange("(t p) f -> p t f", p=128)[:, :, c8 * 128:(c8 + 1) * 128], khs[d])],
                  reads=[khs_b[d]], writes=[gb["gl_KH"]], fresh=False)
    for c8 in range(8):
        par = c8 % 2
        for d in range(2):
            rb = 32 * d
            t0_, t1_, t2_ = tt
            for (c0, cn) in TILES_ALL:
                bk = C.next_bank((0, 1, 2, 3))
                C.mm(C.psum[:, bk, :cn], [(wa2[rb:rb + 16, c8 * 128:(c8 + 1) * 128], aT[rb:rb + 16, c0:c0 + cn])], reads=[ab], bank_b=C.bank[bk])
                S.op("act", lambda e: e.activation(out=t0_[:, c0:c0 + cn], in_=C.psum[:, bk, :cn], func=AF.Exp, scale=-1.0,
                                                    bias=nba[:, d, c8:c8 + 1]),
                     reads=[C.bank[bk], ab], writes=[tt_b[0]], fresh=(c0 == 0))
            S.op("act", lambda e: e.activation(out=t0_, in_=t0_, func=AF.Ln, bias=C.cst[:, 0:1], scale=1.0),
                 reads=[C.cb], writes=[tt_b[0]], fresh=False)
            S.op("dve", lambda e: e.tensor_tensor_scan(out=t1_, data0=rmask, data1=t0_, initial=0.0, op0=ALU.mult, op1=ALU.add),
                 reads=[tt_b[0], ab], writes=[tt_b[1]])
            cs3 = t1_.rearrange("p (c t) -> p c t", t=GC)
            S.op("act", lambda e: e.activation(out=dect[:, d, c8 * NCH:(c8 + 1) * NCH], in_=cs3[:, :, GC - 1], func=AF.Exp, scale=-1.0 / 16),
                 reads=[tt_b[1]], writes=[ab], fresh=False)
            fq, fk, fkh = FF_[3 * d:3 * d + 3]
            fqb, fkb, fkhb = F_b[3 * d:3 * d + 3]
            if d == 0:
                for ch in range(NCH):
                    S.op("dve", lambda e: e.tensor_scalar(out=t2_[:, ch * GC:(ch + 1) * GC], in0=t1_[:, ch * GC:(ch + 1) * GC],
                                                           scalar1=t1_[:, ch * GC + GC - 1:ch * GC + GC], scalar2=None, op0=ALU.subtract),
                         reads=[tt_b[1]], writes=[tt_b[2]], fresh=(ch == 0))
                S.op("act", lambda e: e.activation(out=fq, in_=t1_, func=AF.Exp, scale=-1.0 / 16, bias=C.cst[:, 1:2]), reads=[tt_b[1], C.cb], writes=[fqb])
                S.op("act", lambda e: e.activation(out=fk, in_=t1_, func=AF.Exp, scale=1.0 / 16), reads=[tt_b[1]], writes=[fkb])
                S.op("act", lambda e: e.activation(out=fkh, in_=t2_, func=AF.Exp, scale=1.0 / 16), reads=[tt_b[2]], writes=[fkhb])
            else:
                S.op("dve", lambda e: e.tensor_tensor(out=t2_, in0=t1_, in1=t0_, op=ALU.subtract), reads=[tt_b[0], tt_b[1]], writes=[tt_b[2]])
                for ch in range(NCH):
                    S.op("dve", lambda e: e.tensor_scalar(out=t0_[:, ch * GC:(ch + 1) * GC], in0=t2_[:, ch * GC:(ch + 1) * GC],
                                                           scalar1=-1.0, scalar2=t1_[:, ch * GC + GC - 1:ch * GC + GC], op0=ALU.mult, op1=ALU.add),
                         reads=[tt_b[1], tt_b[2]], writes=[tt_b[0]], fresh=(ch == 0))
                S.op("act", lambda e: e.activation(out=fq, in_=t0_, func=AF.Exp, scale=-1.0 / 16, bias=C.cst[:, 1:2]), reads=[tt_b[0], C.cb], writes=[fqb])
                S.op("act", lambda e: e.activation(out=fk, in_=t0_, func=AF.Exp, scale=1.0 / 16), reads=[tt_b[0]], writes=[fkb])
                S.op("act", lambda e: e.activation(out=fkh, in_=t2_, func=AF.Exp, scale=-1.0 / 16), reads=[tt_b[2]], writes=[fkhb])
        q1, q2, k1, k2 = qk[par]
        q1b, q2b, k1b, k2b = qk_b[par]
        khT = khT2[par]
        khT_b = khT2_b[par]

        def epi_qk(mc, t0, tn, bks):
            bq, bkk = bks
            for (dst, db, bank, F, Fb, first) in ((q1, q1b, bq, FF_[0], F_b[0], True), (q2, q2b, bq, FF_[3], F_b[3], True),
                                                  (k1, k1b, bkk, FF_[1], F_b[1], True), (k2, k2b, bkk, FF_[4], F_b[4], True),
                                                  (khT[0], khT_b[0], bkk, FF_[2], F_b[2], True), (khT[1], khT_b[1], bkk, FF_[5], F_b[5], True)):
                S.op("dve", lambda e: e.tensor_tensor(out=dst[:, t0:t0 + tn], in0=C.psum[:, bank, :tn], in1=F[:, t0:t0 + tn], op=ALU.mult),
                     reads=[C.bank[bank], Fb], writes=[db], fresh=(t0 == 0))
        linear_fm(C, [T["gl_wq"][c8:c8 + 1], T["gl_wk"][c8:c8 + 1]], 1, h, TILES_ALL, epi_qk, per_load=1, bank_sets=((4, 5), (6, 7)))
        for d in range(2):
            S.dma("sp", C.dsems[24 + par * 4 + d], [(T["gl_QT"][d].rearrange("p (k t) -> p k t", k=8)[:, c8, :], (q1, q2)[d])],
                  reads=[(q1b, q2b)[d]], writes=[gb["gl_QT"]], fresh=False)
            S.dma("sp", C.dsems[26 + par * 4 + d], [(T["gl_KT"][d].rearrange("p (k t) -> p k t", k=8)[:, c8, :], (k1, k2)[d])],
                  reads=[(k1b, k2b)[d]], writes=[gb["gl_KT"]], fresh=False)
        if c8 >= 1:
            kh_transposes(c8 - 1)
    kh_transposes(7)
    S.dma("sp", C.dsems[34], [(T["gl_DEC"].rearrange("d p f -> p d f"), dect)], reads=[ab], writes=[gb["gl_DEC"]])


def gla_scan(C, T, d, last=False):
    S = C.S
    gb = T["gl_b"]
    S.barrier()
    St = C.carve(BC_OFF, F32, [8, 512])
    Sb = C.carve(BC_OFF + 16384, BF16, [8, 512])
    S_b = [Buf() for _ in range(8)]
    Sb_b = [Buf() for _ in range(8)]
    o_t = [C.carve(BC_OFF + 24576 + i * 8192, F32, [D]) for i in range(2)]
    o_b = [Buf(), Buf()]
    qt = [C.carve(BC_OFF + 40960 + i * 2048, BF16, [8, GC]) for i in range(2)]
    kt = [C.carve(BC_OFF + 45056 + i * 2048, BF16, [8, GC]) for i in range(2)]
    kh = [C.carve(BC_OFF + 49152 + i * 2048, BF16, [1024]) for i in range(2)]
    vv = [C.carve(BC_OFF + 53248 + i * 4096, BF16, [D]) for i in range(2)]
    in_b = [Buf(), Buf()]
    att = C.carve(BC_OFF + 61440, BF16, [4, GC])
    att_b = Buf()
    mask = C.carve(BC_OFF + 62464, F32, [2, GC])
    dec = C.carve(BC_OFF + 63488, F32, [2, 80])
    cb2 = Buf()
    S.dma("sp", C.dsems[16], [(mask, T["gl_mask"]), (dec, T["gl_DEC"].rearrange("d p f -> p d f"))], reads=[gb["gl_DEC"]], writes=[cb2])
    if d == 1:
        o1 = [C.carve(BC_OFF + 64512 + i * 8192, F32, [D]) for i in range(2)]
        og = [C.carve(D_OFF + i * 4096, BF16, [D]) for i in range(2)]
        gB = C.carve(D_OFF + 8192, F32, [D])
        S.dma("sp", C.dsems[17], [(gB, T["gl_ong"].partition_broadcast(128))], writes=[cb2], fresh=False)
        fin_b = [Buf(), Buf()]
        on = C.carve(D_OFF + 16384 - 0, BF16, [8]) if False else None
        oT = ActT(C.carve(A_OFF, BF16, [KC, NT]))
    order = list(range(NCH)) if d == 0 else [1, 0] + list(range(NCH - 1, 1, -1))
    QTv = T["gl_QT"][d].rearrange("p (k t) -> p k t", k=8)
    KTv = T["gl_KT"][d].rearrange("p (k t) -> p k t", k=8)
    KHv = T["gl_KH"][d].rearrange("(t p) f -> p t f", p=128)
    Vv = T["gl_V"].rearrange("(t p) f -> p t f", p=128)

    inB_b = [Buf(), Buf()]

    def load(si):
        c = order[si]
        i = si % 2
        prs = [(qt[i], QTv[:, :, c * GC:(c + 1) * GC]), (kt[i], KTv[:, :, c * GC:(c + 1) * GC]), (kh[i], KHv[:, c, :]), (vv[i], Vv[:, c, :])]
        rd = [gb["gl_QT"], gb["gl_KT"], gb["gl_KH"], gb["gl_V"]]
        S.dma("sp", C.dsems[18 + i], prs, reads=rd, writes=[in_b[i]])

    def load_b(si):
        c = order[si]
        i = si % 2
        prs = [(o1[i], T["gl_O1"].rearrange("(t p) f -> p t f", p=128)[:, c, :]), (og[i], T["gl_OG"].rearrange("(t p) f -> p t f", p=128)[:, c, :])]
        S.dma("sp", C.dsems[20 + i], prs, reads=[gb["gl_O1"], gb["gl_OG"]], writes=[inB_b[i]])

    def fin2(si):
        c = order[si]
        i = si % 2
        ot = o_t[i]
        sq = o1[i]
        cl = C.carve(D_OFF + 16384 + i * 64, F32, [8])
        for hd in range(4):
            S.op("act", lambda e: e.activation(out=sq[:, hd * 512:(hd + 1) * 512], in_=ot[:, hd * 512:(hd + 1) * 512], func=AF.Square,
                                                accum_out=cl[:, hd:hd + 1]),
                 reads=[o_b[i]], writes=[fin_b[i], inB_b[i]], fresh=False)
        S.op("act", lambda e: e.activation(out=cl[:, 4:8], in_=cl[:, 0:4], func=AF.Sqrt, bias=C.epst[:], scale=1.0 / 512),
             reads=[C.cb], writes=[fin_b[i]], fresh=False)
        S.op("dve", lambda e: e.reciprocal(out=cl[:, 4:8], in_=cl[:, 4:8]), reads=[], writes=[fin_b[i]], fresh=False)
        for hd in range(4):
            S.op("dve", lambda e: e.scalar_tensor_tensor(out=sq[:, hd * 512:(hd + 1) * 512], in0=ot[:, hd * 512:(hd + 1) * 512],
                                                          scalar=cl[:, 4 + hd:5 + hd], in1=gB[:, hd * 512:(hd + 1) * 512],
                                                          op0=ALU.mult, op1=ALU.mult),
                 reads=[o_b[i], cb2], writes=[fin_b[i], inB_b[i]], fresh=False)
        onb = og[i]
        S.op("pool", lambda e: e.tensor_tensor(out=onb, in0=sq, in1=og[i], op=ALU.mult), reads=[fin_b[i]], writes=[inB_b[i]], fresh=False)
        for half in range(2):
            bz = C.next_bank(ubanks)
            pzb = C.psum[:, bz, :].bitcast(BF16)
            fns = [lambda e, c_=c_: e.transpose(out=pzb[:, c_ * 128:(c_ + 1) * 128],
                                                in_=onb[:, (half * 8 + c_) * 128:(half * 8 + c_ + 1) * 128], identity=C.ident[:])
                   for c_ in range(8)]
            S.group("pe", fns, reads=[inB_b[i], C.cb], writes=[C.bank[bz]])
            S.op("act", lambda e: e.activation(out=oT.ap[:, half * 8:half * 8 + 8, c * GC:(c + 1) * GC],
                                                in_=pzb.rearrange("p (c q) -> p c q", q=128), func=AF.Copy),
                 reads=[C.bank[bz]], writes=oT.ball(range(half * 8, half * 8 + 8), c * GC, (c + 1) * GC), fresh=False)

    ubanks = (5, 6, 7)
    load(0)
    if d == 1:
        load_b(0)
    have_state = False
    pend = None
    for si, c in enumerate(order):
        i = si % 2
        if si + 1 < len(order):
            load(si + 1)
        if d == 1 and si == 2:
            if pend is not None:
                fin2(pend)
                pend = None
            if "gl_S1g" in T:
                G = T["gl_S1g"]
                tmpB = C.carve(BC_OFF + 24576, F32, [8, 512])
                S.dma("sp", C.dsems[22], [(St, G[0:128, :].rearrange("p (k f) -> p k f", k=8)),
                                           (tmpB, G[128:256, :].rearrange("p (k f) -> p k f", k=8))],
                      reads=[gb["gl_S1g"]], writes=S_b + o_b)
                for c8 in range(8):
                    S.op("dve", lambda e: e.tensor_scalar(out=St[:, c8, :], in0=St[:, c8, :], scalar1=C.sel[:, 0:1], scalar2=None, op0=ALU.mult),
                         reads=[C.cb], writes=[S_b[c8]])
                    S.op("dve", lambda e: e.scalar_tensor_tensor(out=St[:, c8, :], in0=tmpB[:, c8, :], scalar=C.sel[:, 1:2], in1=St[:, c8, :],
                                                                  op0=ALU.mult, op1=ALU.add),
                         reads=o_b + [C.cb], writes=[S_b[c8]])
            else:
                S.dma("sp", C.dsems[22], [(St, T["gl_S1in"].rearrange("p (k f) -> p k f", k=8))], writes=S_b)
            for c8 in range(8):
                S.op("act", lambda e: e.activation(out=Sb[:, c8, :], in_=St[:, c8, :], func=AF.Copy), reads=[S_b[c8]], writes=[Sb_b[c8]])
            have_state = True
        fns = []
        for hd in range(4):
            for kc in range(2):
                fns.append(lambda e, hd=hd, kc=kc: e.matmul(C.psum[:, 4, hd * GC:(hd + 1) * GC], lhsT=kt[i][:, 2 * hd + kc, :],
                                                            rhs=qt[i][:, 2 * hd + kc, :], start=(kc == 0), stop=(kc == 1)))
        S.group("pe", fns, reads=[in_b[i]], writes=[C.bank[4]])
        if have_state:
            for hd in range(4):
                fns = [lambda e, kc=kc: e.matmul(C.psum[:, hd, :], lhsT=qt[i][:, 2 * hd + kc, :], rhs=Sb[:, 2 * hd + kc, :],
                                                 start=(kc == 0), stop=False) for kc in range(2)]
                S.group("pe", fns, reads=[in_b[i], Sb_b[2 * hd], Sb_b[2 * hd + 1]], writes=[C.bank[hd]])
        for hd in range(4):
            S.op("dve", lambda e: e.tensor_tensor(out=att[:, hd, :], in0=C.psum[:, 4, hd * GC:(hd + 1) * GC], in1=mask[:, d, :], op=ALU.mult),
                 reads=[C.bank[4], cb2], writes=[att_b], fresh=(hd == 0))
        had_state = have_state
        need_update = not (d == 1 and si == 1)
        if need_update:
            for c8 in range(8):
                hd = c8 // 2
                bu = C.next_bank(ubanks)
                S.group("pe", [lambda e: e.matmul(C.psum[:, bu, :], lhsT=kh[i][:, c8 * 128:(c8 + 1) * 128], rhs=vv[i][:, hd * 512:(hd + 1) * 512],
                                                  start=True, stop=True)], reads=[in_b[i]], writes=[C.bank[bu]])
                if have_state:
                    S.op("dve", lambda e: e.scalar_tensor_tensor(out=St[:, c8, :], in0=St[:, c8, :], scalar=dec[:, d, c8 * NCH + c:c8 * NCH + c + 1],
                                                                  in1=C.psum[:, bu, :], op0=ALU.mult, op1=ALU.add),
                         reads=[C.bank[bu], cb2], writes=[S_b[c8]])
                else:
                    S.op("dve", lambda e: e.tensor_copy(out=St[:, c8, :], in_=C.psum[:, bu, :]), reads=[C.bank[bu]], writes=[S_b[c8]])
                S.op("act", lambda e: e.activation(out=Sb[:, c8, :], in_=St[:, c8, :], func=AF.Copy), reads=[S_b[c8]], writes=[Sb_b[c8]])
            have_state = True
        for hd in range(4):
            S.group("pe", [lambda e: e.matmul(C.psum[:, hd, :], lhsT=att[:, hd, :], rhs=vv[i][:, hd * 512:(hd + 1) * 512],
                                              start=(not had_state), stop=True)],
                    reads=[att_b, in_b[i]], writes=[C.bank[hd]], fresh=(not had_state))
        ot = o_t[i]
        skip_fin = (d == 1 and last and c < 2)
        if d == 0:
            for hd in range(4):
                S.op("act", lambda e: e.activation(out=ot[:, hd * 512:(hd + 1) * 512], in_=C.psum[:, hd, :], func=AF.Copy),
                     reads=[C.bank[hd]], writes=[o_b[i]], fresh=(hd == 0))
            S.dma("sp", C.dsems[20 + i], [(T["gl_O1"].rearrange("(t p) f -> p t f", p=128)[:, c, :], ot)], reads=[o_b[i]],
                  writes=[gb["gl_O1"]], fresh=False)
        else:
            if not skip_fin:
                for hd in range(4):
                    S.op("dve", lambda e: e.tensor_tensor(out=ot[:, hd * 512:(hd + 1) * 512], in0=C.psum[:, hd, :],
                                                           in1=o1[i][:, hd * 512:(hd + 1) * 512], op=ALU.add),
                         reads=[C.bank[hd], inB_b[i]], writes=[o_b[i]], fresh=(hd == 0))
            if pend is not None:
                fin2(pend)
            pend = None if skip_fin else si
            if si + 1 < len(order):
                load_b(si + 1)
    if d == 1 and pend is not None:
        fin2(pend)
    if d == 0:
        S.dma("sp", C.dsems[22], [(T["gl_S1"].rearrange("p (k f) -> p k f", k=8), St)], reads=S_b, writes=[gb["gl_S1"]])
        if "gl_S1g" in T:
            pair_allgather(C, T["gl_S1"], T["gl_S1g"], gb["gl_S1"], gb["gl_S1g"])
        return None
    return oT


def fm_weight(W):
    K, M = W.shape
    return np.ascontiguousarray(W.reshape(K // 128, 128, M // 128, 128).transpose(2, 1, 0, 3))


def tm_weight(W, nb=512):
    K, M = W.shape
    return np.ascontiguousarray(W.reshape(K // 128, 128, M // nb, nb).transpose(2, 1, 0, 3))


def fm_vec(v):
    v = np.asarray(v)
    n = v.shape[-1] // 128
    r = v.reshape(v.shape[:-1] + (n, 128))
    return np.ascontiguousarray(np.moveaxis(r, -1, 0))


def dram(nc, name, shape, dtype, kind):
    if kind is None:
        return nc.dram_tensor(name, [int(x) for x in shape], dtype).ap()
    return nc.dram_tensor(name, [int(x) for x in shape], dtype, kind=kind).ap()


def init_consts(C, T=None):
    S = C.S
    S.op("dve", lambda e: e.memset(C.ones[:], 1.0), writes=[C.cb], fresh=False)
    S.op("dve", lambda e: e.memset(C.epst[:], EPS), writes=[C.cb], fresh=False)
    S.op("dve", lambda e: e.memset(C.cst[:, 0:1], 1.0), writes=[C.cb], fresh=False)
    S.op("dve", lambda e: e.memset(C.cst[:, 1:2], LNQ), writes=[C.cb], fresh=False)
    if T is not None and "ident" in T:
        S.dma("sp", C.dsems[42], [(C.ident[:], T["ident"])], writes=[C.cb], fresh=False)
    if T is not None and "sel" in T:
        S.dma("sp", C.dsems[43], [(C.sel[:], T["sel"])], writes=[C.cb], fresh=False)


def build_launch(steps, tensors):
    nc = bass.Bass("TRN2", target_bir_lowering=False)
    T = {}
    for name, (shape, dt_, kind) in tensors.items():
        T[name] = dram(nc, name, shape, dt_, kind)
    T["XD_b"] = [Buf() for _ in range(KC)]
    T["sw_b"] = Buf()
    T["gl_b"] = Grid()
    with ExitStack() as es:
        C = Ctx(nc, es)
        C.v = {}
        init_consts(C, T)
        if "XD" not in T:
            T["XD"] = T["XDin"]
        elif "XDin" in T:
            C.S.dma("sp", C.dsems[41], [(T["XD"], T["XDin"])], writes=T["XD_b"])
        for st in steps:
            st(C, T)
        C.S.final_wait()
    return nc


def st_ada(l):
    def f(C, T):
        ada_now(C, l, T)
    f.tag = "ada"
    return f


def st_ada0_a():
    def f(C, T):
        ada_inputs(C, 0, T)
        for m0 in range(0, 32, 4):
            ada_quad(C, 0, T, m0, 4)
        ada_finish(C, 0, 4, [0, 1])
        C.cur_l = 0
    f.tag = "ada0a"
    return f


def st_ada0_b():
    def f(C, T):
        for m0 in range(32, 96, 4):
            ada_quad(C, 0, T, m0, 4)
        ada_finish(C, 0, 4, [2, 3, 4, 5])
    f.tag = "ada0b"
    return f


def st_prenorm(sub, last=False, first=False):
    def f(C, T):
        C.v["h"] = prenorm(C, T, sub, last, first)
    f.tag = "pre"
    return f


def st_post(sub, last=False):
    def f(C, T):
        postnorm_resid(C, T, C.v["y"], sub, last)
    f.tag = "post"
    return f


def st_postpre(sub_post, l_post, sub_pre, l_pre, last=False):
    def f(C, T):
        C.v["h"] = post_pre(C, T, C.v["y"], sub_post, l_post, sub_pre, l_pre, last)
        C.cur_l = l_pre
    f.tag = "postpre"
    return f


def st_ffn(l, last=False, next_ada=None):
    def f(C, T):
        hook = None
        if next_ada is not None:
            gen = ada_gen(C, next_ada, T)

            def hook(g):
                for _ in range(3 if g < 2 else 2):
                    next(gen, None)
        C.v["y"] = ffn(C, T, l, C.v["h"], last, hook)
        if next_ada is not None:
            for _ in gen:
                pass
    f.tag = "ffn"
    return f


def st_setl(l):
    def f(C, T):
        C.cur_l = l
    f.tag = "setl"
    return f


def st_gmlp():
    def f(C, T):
        C.v["y"] = gmlp_mixer(C, T, C.v["h"])
    f.tag = "gmlp"
    return f


def st_swa_proj():
    def f(C, T):
        swa_proj(C, T, C.v["h"])
    f.tag = "swap"
    return f


def st_swa_attn():
    def f(C, T):
        oT = swa_attn(C, T)
        C.S.barrier()
        C.v["y"] = out_proj(C, T["sw_wo"], oT, TILES_ALL)
    f.tag = "swaa"
    return f


def st_gla_a():
    def f(C, T):
        gla_proj(C, T, C.v["h"])
        gla_scan(C, T, 0)
    f.tag = "glaa"
    return f


def st_gla_b(last=False):
    def f(C, T):
        oT = gla_scan(C, T, 1, last)
        C.S.barrier()
        C.v["y"] = out_proj(C, T["gl_wo"], oT, TILES_LAT if last else TILES_ALL)
    f.tag = "glab"
    return f


def spec_gla_w():
    return {"gl_wa": ([1, 128, KC, 128], F32), "gl_wa2": ([128, 1024], F32), "gl_ba": ([128, 2, 8], F32),
            "gl_wq": ([8, 128, KC, 128], F32), "gl_wk": ([8, 128, KC, 128], F32),
            "gl_wv": ([4, 128, KC, 512], F32), "gl_wog": ([4, 128, KC, 512], F32), "gl_mask": ([128, 2, 128], F32)}


def spec_gla_state():
    return {"gl_QT": ([2, 128, 8 * NT], BF16), "gl_KT": ([2, 128, 8 * NT], BF16), "gl_KH": ([2, NT, 1024], BF16),
            "gl_DEC": ([2, 128, 80], F32), "gl_V": ([NT, D], BF16), "gl_OG": ([NT, D], BF16), "gl_O1": ([NT, D], F32)}


def spec_gla_b():
    return {"gl_S1in": ([128, 8 * 512], F32), "gl_mask": ([128, 2, 128], F32), "gl_ong": ([D], F32), "gl_wo": ([KC, 128, KC, 128], F32)}


def spec_common():
    return {"c2": ([128, KC, 2], F32), "ident": ([128, 128], BF16)}


def spec_ada(l):
    return {f"ada_w{l}": ([96, 128, KC, 128], F32), f"ada_b{l}": ([128, 96], F32), f"norm_g{l}": ([128, 4, KC], F32)}


def spec_ffn(l):
    return {f"w1_{l}": ([NFC, 128, KC, 128], F32), f"w3_{l}": ([NFC, 128, KC, 128], F32), f"w2_{l}": ([FF, D], F32)}


def spec_layer(l):
    return {**spec_ada(l), **spec_ffn(l)}


def spec_gmlp():
    return {"gm_wv": ([4, 128, KC, 512], F32), "gm_wu": ([KC, 128, KC, 128], F32), "gm_ln_g": ([D], F32), "gm_ln_b": ([D], F32),
            "gm_wsT": ([128, 2048], F32), "gm_bs": ([16, 128], F32), "gm_wo": ([KC, 128, KC, 128], F32)}


def spec_swa_w():
    return {"sw_wq": ([KC, 128, KC, 128], F32), "sw_wqp": ([KC, 128, KC, 128], F32), "sw_wk": ([4, 128, KC, 128], F32),
            "sw_wkp": ([4, 128, KC, 128], F32), "sw_wv": ([1, 128, KC, 256], F32), "rope_tab": ([4, 128, NT], F32)}


def spec_swa_state():
    return {"sw_qT": ([128, KC * NT], BF16), "sw_KD": ([128, 4 * NT], BF16), "sw_VT": ([128, NT // 128 * 256], BF16)}


def spec_swa_attn():
    return {"sw_KDh": ([128, 4 * 128], BF16), "sw_VTh": ([128, 256], BF16), "sw_masks": ([128, 3, 640], F32), "sw_sink": ([32], F32),
            "sw_wo": ([KC, 128, KC, 128], F32)}


ROPE_SRC = np.array([(d + 16) if (d % 32) < 16 else (d - 16) for d in range(64)])
ROPE_SIGN = np.array([-1.0 if (d % 32) < 16 else 1.0 for d in range(64)], np.float32)


class Host:
    def __init__(self, inputs, core):
        self.inp = inputs
        self.core = core
        self.b = core // 2
        self.side = core % 2
        self.gslot = 0
        self.cache = {}

    def lat_global(self):
        i = np.arange(NLAT)
        return i if self.side == 0 else (2 * NLAT - 1 - i)

    def get(self, name):
        key = (name, self.gslot) if name.startswith("gl_") else name
        if key not in self.cache:
            self.cache[key] = self._make(name)
        return self.cache[key]

    def _make(self, name):
        z = self.inp
        b, side = self.b, self.side
        if name == "XDin":
            ctx = z["ctx"][b]
            x = z["x"][b][self.lat_global()]
            if side == 1:
                ctx = ctx[::-1]
            return np.ascontiguousarray(np.concatenate([ctx, x], 0).T)
        if name == "c2":
            return np.ascontiguousarray(fm_vec(np.stack([z["c"][b], z["c_ctx"]], 0)).transpose(0, 2, 1))
        if name == "ident":
            return np.eye(128, dtype=np.float32).astype(ml_dtypes.bfloat16)
        if name.startswith("ada_w"):
            return fm_weight(z["ada_w"][int(name[5:])])
        if name.startswith("ada_b"):
            return fm_vec(z["ada_b"][int(name[5:])])
        if name.startswith("norm_g"):
            return fm_vec(z["norm_g"][int(name[6:])])
        if name.startswith("w1_"):
            return fm_weight(z["ffn_w1"][int(name[3:])])
        if name.startswith("w3_"):
            return fm_weight(z["ffn_w3"][int(name[3:])])
        if name.startswith("w2_"):
            return np.ascontiguousarray(z["ffn_w2"][int(name[3:])])
        if name == "gm_wv":
            return tm_weight(z["gmlp_w_in"][0][:, D:])
        if name == "gm_wu":
            return fm_weight(z["gmlp_w_in"][0][:, :D])
        if name == "gm_ln_g":
            return np.ascontiguousarray(z["gmlp_ln_g"][0])
        if name == "gm_ln_b":
            return np.ascontiguousarray(z["gmlp_ln_b"][0])
        if name == "gm_wsT":
            ws = z["gmlp_ws"][0]
            if side == 1:
                ws = ws[:, ::-1, ::-1]
            return np.ascontiguousarray(ws.transpose(2, 0, 1).reshape(128, 2048))
        if name == "gm_bs":
            bs = z["gmlp_bs"][0]
            if side == 1:
                bs = bs[:, ::-1]
            return np.ascontiguousarray(bs)
        if name == "gm_wo":
            return fm_weight(z["gmlp_wo"][0])
        if name in ("sw_wq", "sw_wqp", "sw_wk", "sw_wkp", "sw_wv"):
            W = z["attn_w_in"][0]
            nq = 2048
            if name == "sw_wq":
                return fm_weight(W[:, :nq])
            if name == "sw_wqp":
                idx = (np.arange(nq) // 64) * 64 + ROPE_SRC[np.arange(nq) % 64]
                return fm_weight(W[:, idx])
            if name in ("sw_wk", "sw_wkp"):
                src = ROPE_SRC if name == "sw_wkp" else np.arange(64)
                idx = np.concatenate([nq + kvh * 64 + np.concatenate([src, src]) for kvh in range(4)])
                return fm_weight(W[:, idx])
            return tm_weight(W[:, nq + 256:], 256)
        if name == "rope_tab":
            t = self.lat_global().astype(np.float32)
            inv = (10000.0 ** (-np.arange(16, dtype=np.float32) / 16)).astype(np.float32)
            row = np.floor(t / 64)
            col = t - row * 64
            ang = np.concatenate([row[:, None] * inv, row[:, None] * inv, col[:, None] * inv, col[:, None] * inv], -1)
            cos = np.cos(ang).T.astype(np.float32)
            sin = (np.sin(ang) * ROPE_SIGN[None]).T.astype(np.float32)
            tab = np.zeros((4, 128, NT), np.float32)
            tab[0, :, :NCTX] = 0.125
            tab[2, :, :NCTX] = 1.0
            tab[0, :, NCTX:] = 0.125 * np.tile(cos, (2, 1))
            tab[1, :, NCTX:] = 0.125 * np.tile(sin, (2, 1))
            tab[2, :, NCTX:] = np.tile(cos, (2, 1))
            tab[3, :, NCTX:] = np.tile(sin, (2, 1))
            return tab
        if name == "sw_masks":
            qi = np.arange(128)[:, None]
            ki = np.arange(128)[None, :]
            ge = np.where(ki >= qi, 0.0, NEG).astype(np.float32)
            le = np.where(ki <= qi, 0.0, NEG).astype(np.float32)
            halo = np.where(ki + qi >= 127, 0.0, NEG).astype(np.float32)
            m = np.zeros((128, 3, 640), np.float32)
            m[:, 0, 384:512] = le
            m[:, 1, 256:384] = ge
            m[:, 1, 512:640] = le
            m[:, 2, 256:384] = ge
            m[:, 2, 512:640] = halo
            return m
        if name == "sw_sink":
            return np.ascontiguousarray(z["attn_sink"][0])
        if name == "sw_wo":
            return fm_weight(z["attn_wo"][0])
        if name.startswith("gl_"):
            gs = self.gslot
            W = z["gla_w_in"][gs]
            i1, i2 = (0, 1) if side == 0 else (1, 0)
            if name == "gl_wa":
                Wa = np.zeros((D, 128), np.float32)
                Wa[:, 0:16] = W[:, 6144 + 16 * i1:6144 + 16 * i1 + 16]
                Wa[:, 32:48] = W[:, 6144 + 16 * i2:6144 + 16 * i2 + 16]
                return fm_weight(Wa)
            if name == "gl_wa2":
                w = np.zeros((128, 1024), np.float32)
                w[0:16] = z["gla_wa2"][gs, i1]
                w[32:48] = z["gla_wa2"][gs, i2]
                return w
            if name == "gl_ba":
                ba = z["gla_ba"][gs][[i1, i2]]
                return np.ascontiguousarray(ba.reshape(2, 8, 128).transpose(2, 0, 1))
            if name == "gl_wq":
                return fm_weight(W[:, :1024])
            if name == "gl_wk":
                return fm_weight(W[:, 1024:2048])
            if name == "gl_wv":
                return tm_weight(W[:, 2048:4096])
            if name == "gl_wog":
                return tm_weight(W[:, 4096:6144])
            if name == "gl_mask":
                j = np.arange(128)[:, None]
                i = np.arange(128)[None, :]
                m = np.zeros((128, 2, 128), np.float32)
                m[:, 0, :] = (j <= i)
                m[:, 1, :] = (j >= i)
                return m
            if name == "gl_ong":
                return np.ascontiguousarray(z["gla_onorm_g"][gs])
            if name == "gl_wo":
                return fm_weight(z["gla_wo"][gs])
        raise KeyError(name)


SIDE_DEP = ("XDin", "c2", "rope_tab", "gm_wsT", "gm_bs", "gl_wa", "gl_wa2", "gl_ba")
_PROGS = {}


def _mk(specs_in, specs_out):
    tens = {}
    for sp in specs_in:
        for k, (sh, dt_) in sp.items():
            tens[k] = (sh, dt_, "ExternalInput")
    for sp in specs_out:
        for k, (sh, dt_) in sp.items():
            tens[k] = (sh, dt_, "ExternalOutput")
    return tens


def _launch_defs():
    XI = {"XDin": ([D, NT], F32)}
    XO = {"XD": ([D, NT], F32)}
    S1 = {"gl_S1": ([128, 8 * 512], F32)}
    L = []
    L.append((_mk([spec_common(), spec_ada(0), spec_gla_w(), XI], [spec_gla_state(), S1]),
              [st_ada(0), st_setl(0), st_prenorm(0), st_gla_a()], 0))
    L.append((_mk([spec_common(), spec_layer(0), spec_ada(1), spec_gla_state(), spec_gla_b(), spec_swa_w(), XI], [XO, spec_swa_state()]),
              [st_ada(0), st_setl(0), st_gla_b(), st_post(0), st_prenorm(1), st_ffn(0, next_ada=1), st_post(1),
               st_setl(1), st_prenorm(0), st_swa_proj()], 0))
    L.append((_mk([spec_common(), spec_layer(1), spec_layer(2), spec_ada(3), spec_swa_state(), spec_swa_attn(), spec_gmlp(), spec_gla_w(), XI],
                  [XO, spec_gla_state(), S1]),
              [st_ada(1), st_setl(1), st_swa_attn(), st_post(0), st_prenorm(1), st_ffn(1, next_ada=2), st_post(1),
               st_setl(2), st_prenorm(0), st_gmlp(), st_post(0), st_prenorm(1), st_ffn(2, next_ada=3), st_post(1),
               st_setl(3), st_prenorm(0), st_gla_a()], 1))
    L.append((_mk([spec_common(), spec_layer(3), spec_gla_state(), spec_gla_b(), XI], [XO]),
              [st_ada(3), st_setl(3), st_gla_b(True), st_post(0, True), st_prenorm(1, True), st_ffn(3, True), st_post(1, True)], 1))
    return L


def _fused_def():
    def internal(sp):
        return {k: (sh, dt_, None) for k, (sh, dt_) in sp.items()}
    tens = _mk([spec_common(), {"sel": ([128, 2], F32)}, spec_layer(0), spec_layer(1), spec_layer(2), spec_layer(3),
                spec_gla_w(), {"gl_ong": ([D], F32), "gl_wo": ([KC, 128, KC, 128], F32)},
                {k + "_B": v for k, v in spec_gla_w().items() if k != "gl_mask"}, {"gl_ong_B": ([D], F32), "gl_wo_B": ([KC, 128, KC, 128], F32)},
                spec_swa_w(), {k: v for k, v in spec_swa_attn().items() if k not in ("sw_KDh", "sw_VTh")}, spec_gmlp(),
                {"XDin": ([D, NT], F32)}], [{"XD": ([D, NT], F32)}])
    tens.update(internal(spec_gla_state()))
    tens.update(internal({"gl_S1": ([128, 8 * 512], F32), "gl_S1g": ([256, 8 * 512], F32)}))
    tens.update(internal(spec_swa_state()))
    tens.update(internal({"sw_hx": ([128, 768], BF16), "sw_hxg": ([256, 768], BF16)}))

    def use_slot(B):
        def f(C, T):
            for k in list(spec_gla_w().keys()) + ["gl_ong", "gl_wo"]:
                if k == "gl_mask":
                    continue
                if "_A_" + k not in T:
                    T["_A_" + k] = T[k]
                T[k] = T[k + "_B"] if B else T["_A_" + k]
        return f
    steps = [st_ada0_a(), st_prenorm(0, first=True), st_ada0_b(), st_gla_a(), st_gla_b(), st_postpre(0, 0, 1, 0), st_ffn(0, next_ada=1),
             st_postpre(1, 0, 0, 1), st_swa_proj(), st_swa_attn(), st_postpre(0, 1, 1, 1), st_ffn(1, next_ada=2),
             st_postpre(1, 1, 0, 2), st_gmlp(), st_postpre(0, 2, 1, 2), st_ffn(2, next_ada=3),
             st_postpre(1, 2, 0, 3), use_slot(True), st_gla_a(), st_gla_b(True), st_postpre(0, 3, 1, 3, True), st_ffn(3, True),
             st_post(1, True)]
    return tens, steps


def kernel(**inputs):
    z = {k: np.asarray(v) for k, v in inputs.items()}
    ncores = 8
    hosts = [Host(z, c) for c in range(ncores)]
    tens, steps = _fused_def()
    if "fused" not in _PROGS:
        _PROGS["fused"] = build_launch(steps, tens)
    nc = _PROGS["fused"]
    shared = {}
    in_maps = []
    for c in range(ncores):
        d = {}
        for name, (sh, dt_, kind) in tens.items():
            if kind != "ExternalInput":
                continue
            base, gs = (name[:-2], 1) if name.endswith("_B") else (name, 0)
            hosts[c].gslot = gs
            if base in SIDE_DEP:
                d[name] = hosts[c].get(base)
            elif base == "sel":
                d[name] = np.tile(np.array([[float(c % 2), 1.0 - float(c % 2)]], np.float32), (128, 1))
            else:
                if name not in shared:
                    shared[name] = hosts[c].get(base)
                d[name] = shared[name]
        in_maps.append(d)
    res = run_bass_kernel_spmd(nc, in_maps, core_ids=list(range(ncores)))
    out = np.empty((4, 2 * NLAT, D), np.float32)
    for c in range(ncores):
        xT = np.asarray(res.results[c]["XD"])
        out[hosts[c].b, hosts[c].lat_global(), :] = xT[:, NCTX:].T
    return out


def kernel_unfused(**inputs):
    z = {k: np.asarray(v) for k, v in inputs.items()}
    ncores = 8
    hosts = [Host(z, c) for c in range(ncores)]
    shared = {}

    def get(c, name):
        if name in SIDE_DEP:
            return hosts[c].get(name)
        key = (name, hosts[c].gslot) if name.startswith("gl_") else name
        if key not in shared:
            shared[key] = hosts[c].get(name)
        return shared[key]

    defs = _launch_defs()
    state = [dict() for _ in range(ncores)]
    for li, (tens, steps, gslot) in enumerate(defs):
        if li not in _PROGS:
            _PROGS[li] = build_launch(steps, tens)
        nc = _PROGS[li]
        for h in hosts:
            h.gslot = gslot
        in_maps = []
        for c in range(ncores):
            d = {}
            for name, (sh, dt_, kind) in tens.items():
                if kind != "ExternalInput":
                    continue
                if name in state[c]:
                    d[name] = state[c][name]
                else:
                    d[name] = get(c, name)
            in_maps.append(d)
        res = run_bass_kernel_spmd(nc, in_maps, core_ids=list(range(ncores)))
        outs = res.results
        shared.clear()
        for c in range(ncores):
            r = outs[c]
            p = outs[c ^ 1]
            st = state[c]
            for name in r:
                if name == "XD":
                    st["XDin"] = r["XD"]
                elif name == "gl_S1":
                    st["gl_S1in"] = p["gl_S1"]
                else:
                    st[name] = r[name]
            if "sw_KD" in r:
                st["sw_KDh"] = np.ascontiguousarray(np.asarray(p["sw_KD"]).reshape(128, 4, NT)[:, :, NT - 128:].reshape(128, 512))
                st["sw_VTh"] = np.ascontiguousarray(np.asarray(p["sw_VT"]).reshape(128, NT // 128, 256)[:, -1, :])
    out = np.empty((4, 2 * NLAT, D), np.float32)
    for c in range(ncores):
        xT = state[c]["XDin"]
        out[hosts[c].b, hosts[c].lat_global(), :] = xT[:, NCTX:].T
    return out
```
